# Optimizing a Trainium2 kernel written in Bass

```python
import jax
import jax.numpy as jnp
from jax import lax
import numpy as np

D_MODEL = 2048
BATCH = 4
SEQ = 4096
DEPTH = 2

GRID_W = 64
CTX_LEN = 256
N_MIXERS = 2
NORM_EPS = 1e-6

RET_HEADS = 8
RET_DK = D_MODEL // RET_HEADS
RET_DV = 2 * RET_DK
RET_CHUNK = 128
RET_ROPE_BASE = 10000.0

ATT_HEAD_DIM = 128
ATT_Q_HEADS = D_MODEL // ATT_HEAD_DIM
ATT_KV_HEADS = 4
ATT_GROUP = ATT_Q_HEADS // ATT_KV_HEADS
WINDOW = 128
ATT_BLOCK = 128
ROPE_BASE = 10000.0

D_FF = 4 * D_MODEL

N_RET_LAYERS = (DEPTH + 1) // 2
N_ATT_LAYERS = DEPTH // 2
NEG_INF = -1e30

kernel_name = "hybrid_retention_swa_dit"


def rms_norm(x, g):
    xf = x.astype(jnp.float32)
    y = xf * lax.rsqrt(jnp.mean(xf * xf, axis=-1, keepdims=True) + NORM_EPS)
    return (y * g.astype(jnp.float32)).astype(x.dtype)


def modulate(h, shift, scale):
    return h * (1 + scale) + shift


def rope_angles(pos, dim, base):
    inv_freq = base ** (-jnp.arange(0, dim, 2, dtype=jnp.float32) / dim)
    return pos.astype(jnp.float32)[:, None] * inv_freq[None, :]


def apply_rope(x, ang):
    x1, x2 = jnp.split(x.astype(jnp.float32), 2, axis=-1)
    cos = jnp.cos(ang)[:, None, :]
    sin = jnp.sin(ang)[:, None, :]
    return jnp.concatenate([x1 * cos - x2 * sin, x2 * cos + x1 * sin], axis=-1).astype(x.dtype)


def axial_rope(x, rows, cols):
    half = x.shape[-1] // 2
    xr = apply_rope(x[..., :half], rope_angles(rows, half, ROPE_BASE))
    xc = apply_rope(x[..., half:], rope_angles(cols, half, ROPE_BASE))
    return jnp.concatenate([xr, xc], axis=-1)


def retention_scan(q, k, v, log_gamma, s0):
    B, L, H, _ = q.shape
    dv = v.shape[-1]
    C = RET_CHUNK
    n = L // C

    def to_chunks(t):
        return t.reshape(B, n, C, H, t.shape[-1]).transpose(1, 0, 3, 2, 4)

    idx = jnp.arange(C, dtype=jnp.float32)
    diff = idx[:, None] - idx[None, :]
    lg = log_gamma.astype(jnp.float32)
    decay_in = jnp.where(diff >= 0, jnp.exp(lg[:, None, None] * jnp.maximum(diff, 0.0)), 0.0)
    decay_q = jnp.exp(lg[:, None] * (idx + 1.0))[None, :, :, None]
    decay_k = jnp.exp(lg[:, None] * (C - 1.0 - idx))[None, :, :, None]
    decay_c = jnp.exp(lg * C)[None, :, None, None]

    def step(s, qkv):
        qc, kc, vc = (t.astype(jnp.float32) for t in qkv)
        scores = jnp.einsum("bhid,bhjd->bhij", qc, kc) * decay_in
        o = (jnp.einsum("bhij,bhjv->bhiv", scores, vc)
             + jnp.einsum("bhid,bhdv->bhiv", qc, s) * decay_q)
        s_new = s * decay_c + jnp.einsum("bhjd,bhjv->bhdv", kc * decay_k, vc)
        return s_new, o

    s_fin, o = lax.scan(step, s0, (to_chunks(q), to_chunks(k), to_chunks(v)))
    o = o.transpose(1, 0, 3, 2, 4).reshape(B, L, H, dv)
    return o, s_fin


def retention_mixer(h_ctx, h_lat, w_in, w_out, decay_fwd, decay_bwd, ang_lat, need_ctx_out):
    qk_w = RET_HEADS * RET_DK
    v_w = RET_HEADS * RET_DV

    def project(h):
        B, L, _ = h.shape
        q, k, v, g = jnp.split(h @ w_in, [qk_w, 2 * qk_w, 2 * qk_w + v_w], axis=-1)
        return (q.reshape(B, L, RET_HEADS, RET_DK),
                k.reshape(B, L, RET_HEADS, RET_DK) * (RET_DK ** -0.5),
                v.reshape(B, L, RET_HEADS, RET_DV),
                g)

    qc, kc, vc, gc = project(h_ctx)
    ql, kl, vl, gl = project(h_lat)
    ql = apply_rope(ql, ang_lat)
    kl = apply_rope(kl, ang_lat)
    lg_f = jax.nn.log_sigmoid(decay_fwd.astype(jnp.float32))
    lg_b = jax.nn.log_sigmoid(decay_bwd.astype(jnp.float32))
    B = h_lat.shape[0]
    s0 = jnp.zeros((B, RET_HEADS, RET_DK, RET_DV), jnp.float32)

    def flip(t):
        return jnp.flip(t, axis=1)

    oc_f, sc_f = retention_scan(qc, kc, vc, lg_f, s0)
    oc_b, sc_b = retention_scan(flip(qc), flip(kc), flip(vc), lg_b, s0)
    ol_f, _ = retention_scan(ql, kl, vl, lg_f, sc_f)
    ol_b, _ = retention_scan(flip(ql), flip(kl), flip(vl), lg_b, sc_b)

    def readout(o, g):
        Bo, Lo = o.shape[:2]
        y = o * lax.rsqrt(jnp.mean(o * o, axis=-1, keepdims=True) + NORM_EPS)
        y = y.reshape(Bo, Lo, RET_HEADS * RET_DV).astype(g.dtype)
        return (jax.nn.silu(g) * y) @ w_out

    out_lat = readout(ol_f + flip(ol_b), gl)
    out_ctx = readout(oc_f + flip(oc_b), gc) if need_ctx_out else None
    return out_ctx, out_lat


def window_attention_mixer(h_ctx, h_lat, w_in, w_out, sink, rows, cols, need_ctx_out):
    Hq, Hkv, G, dh = ATT_Q_HEADS, ATT_KV_HEADS, ATT_GROUP, ATT_HEAD_DIM
    scale = dh ** -0.5
    f32 = jnp.float32

    def project(h):
        B, L, _ = h.shape
        q, k, v = jnp.split(h @ w_in, [Hq * dh, (Hq + Hkv) * dh], axis=-1)
        return q.reshape(B, L, Hq, dh), k.reshape(B, L, Hkv, dh), v.reshape(B, L, Hkv, dh)

    qc, kc, vc = project(h_ctx)
    ql, kl, vl = project(h_lat)
    ql = axial_rope(ql, rows, cols)
    kl = axial_rope(kl, rows, cols)
    B, L = ql.shape[:2]
    nb = L // ATT_BLOCK
    sink_f = sink.astype(f32).reshape(Hkv, G)

    qb = ql.reshape(B, nb, ATT_BLOCK, Hkv, G, dh)

    def band(t):
        tp = jnp.pad(t, ((0, 0), (ATT_BLOCK, ATT_BLOCK), (0, 0), (0, 0)))
        tp = tp.reshape(B, nb + 2, ATT_BLOCK, Hkv, dh)
        return jnp.concatenate([tp[:, :-2], tp[:, 1:-1], tp[:, 2:]], axis=2)

    kb, vb = band(kl), band(vl)
    s_loc = jnp.einsum("bnqkgd,bnskd->bnkgqs", qb, kb, preferred_element_type=f32) * scale
    s_ctx = jnp.einsum("bnqkgd,bckd->bnkgqc", qb, kc, preferred_element_type=f32) * scale
    qi = jnp.arange(ATT_BLOCK)[:, None]
    kj = jnp.arange(3 * ATT_BLOCK)[None, :]
    kpos = (jnp.arange(nb)[:, None, None] - 1) * ATT_BLOCK + kj[None]
    mask = (jnp.abs(kj - ATT_BLOCK - qi)[None] <= WINDOW) & (kpos >= 0) & (kpos < L)
    s_loc = jnp.where(mask[None, :, None, None], s_loc, NEG_INF)
    sink_l = sink_f[None, None, :, :, None, None]
    m = jnp.maximum(jnp.maximum(s_loc.max(-1, keepdims=True), s_ctx.max(-1, keepdims=True)), sink_l)
    p_loc = jnp.exp(s_loc - m)
    p_ctx = jnp.exp(s_ctx - m)
    denom = p_loc.sum(-1, keepdims=True) + p_ctx.sum(-1, keepdims=True) + jnp.exp(sink_l - m)
    o = (jnp.einsum("bnkgqs,bnskd->bnqkgd", p_loc.astype(vl.dtype), vb, preferred_element_type=f32)
         + jnp.einsum("bnkgqc,bckd->bnqkgd", p_ctx.astype(vc.dtype), vc, preferred_element_type=f32))
    o = o / jnp.moveaxis(denom, 4, 2)
    out_lat = o.reshape(B, L, Hq * dh).astype(h_lat.dtype) @ w_out

    if need_ctx_out:
        Bc, Lc = qc.shape[:2]
        qcg = qc.reshape(Bc, Lc, Hkv, G, dh)
        s = jnp.einsum("bqkgd,bckd->bkgqc", qcg, kc, preferred_element_type=f32) * scale
        sink_c = sink_f[None, :, :, None, None]
        mc = jnp.maximum(s.max(-1, keepdims=True), sink_c)
        p = jnp.exp(s - mc)
        denom_c = p.sum(-1, keepdims=True) + jnp.exp(sink_c - mc)
        oc = jnp.einsum("bkgqc,bckd->bqkgd", p.astype(vc.dtype), vc, preferred_element_type=f32)
        oc = oc / jnp.moveaxis(denom_c, 3, 1)
        out_ctx = oc.reshape(Bc, Lc, Hq * dh).astype(h_ctx.dtype) @ w_out
    else:
        out_ctx = None
    return out_ctx, out_lat


def squared_relu_mlp(h, w1, w2):
    return jnp.square(jax.nn.relu(h @ w1)) @ w2


def setup_inputs(seed: int = 0) -> dict:
    key = jax.random.key(seed)
    ks = jax.random.split(key, 20)
    f32 = jnp.float32
    D = D_MODEL

    def nrm(k, shape, fan_in):
        return jax.random.normal(k, shape, f32) * (fan_in ** -0.5)

    ret_in_w = RET_HEADS * (2 * RET_DK + 2 * RET_DV)
    att_in_w = (ATT_Q_HEADS + 2 * ATT_KV_HEADS) * ATT_HEAD_DIM
    gamma = 1.0 - 2.0 ** (-5.0 - jnp.arange(RET_HEADS, dtype=f32))
    decay_logit = jnp.log(gamma) - jnp.log1p(-gamma)

    return {
        "x": jax.random.normal(ks[0], (BATCH, SEQ, D), f32),
        "c": jax.random.normal(ks[1], (BATCH, D), f32),
        "ctx": jax.random.normal(ks[2], (BATCH, CTX_LEN, D), f32),
        "c_ctx": jax.random.normal(ks[3], (D,), f32),
        "ada_w": nrm(ks[4], (DEPTH, D, 6 * D), D),
        "ada_b": 0.02 * jax.random.normal(ks[5], (DEPTH, 6 * D), f32),
        "norm_mix_g": 1.0 + 0.02 * jax.random.normal(ks[6], (DEPTH, D), f32),
        "norm_mlp_g": 1.0 + 0.02 * jax.random.normal(ks[7], (DEPTH, D), f32),
        "mlp_w1": nrm(ks[8], (DEPTH, D, D_FF), D),
        "mlp_w2": nrm(ks[9], (DEPTH, D_FF, D), D_FF),
        "ret_w_in": nrm(ks[10], (N_RET_LAYERS, D, ret_in_w), D),
        "ret_w_out": nrm(ks[11], (N_RET_LAYERS, RET_HEADS * RET_DV, D), RET_HEADS * RET_DV),
        "ret_decay_fwd": decay_logit[None, :] + 0.1 * jax.random.normal(ks[12], (N_RET_LAYERS, RET_HEADS), f32),
        "ret_decay_bwd": decay_logit[None, :] + 0.1 * jax.random.normal(ks[13], (N_RET_LAYERS, RET_HEADS), f32),
        "attn_w_in": nrm(ks[14], (N_ATT_LAYERS, D, att_in_w), D),
        "attn_w_out": nrm(ks[15], (N_ATT_LAYERS, ATT_Q_HEADS * ATT_HEAD_DIM, D), ATT_Q_HEADS * ATT_HEAD_DIM),
        "attn_sink": jax.random.normal(ks[16], (N_ATT_LAYERS, ATT_Q_HEADS), f32),
        "final_norm_g": 1.0 + 0.02 * jax.random.normal(ks[17], (D,), f32),
    }


def reference(x, c, ctx, c_ctx, ada_w, ada_b, norm_mix_g, norm_mlp_g, mlp_w1, mlp_w2,
              ret_w_in, ret_w_out, ret_decay_fwd, ret_decay_bwd,
              attn_w_in, attn_w_out, attn_sink, final_norm_g):
    L = x.shape[1]
    ROWS = L // GRID_W
    rows = jnp.repeat(jnp.arange(ROWS), GRID_W)
    cols = jnp.arange(ROWS * GRID_W) % GRID_W
    ret_ang = rope_angles(jnp.arange(L), RET_DK, RET_ROPE_BASE)
    cond_lat = jax.nn.silu(c)
    cond_ctx = jax.nn.silu(c_ctx)
    h_ctx = ctx

    for i in range(DEPTH):
        last = i == DEPTH - 1
        j = i // N_MIXERS
        mod_lat = (cond_lat @ ada_w[i] + ada_b[i])[:, None, :]
        mod_ctx = (cond_ctx @ ada_w[i] + ada_b[i])[None, None, :]
        sh1_l, sc1_l, g1_l, sh2_l, sc2_l, g2_l = jnp.split(mod_lat, 6, axis=-1)
        sh1_c, sc1_c, g1_c, sh2_c, sc2_c, g2_c = jnp.split(mod_ctx, 6, axis=-1)

        a_lat = modulate(rms_norm(x, norm_mix_g[i]), sh1_l, sc1_l)
        a_ctx = modulate(rms_norm(h_ctx, norm_mix_g[i]), sh1_c, sc1_c)
        if i % N_MIXERS == 0:
            o_ctx, o_lat = retention_mixer(a_ctx, a_lat, ret_w_in[j], ret_w_out[j],
                                           ret_decay_fwd[j], ret_decay_bwd[j], ret_ang, not last)
        else:
            o_ctx, o_lat = window_attention_mixer(a_ctx, a_lat, attn_w_in[j], attn_w_out[j],
                                                  attn_sink[j], rows, cols, not last)

        x = x + g1_l * o_lat
        x = x + g2_l * squared_relu_mlp(modulate(rms_norm(x, norm_mlp_g[i]), sh2_l, sc2_l),
                                        mlp_w1[i], mlp_w2[i])
        if not last:
            h_ctx = h_ctx + g1_c * o_ctx
            h_ctx = h_ctx + g2_c * squared_relu_mlp(modulate(rms_norm(h_ctx, norm_mlp_g[i]), sh2_c, sc2_c),
                                                    mlp_w1[i], mlp_w2[i])

    return rms_norm(x, final_norm_g)
```

```python
import numpy as np
from contextlib import ExitStack
import concourse.bass as bass
import concourse.mybir as mybir
from concourse.bass_utils import run_bass_kernel_spmd

F32 = mybir.dt.float32
BF16 = mybir.dt.bfloat16
AF = mybir.ActivationFunctionType
ALU = mybir.AluOpType
AX = mybir.AxisListType

D = 2048
KC = 16
DFF = 8192
NCTX = 256
R_ALL = 4352
R_FULL = 2432
NT_ALL = 34
NT_FULL = 19
EPS = 1e-6
NS_DMA = 8
ARENA_BYTES = 211968


class Ins:
    __slots__ = ("eng", "fn", "waits", "signal", "sem", "val", "prev_val", "is_dma")

    def __init__(self, eng, fn, is_dma):
        self.eng = eng
        self.fn = fn
        self.waits = []
        self.signal = is_dma
        self.sem = None
        self.val = 0
        self.prev_val = 0
        self.is_dma = is_dma


class Buf:
    __slots__ = ("ap", "w", "r", "name")

    def __init__(self, ap, name=""):
        self.ap = ap
        self.w = None
        self.r = []
        self.name = name


class Sched:
    ENGS = ("sync", "gpsimd", "tensor", "vector", "scalar")

    def __init__(self):
        self.streams = {e: [] for e in self.ENGS}
        self.bar = {e: [] for e in self.ENGS}

    def op(self, eng, fn, reads=(), writes=(), dma=False):
        ins = Ins(eng, fn, dma)
        deps = []
        for b in reads:
            if b.w is not None:
                deps.append(b.w)
        for b in writes:
            if b.w is not None:
                deps.append(b.w)
            deps.extend(b.r)
        if self.bar[eng]:
            deps.extend(self.bar[eng])
            self.bar[eng] = []
        seen = set()
        for d in deps:
            if id(d) in seen or d is ins:
                continue
            seen.add(id(d))
            if d.eng == eng and not d.is_dma and eng == "tensor":
                continue
            ins.waits.append(d)
            d.signal = True
        for b in reads:
            if not dma:
                b.r = [x for x in b.r if x.is_dma or x.eng != eng]
            b.r.append(ins)
        for b in writes:
            b.w = ins
            b.r = []
        self.streams[eng].append(ins)
        return ins

    def dma(self, q, out, in_, reads=(), writes=(), slow=False):
        if slow:
            return self.op(q, lambda e: e.dma_start(out=out, in_=in_, allow_slow_non_contiguous=True),
                           reads, writes, dma=True)
        return self.op(q, lambda e: e.dma_start(out=out, in_=in_), reads, writes, dma=True)

    def barrier(self):
        deps = []
        for e in self.ENGS:
            st = self.streams[e]
            nd = 0
            last_c = None
            for ins in reversed(st):
                if ins.is_dma:
                    if nd < NS_DMA:
                        deps.append(ins)
                        nd += 1
                elif last_c is None:
                    last_c = ins
                    deps.append(ins)
                if nd >= NS_DMA and last_c is not None:
                    break
        for e in self.ENGS:
            self.bar[e] = list(deps)

    def finalize(self, eng_sems, dma_sems):
        for e in self.ENGS:
            cnt = 0
            di = 0
            for ins in self.streams[e]:
                if ins.is_dma:
                    s = di % NS_DMA
                    k = di // NS_DMA
                    ins.sem = dma_sems[e][s]
                    ins.val = 16 * (k + 1)
                    ins.prev_val = 16 * k
                    di += 1
                elif ins.signal:
                    cnt += 1
                    ins.sem = eng_sems[e]
                    ins.val = cnt

    def emit(self, ename, eng):
        known = {}
        for ins in self.streams[ename]:
            waits = {}
            for d in ins.waits:
                k = id(d.sem)
                if k not in waits or waits[k][1] < d.val:
                    waits[k] = (d.sem, d.val)
            if ins.is_dma and ins.prev_val > 0:
                k = id(ins.sem)
                if k not in waits or waits[k][1] < ins.prev_val:
                    waits[k] = (ins.sem, ins.prev_val)
            for k, (sem, val) in waits.items():
                if known.get(k, 0) >= val:
                    continue
                eng.wait_ge(sem, val)
                known[k] = val
            bi = ins.fn(eng)
            if ins.is_dma:
                bi.then_inc(ins.sem, 16)
            elif ins.signal:
                bi.then_inc(ins.sem, 1)


class Arena:
    def __init__(self, ap):
        self.base = ap
        self.off = 0

    def reset(self):
        self.off = 0

    def alloc(self, shape, dt, name=""):
        esz = 4 if dt == F32 else 2
        n = int(np.prod(shape[1:]))
        nbytes = (n * esz + 31) // 32 * 32
        assert self.off + nbytes <= ARENA_BYTES, (name, self.off, nbytes)
        a = self.base[:, self.off // 2: self.off // 2 + n * esz // 2]
        self.off += nbytes
        if dt != BF16:
            a = a.bitcast(dt)
        if len(shape) == 3:
            a = a.rearrange("p (a b) -> p a b", a=shape[1])
        elif len(shape) == 4:
            a = a.rearrange("p (a b c) -> p a b c", a=shape[1], b=shape[2])
        if shape[0] != 128:
            a = a[0:shape[0]]
        return Buf(a, name)


def build_program(debug=None):
    nc = bass.Bass("TRN2", target_bir_lowering=False)
    dbg = set(debug or [])

    def din(name, shape, dt=F32):
        return nc.dram_tensor(name, list(shape), dt, kind="ExternalInput").ap()

    def dscr(name, shape, dt):
        kind = "ExternalOutput" if name in dbg else "Internal"
        return nc.dram_tensor(name, list(shape), dt, kind=kind).ap()

    xin = din("xin", [R_ALL, D])
    cvec = din("cvec", [2, D])
    ada_w = din("ada_w", [2, D, 6 * D])
    ada_b = din("ada_b", [2, 6 * D])
    norm_mix_g = din("norm_mix_g", [2, D])
    norm_mlp_g = din("norm_mlp_g", [2, D])
    mlp_w1 = din("mlp_w1", [2, D, DFF])
    mlp_w2 = din("mlp_w2", [2, DFF, D])
    ret_w_in = din("ret_w_in", [D, 12288])
    ret_w_out = din("ret_w_out", [4096, D])
    dec = din("dec", [1, 16])
    attn_w_in = din("attn_w_in", [D, 3072])
    attn_w_out = din("attn_w_out", [D, D])
    attn_sink = din("attn_sink", [1, 16])
    final_g = din("final_g", [1, D])
    consts = din("consts", [128, 1024 + 8])
    rope_r = din("rope_r", [R_ALL, 1024])
    rope_a = din("rope_a", [R_FULL, 1024])
    amask = din("amask", [128, 768])
    out = nc.dram_tensor("out", [2048, D], F32, kind="ExternalOutput").ap()

    mod_d = dscr("mod_d", [2, 2, 6 * D], F32)
    qT_d = dscr("qT_d", [16, 128, R_FULL], BF16)
    kT_d = dscr("kT_d", [16, 128, R_FULL], BF16)
    k_d = dscr("k_d", [R_ALL, 2048], BF16)
    v_d = dscr("v_d", [R_ALL, 4096], BF16)
    sg_d = dscr("sg_d", [R_FULL, 4096], BF16)
    yT_d = dscr("yT_d", [32, 128, R_FULL], BF16)
    x1_d = dscr("x1_d", [R_FULL, D], F32)
    x2_d = dscr("x2_d", [R_FULL, D], F32)
    aq_d = dscr("aq_d", [16, 128, R_FULL], BF16)
    akT_d = dscr("akT_d", [4, 128, R_FULL], BF16)
    av_d = dscr("av_d", [R_FULL, 512], BF16)
    oT_d = dscr("oT_d", [16, 128, 2048], BF16)
    x3_d = dscr("x3_d", [2048, D], F32)
    x4_d = dscr("x4_d", [2048, D], F32)

    S = Sched()

    with ExitStack() as es:
        arena_t = es.enter_context(nc.sbuf_tensor("arena", [128, ARENA_BYTES // 2], BF16))
        psA = es.enter_context(nc.psum_tensor("psA", [128, 8, 512], F32))
        eng_sems = {e: es.enter_context(nc.semaphore(f"sem_{e}")) for e in Sched.ENGS}
        dma_sems = {e: [es.enter_context(nc.semaphore(f"dsem_{e}{i}")) for i in range(NS_DMA)]
                    for e in ("sync", "gpsimd")}
        A = Arena(arena_t)
        PS = [Buf(psA[:, i, :], f"ps{i}") for i in range(8)]
        ps_bf = [psA[:, i, :].bitcast(BF16) for i in range(8)]

        DRAMB = {}

        def dbuf(name):
            if name not in DRAMB:
                DRAMB[name] = Buf(None, name)
            return DRAMB[name]

        rr = {"ps": 0}

        def next_ps(lo=0, hi=4):
            i = lo + rr.setdefault((lo, hi), 0) % (hi - lo)
            rr[(lo, hi)] += 1
            return i

        P_ident = A.alloc([128, 128], F32, "ident")
        P_identb = A.alloc([128, 128], BF16, "identb")
        P_c = A.alloc([128, 1024 + 8], F32, "consts")
        P_modF = A.alloc([128, 2, 2, 96], F32, "modF")
        P_gF = A.alloc([128, 2, 2, 16], F32, "gF")
        P_AB = A.alloc([128, 4, 2, 32], F32, "AB")
        P_dec = A.alloc([128, 16], F32, "dec")
        P_lg = A.alloc([128, 16], F32, "lg")
        P_sink = A.alloc([128, 16], F32, "sink")
        P_eps = A.alloc([128, 1], F32, "eps")
        persist_off = A.off

        S.dma("sync", P_c.ap, consts, writes=[P_c])
        S.dma("sync", P_ident.ap, consts[:, 0:128], writes=[P_ident])
        S.dma("sync", P_dec.ap, dec.partition_broadcast(128), writes=[P_dec])
        S.dma("sync", P_sink.ap, attn_sink.partition_broadcast(128), writes=[P_sink])
        S.op("vector", lambda e: e.tensor_copy(out=P_identb.ap, in_=P_ident.ap), [P_ident], [P_identb])
        S.op("vector", lambda e: e.memset(P_eps.ap, EPS), [], [P_eps])
        for l in range(2):
            S.dma("sync", P_gF.ap[:, l, 0, :], norm_mix_g[l].rearrange("(c p) -> p c", p=128), writes=[P_gF], slow=True)
            S.dma("sync", P_gF.ap[:, l, 1, :], norm_mlp_g[l].rearrange("(c p) -> p c", p=128), writes=[P_gF], slow=True)

        def cst(lo, hi):
            return P_c.ap[:, lo:hi]

        def phase_ada():
            A.off = persist_off
            cT = A.alloc([128, 16, 2], F32, "cT")
            cTb = A.alloc([128, 16, 2], BF16, "cTb")
            brow = A.alloc([2, 6 * D], F32, "brow")
            mrow = A.alloc([2, 6 * D], F32, "mrow")
            wb = [A.alloc([128, 16, 512], BF16, f"adaw{i}") for i in range(4)]
            for r in range(2):
                S.dma("sync", cT.ap[:, :, r], cvec[r].rearrange("(c p) -> p c", p=128), writes=[cT], slow=True)
            S.op("scalar", lambda e: e.activation(out=cTb.ap, in_=cT.ap, func=AF.Silu), [cT], [cTb])
            asteps = [(l, nb) for l in range(2) for nb in range(24)]
            st = {"p": 0}

            def aload(upto):
                while st["p"] <= min(upto, len(asteps) - 1):
                    l2, nb2 = asteps[st["p"]]
                    w2 = wb[st["p"] % 4]
                    S.dma("gpsimd", w2.ap, ada_w[l2, :, nb2 * 512:(nb2 + 1) * 512].rearrange("(c p) n -> p c n", p=128),
                          writes=[w2])
                    st["p"] += 1

            for l in range(2):
                S.dma("sync", brow.ap, ada_b[l:l + 1, :].partition_broadcast(2), writes=[brow])
                for nb in range(24):
                    i = l * 24 + nb
                    aload(i + 3)
                    w = wb[i % 4]
                    pi = next_ps(0, 4)
                    for kc in range(16):
                        S.op("tensor", lambda e, pi=pi, kc=kc, w=w: e.matmul(
                            PS[pi].ap[0:2, :], lhsT=cTb.ap[:, kc, :], rhs=w.ap[:, kc, :],
                            start=(kc == 0), stop=(kc == 15)), [cTb, w], [PS[pi]])
                    S.op("vector", lambda e, pi=pi, nb=nb: e.tensor_tensor(
                        out=mrow.ap[:, nb * 512:(nb + 1) * 512], in0=PS[pi].ap[0:2, :],
                        in1=brow.ap[:, nb * 512:(nb + 1) * 512], op=ALU.add), [PS[pi], brow], [mrow])
                S.dma("sync", mod_d[l], mrow.ap, reads=[mrow], writes=[dbuf("mod_d")])
            for l in range(2):
                for r in range(2):
                    S.dma("sync", P_modF.ap[:, l, r, :], mod_d[l, r].rearrange("(c p) -> p c", p=128),
                          reads=[dbuf("mod_d")], writes=[P_modF], slow=True)
            for l in range(2):
                for sub in range(2):
                    sh0 = 0 if sub == 0 else 48
                    sc0 = sh0 + 16
                    for r in range(2):
                        S.op("vector", lambda e, l=l, sub=sub, r=r, sc0=sc0: e.scalar_tensor_tensor(
                            out=P_AB.ap[:, l * 2 + sub, r, 0:16], in0=P_modF.ap[:, l, r, sc0:sc0 + 16], scalar=1.0,
                            in1=P_gF.ap[:, l, sub, :], op0=ALU.add, op1=ALU.mult), [P_modF, P_gF], [P_AB])
                        S.op("vector", lambda e, l=l, sub=sub, r=r, sh0=sh0: e.tensor_copy(
                            out=P_AB.ap[:, l * 2 + sub, r, 16:32], in_=P_modF.ap[:, l, r, sh0:sh0 + 16]),
                            [P_modF], [P_AB])
            S.op("scalar", lambda e: e.activation(out=P_lg.ap, in_=P_dec.ap, func=AF.Exp, scale=-1.0), [P_dec], [P_lg])
            S.op("scalar", lambda e: e.activation(out=P_lg.ap, in_=P_lg.ap, func=AF.Ln, bias=1.0), [P_lg], [P_lg])
            S.op("vector", lambda e: e.tensor_scalar(out=P_lg.ap, in0=P_lg.ap, scalar1=-1.0, scalar2=None,
                                                     op0=ALU.mult), [P_lg], [P_lg])
            S.barrier()

        def norm_mod_T(xt, row, lsub, aTg, gi, work):
            junk, ssq, xn = work["junk"], work["ssq"], work["xn"]
            S.op("scalar", lambda e: e.activation(out=junk.ap, in_=xt.ap, func=AF.Square,
                                                  scale=float(D ** -0.5), accum_out=ssq.ap), [xt], [xn, ssq])
            S.op("vector", lambda e: e.tensor_scalar(out=ssq.ap, in0=ssq.ap, scalar1=EPS, scalar2=None,
                                                     op0=ALU.add), [ssq], [ssq])
            S.op("scalar", lambda e: e.activation(out=ssq.ap, in_=ssq.ap, func=AF.Sqrt), [ssq], [ssq])
            S.op("vector", lambda e: e.reciprocal(out=ssq.ap, in_=ssq.ap), [ssq], [ssq])
            S.op("scalar", lambda e: e.activation(out=xn.ap, in_=xt.ap, func=AF.Copy, scale=ssq.ap[:, 0:1]),
                 [xt, ssq], [xn])
            for b4 in range(4):
                pi = next_ps(4, 8)
                for j in range(4):
                    kc = b4 * 4 + j
                    S.op("tensor", lambda e, pi=pi, j=j, kc=kc: e.transpose(
                        PS[pi].ap[:, j * 128:(j + 1) * 128], xn.ap[:, kc * 128:(kc + 1) * 128], P_ident.ap),
                        [xn, P_ident], [PS[pi]])
                for j in range(4):
                    kc = b4 * 4 + j
                    S.op("vector", lambda e, pi=pi, j=j, kc=kc: e.tensor_scalar(
                        out=aTg.ap[:, kc, gi * 128:(gi + 1) * 128], in0=PS[pi].ap[:, j * 128:(j + 1) * 128],
                        scalar1=P_AB.ap[:, lsub, row, kc:kc + 1], scalar2=P_AB.ap[:, lsub, row, 16 + kc:17 + kc],
                        op0=ALU.mult, op1=ALU.add), [PS[pi], P_AB], [aTg])

        def alloc_norm_work():
            xn = A.alloc([128, D], F32, "xn")
            return {"junk": xn, "ssq": A.alloc([128, 1], F32, "ssq"), "xn": xn}

        converted = set()
        GCTX = {}

        def wscratch(name, K, N, kp, ncols):
            nkq = K // 128 // kp
            nbi = N // ncols
            t = nc.dram_tensor("ws_" + name, [nkq, nbi, 128, kp * ncols], BF16, kind="Internal").ap()
            return {"ap": t, "kp": kp, "ncols": ncols, "name": name}

        def wload(ws, Wv, kq, bi, w):
            key = (ws["name"], kq, bi)
            kp, ncols = ws["kp"], ws["ncols"]
            sc = ws["ap"][kq, bi].rearrange("p (c n) -> p c n", c=kp)
            if key in converted:
                S.dma("sync", w.ap, sc, reads=[dbuf(key)], writes=[w])
            else:
                S.dma("gpsimd", w.ap, Wv[:, kq * kp:(kq + 1) * kp, bi * ncols:(bi + 1) * ncols], writes=[w])
                S.dma("gpsimd", sc, w.ap, reads=[w], writes=[dbuf(key)])
                converted.add(key)

        def wconvert(ws, Wv, stage, nhalf=1):
            kp, ncols = ws["kp"], ws["ncols"]
            nkq, nbi = ws["ap"].shape[0], ws["ap"].shape[1]
            hk = kp // nhalf
            i = 0
            for kq in range(nkq):
                for bi in range(nbi):
                    key = (ws["name"], kq, bi)
                    if key in converted:
                        continue
                    sc = ws["ap"][kq, bi].rearrange("p (c n) -> p c n", c=kp)
                    for hf in range(nhalf):
                        st = stage[i % len(stage)]
                        i += 1
                        sv = st.ap.rearrange("p a b -> p (a b)")[:, 0:hk * ncols].rearrange("p (c n) -> p c n", c=hk)
                        S.dma("gpsimd", sv,
                              Wv[:, kq * kp + hf * hk:kq * kp + (hf + 1) * hk, bi * ncols:(bi + 1) * ncols],
                              writes=[st])
                        S.dma("gpsimd", sc[:, hf * hk:(hf + 1) * hk, :], sv, reads=[st],
                              writes=[dbuf(key)])
                    converted.add(key)

        class WStream:
            def __init__(self, ws, W, wbufs, steps):
                self.ws, self.wbufs, self.steps = ws, wbufs, steps
                self.Wv = W.rearrange("(c p) n -> p c n", p=128)
                self.planned = 0
                self.pf = len(wbufs) - 1

            def get(self, i):
                upto = min(i + self.pf, len(self.steps) - 1)
                while self.planned <= upto:
                    kq, bi = self.steps[self.planned]
                    wload(self.ws, self.Wv, kq, bi, self.wbufs[self.planned % len(self.wbufs)])
                    self.planned += 1
                return self.wbufs[i % len(self.wbufs)]

        EARLY = set()

        def gemm_tok(groups, prep, ws, W, kcn, blocks, evac, wbufs, aTgs, post=None, kparts=1, pre=None,
                     prep_early=None, early=True):
            kp = kcn // kparts
            steps = []
            for g, grp in enumerate(groups):
                for bi, (c0, ncols) in enumerate(blocks(grp)):
                    for kq in range(kparts):
                        steps.append((g, bi, c0, ncols, kq))
            stream = WStream(ws, W, wbufs, [(kq, c0 // ncols) for (g, bi, c0, ncols, kq) in steps])
            pend = []
            GCTX["defer"] = lambda fn: pend.append([0, fn])
            prepped = set()
            nblk = {}
            for (g, bi, c0, ncols, kq) in steps:
                nblk[g] = max(nblk.get(g, 0), bi + 1)
            for i, (g, bi, c0, ncols, kq) in enumerate(steps):
                grp = groups[g]
                aTg = aTgs[g % len(aTgs)]
                w = stream.get(i)
                if bi == 0 and kq == 0 and g not in prepped:
                    if prep_early is not None:
                        prep_early(grp, aTg)
                    prep(grp, aTg)
                    prepped.add(g)
                if early and bi == nblk[g] - 1 and kq == 0 and g + 1 < len(groups) and (g + 1) not in prepped:
                    if prep_early is not None:
                        prep_early(groups[g + 1], aTgs[(g + 1) % len(aTgs)])
                        prepped.add(g + 1)
                        EARLY.add(g + 1)
                    elif len(aTgs) > 1:
                        prep(groups[g + 1], aTgs[(g + 1) % len(aTgs)])
                        prepped.add(g + 1)
                if bi == 0 and kq == 0 and g in EARLY:
                    EARLY.discard(g)
                    prep(grp, aTg)
                if kq == 0 and pre is not None:
                    pre(grp, bi, c0, ncols)
                for gi, t in enumerate(grp):
                    if kparts == 1:
                        pi = next_ps(0, 4)
                    else:
                        pi = (bi % 2) * 4 + gi
                    for k2 in range(kp):
                        kc = kq * kp + k2
                        S.op("tensor", lambda e, pi=pi, kc=kc, k2=k2, w=w, aTg=aTg, gi=gi, ncols=ncols: e.matmul(
                            PS[pi].ap[:, 0:ncols], lhsT=aTg.ap[:, kc, gi * 128:(gi + 1) * 128],
                            rhs=w.ap[:, k2, 0:ncols], start=(kc == 0), stop=(kc == kcn - 1)),
                            [aTg, w], [PS[pi]])
                    for p in pend:
                        p[0] += 1
                    while pend and pend[0][0] >= 2:
                        pend.pop(0)[1]()
                    if kq == kparts - 1:
                        evac(t, gi, bi, c0, ncols, pi)
                last = (i + 1 == len(steps)) or steps[i + 1][0] != g
                if last:
                    while pend:
                        pend.pop(0)[1]()
                    if post is not None:
                        post(grp)

        WS = {
            "ret_w_in": wscratch("ret_w_in", D, 12288, 16, 512),
            "ret_w_out": wscratch("ret_w_out", 4096, D, 16, 512),
            "mlp_w1_0": wscratch("mlp_w1_0", D, DFF, 16, 256),
            "mlp_w2_0": wscratch("mlp_w2_0", DFF, D, 16, 512),
            "mlp_w1_1": wscratch("mlp_w1_1", D, DFF, 16, 256),
            "mlp_w2_1": wscratch("mlp_w2_1", DFF, D, 16, 512),
            "attn_w_in": wscratch("attn_w_in", D, 3072, 16, 512),
            "attn_w_out": wscratch("attn_w_out", D, D, 16, 512),
        }

        def wview(W):
            return W.rearrange("(c p) n -> p c n", p=128)

        def rope_block(xs, tab, gi_unused, outb, B, work):
            t1, t2 = work["t1"], work["t2"]
            n = 512 // (2 * B)
            xv = xs.ap.rearrange("p (n two b) -> p n two b", two=2, b=B)
            sv = tab.ap[:, 512:1024].rearrange("p (n two b) -> p n two b", two=2, b=B)
            t2v = t2.ap.rearrange("p (n two b) -> p n two b", two=2, b=B)
            S.op("vector", lambda e: e.tensor_tensor(out=t1.ap, in0=xs.ap, in1=tab.ap[:, 0:512], op=ALU.mult),
                 [xs, tab], [t1])
            S.op("vector", lambda e: e.tensor_tensor(out=t2v[:, :, 0, :], in0=xv[:, :, 1, :], in1=sv[:, :, 0, :],
                                                     op=ALU.mult), [xs, tab], [t2])
            S.op("vector", lambda e: e.tensor_tensor(out=t2v[:, :, 1, :], in0=xv[:, :, 0, :], in1=sv[:, :, 1, :],
                                                     op=ALU.mult), [xs, tab], [t2])
            S.op("vector", lambda e: e.tensor_tensor(out=outb[1], in0=t1.ap, in1=t2.ap, op=ALU.add),
                 [t1, t2], [outb[0]])

        def phase_ret_proj():
            A.off = persist_off
            wk = alloc_norm_work()
            xts = [A.alloc([128, D], F32, f"xt{i}") for i in range(2)]
            aTgs = [A.alloc([128, 16, 512], BF16, f"aTg{i}") for i in range(2)]
            wbufs = [A.alloc([128, 16, 512], BF16, f"w{i}") for i in range(3)]
            tabs = [A.alloc([128, 1024], F32, f"tab{i}") for i in range(4)]
            xs_b = [A.alloc([128, 512], F32, f"xs{i}") for i in range(2)]
            rw = {"t1": A.alloc([128, 512], F32, "t1"), "t2": A.alloc([128, 512], F32, "t2")}
            ob = [A.alloc([128, 512], BF16, f"ob{i}") for i in range(6)]
            qTg = A.alloc([128, 16, 512], BF16, "qTg")
            kTg = A.alloc([128, 16, 512], BF16, "kTg")
            cnt = {"x": 0, "xs": 0, "ob": 0}
            full_groups = [[0, 1, 2, 3], [4, 5, 6, 7], [8, 9, 10, 11], [12, 13, 14, 15], [16, 17, 18]]
            far_groups = [[19, 20, 21, 22], [23, 24, 25, 26], [27, 28, 29, 30], [31, 32, 33]]
            groups = full_groups + far_groups
            tabmap = {}

            def prep(grp, aTg):
                for gi, t in enumerate(grp):
                    xt = xts[cnt["x"] % 2]
                    cnt["x"] += 1
                    S.dma("sync", xt.ap, xin[t * 128:(t + 1) * 128, :], writes=[xt])
                    norm_mod_T(xt, 1 if t < 2 else 0, 0, aTg, gi, wk)
                    tb = tabs[gi]
                    S.dma("sync", tb.ap, rope_r[t * 128:(t + 1) * 128, :], writes=[tb])
                    tabmap[t] = tb

            def blocks(grp):
                if grp[0] >= NT_FULL:
                    return [(2048 + i * 512, 512) for i in range(4)] + [(4096 + i * 512, 512) for i in range(8)]
                return [(i * 512, 512) for i in range(24)]

            def evac(t, gi, bi, c0, ncols, pi):
                full = t < NT_FULL
                r0 = t * 128
                if c0 < 4096:
                    isq = c0 < 2048
                    xs = xs_b[cnt["xs"] % 2]
                    cnt["xs"] += 1
                    S.op("scalar", lambda e: e.activation(out=xs.ap, in_=PS[pi].ap, func=AF.Copy,
                                                          scale=1.0 if isq else 0.0625), [PS[pi]], [xs])
                    o = ob[cnt["ob"] % 6]
                    cnt["ob"] += 1
                    rope_block(xs, tabmap[t], gi, (o, o.ap), 128, rw)
                    cb = (c0 % 2048) // 128
                    if not isq:
                        S.dma("sync", k_d[r0:r0 + 128, c0 - 2048:c0 - 2048 + 512], o.ap, reads=[o],
                              writes=[dbuf("k_d")])
                    if full:
                        tg = qTg if isq else kTg

                        def tr(o=o, tg=tg, cb=cb, gi=gi):
                            pj = next_ps(4, 8)
                            for j in range(4):
                                S.op("tensor", lambda e, j=j, pj=pj, o=o: e.transpose(
                                    ps_bf[pj][:, j * 128:(j + 1) * 128], o.ap[:, j * 128:(j + 1) * 128], P_identb.ap),
                                    [o, P_identb], [PS[pj]])
                            S.op("scalar", lambda e, pj=pj, tg=tg, cb=cb, gi=gi: e.activation(
                                out=tg.ap[:, cb:cb + 4, gi * 128:(gi + 1) * 128],
                                in_=ps_bf[pj][:, 0:512].rearrange("p (c t) -> p c t", c=4), func=AF.Copy),
                                [PS[pj]], [tg])
                        GCTX["defer"](tr)
                elif c0 < 8192:
                    o = ob[cnt["ob"] % 6]
                    cnt["ob"] += 1
                    S.op("scalar", lambda e: e.activation(out=o.ap, in_=PS[pi].ap, func=AF.Copy), [PS[pi]], [o])
                    S.dma("sync", v_d[r0:r0 + 128, c0 - 4096:c0 - 4096 + 512], o.ap, reads=[o], writes=[dbuf("v_d")])
                else:
                    o = ob[cnt["ob"] % 6]
                    cnt["ob"] += 1
                    S.op("scalar", lambda e: e.activation(out=o.ap, in_=PS[pi].ap, func=AF.Silu), [PS[pi]], [o])
                    S.dma("sync", sg_d[r0:r0 + 128, c0 - 8192:c0 - 8192 + 512], o.ap, reads=[o],
                          writes=[dbuf("sg_d")])

            stage = [A.alloc([128, 8, 512], BF16, f"stg{i}") for i in range(2)]

            def post(grp):
                if grp[0] == 0:
                    wconvert(WS["ret_w_out"], wview(ret_w_out), stage, nhalf=2)
                    wconvert(WS["mlp_w1_0"], wview(mlp_w1[0]), stage, nhalf=1)
                    wconvert(WS["mlp_w2_0"], wview(mlp_w2[0]), stage, nhalf=2)
                if grp[0] >= NT_FULL:
                    return
                r0 = grp[0] * 128
                n = len(grp) * 128
                S.dma("sync", qT_d[:, :, r0:r0 + n].rearrange("c p t -> p c t"), qTg.ap[:, :, 0:n], reads=[qTg],
                      writes=[dbuf("qT_d")])
                S.dma("sync", kT_d[:, :, r0:r0 + n].rearrange("c p t -> p c t"), kTg.ap[:, :, 0:n], reads=[kTg],
                      writes=[dbuf("kT_d")])

            gemm_tok(groups, prep, WS["ret_w_in"], ret_w_in, 16, blocks, evac, wbufs, aTgs, post)
            S.barrier()

        def phase_ret():
            A.off = persist_off
            qT = A.alloc([128, 2, R_FULL], BF16, "qT")
            kT = A.alloc([128, 2, R_FULL], BF16, "kT")
            kk = A.alloc([128, NT_ALL, 256], BF16, "kk")
            vv = A.alloc([128, NT_ALL, 512], BF16, "vv")
            sg = A.alloc([128, NT_FULL, 512], BF16, "sg")
            snap = A.alloc([128, NT_FULL, 1024], BF16, "snap")
            yT = A.alloc([128, 4, R_FULL], BF16, "yT")
            Sst = [[A.alloc([128, 512], F32, f"S{i}{dc}") for dc in range(2)] for i in range(2)]
            Sbf = [[[A.alloc([128, 512], BF16, f"Sbf{i}{p}{dc}") for dc in range(2)] for p in range(2)]
                   for i in range(2)]
            maskc = A.alloc([128, 128], F32, "maskc")
            mtmp = A.alloc([128, 128], F32, "mtmp")
            dq = [A.alloc([128, 128], F32, f"dq{i}") for i in range(2)]
            dsc = A.alloc([128, 4], F32, "dsc")
            qs = [A.alloc([128, 2, 128], BF16, f"qs{i}") for i in range(4)]
            ks = [A.alloc([128, 256], BF16, f"ks{i}") for i in range(5)]
            pT = [A.alloc([128, 128], BF16, f"pT{i}") for i in range(2)]
            qsc = [A.alloc([128, 2, R_FULL], BF16, f"qsc{i}") for i in range(2)]
            yb = [A.alloc([128, 512], BF16, f"yb{i}") for i in range(2)]
            junk = A.alloc([128, 512], BF16, "junkr")
            ssq = [A.alloc([128, 1], F32, f"ssqr{i}") for i in range(2)]
            cn = {"qs": 0, "ks": 0, "pT": 0, "ow": 0}
            k_v = k_d.rearrange("(t p) c -> p t c", p=128)
            v_v = v_d.rearrange("(t p) c -> p t c", p=128)
            sg_v = sg_d.rearrange("(t p) c -> p t c", p=128)

            par = {0: 0, 1: 0}
            kvp = {}

            def plan_kv(t, di, banks=(0, 4)):
                kb = ks[cn["ks"] % 5]
                cn["ks"] += 1
                S.op("vector", lambda e: e.tensor_scalar(out=kb.ap, in0=kk.ap[:, t, :], scalar1=dsc.ap[:, di:di + 1],
                                                         scalar2=None, op0=ALU.mult), [kk, dsc], [kb])
                pis = []
                for dc in range(2):
                    pi = next_ps(*banks)
                    pis.append(pi)
                    S.op("tensor", lambda e, pi=pi, dc=dc: e.matmul(
                        PS[pi].ap, lhsT=kb.ap[:, dc * 128:(dc + 1) * 128], rhs=vv.ap[:, t, :], start=True, stop=True),
                        [kb, vv], [PS[pi]])
                kvp[(t, di)] = pis

            def state_update(t, di, dst):
                pis = kvp.pop((t, di))
                for dc in range(2):
                    pi = pis[dc]
                    S.op("vector", lambda e, pi=pi, dc=dc: e.scalar_tensor_tensor(
                        out=Sst[di][dc].ap, in0=Sst[di][dc].ap, scalar=dsc.ap[:, 2 + di:3 + di],
                        in1=PS[pi].ap, op0=ALU.mult, op1=ALU.add), [Sst[di][dc], dsc, PS[pi]], [Sst[di][dc]])
                    if dst is not None:
                        S.op("scalar", lambda e, dc=dc: e.activation(out=dst[1][dc], in_=Sst[di][dc].ap,
                                                                     func=AF.Copy), [Sst[di][dc]], [dst[0][dc]])

            class _QV:
                def __init__(self, buf, ap):
                    self.buf, self.ap = buf, ap

            def q_scaled(t, di):
                return _QV(qsc[di], qsc[di].ap[:, :, t * 128:(t + 1) * 128])

            for h in range(8):
                S.dma("sync", qT.ap, qT_d[2 * h:2 * h + 2].rearrange("c p t -> p c t"), reads=[dbuf("qT_d")],
                      writes=[qT])
                S.dma("sync", kT.ap, kT_d[2 * h:2 * h + 2].rearrange("c p t -> p c t"), reads=[dbuf("kT_d")],
                      writes=[kT])
                for t0 in range(0, NT_ALL, 9):
                    t1 = min(NT_ALL, t0 + 9)
                    S.dma("sync", kk.ap[:, t0:t1, :], k_v[:, t0:t1, h * 256:(h + 1) * 256], reads=[dbuf("k_d")],
                          writes=[kk])
                    S.dma("sync", vv.ap[:, t0:t1, :], v_v[:, t0:t1, h * 512:(h + 1) * 512], reads=[dbuf("v_d")],
                          writes=[vv])
                for t0 in range(0, NT_FULL, 10):
                    t1 = min(NT_FULL, t0 + 10)
                    S.dma("sync", sg.ap[:, t0:t1, :], sg_v[:, t0:t1, h * 512:(h + 1) * 512], reads=[dbuf("sg_d")],
                          writes=[sg])
                lgf = P_lg.ap[:, h:h + 1]
                lgb = P_lg.ap[:, 8 + h:9 + h]
                S.op("scalar", lambda e, lgf=lgf: e.activation(out=maskc.ap, in_=cst(128, 256), func=AF.Exp, scale=lgf),
                     [P_c, P_lg], [maskc])
                S.op("vector", lambda e: e.tensor_tensor(out=maskc.ap, in0=maskc.ap, in1=cst(256, 384), op=ALU.mult),
                     [maskc, P_c], [maskc])
                S.op("scalar", lambda e, lgb=lgb: e.activation(out=mtmp.ap, in_=cst(384, 512), func=AF.Exp, scale=lgb),
                     [P_c, P_lg], [mtmp])
                S.op("vector", lambda e: e.tensor_tensor(out=mtmp.ap, in0=mtmp.ap, in1=cst(512, 640), op=ALU.mult),
                     [mtmp, P_c], [mtmp])
                S.op("vector", lambda e: e.tensor_tensor(out=maskc.ap, in0=maskc.ap, in1=mtmp.ap, op=ALU.add),
                     [maskc, mtmp], [maskc])
                S.op("scalar", lambda e, lgf=lgf: e.activation(out=dq[0].ap, in_=cst(640, 768), func=AF.Exp, scale=lgf),
                     [P_c, P_lg], [dq[0]])
                S.op("scalar", lambda e, lgb=lgb: e.activation(out=dq[1].ap, in_=cst(768, 896), func=AF.Exp, scale=lgb),
                     [P_c, P_lg], [dq[1]])
                S.op("scalar", lambda e, lgf=lgf: e.activation(out=dsc.ap[:, 0:1], in_=cst(1024, 1025), func=AF.Exp,
                                                               scale=lgf), [P_c, P_lg], [dsc])
                S.op("scalar", lambda e, lgb=lgb: e.activation(out=dsc.ap[:, 1:2], in_=cst(1025, 1026), func=AF.Exp,
                                                               scale=lgb), [P_c, P_lg], [dsc])
                S.op("scalar", lambda e, lgf=lgf: e.activation(out=dsc.ap[:, 2:3], in_=cst(1026, 1027), func=AF.Exp,
                                                               scale=lgf), [P_c, P_lg], [dsc])
                S.op("scalar", lambda e, lgb=lgb: e.activation(out=dsc.ap[:, 3:4], in_=cst(1026, 1027), func=AF.Exp,
                                                               scale=lgb), [P_c, P_lg], [dsc])
                for di in range(2):
                    for dc in range(2):
                        S.op("vector", lambda e, di=di, dc=dc: e.memset(Sst[di][dc].ap, 0.0), [], [Sst[di][dc]])
                for dc in range(2):
                    sb0 = Sbf[0][par[0]][dc]
                    S.op("vector", lambda e, sb0=sb0: e.memset(sb0.ap, 0.0), [], [sb0])
                for di in range(2):
                    S.op("vector", lambda e, di=di: e.tensor_tensor(
                        out=qsc[di].ap.rearrange("p c (t i) -> p (c t) i", i=128),
                        in0=qT.ap.rearrange("p c (t i) -> p (c t) i", i=128),
                        in1=dq[di].ap.unsqueeze(1).to_broadcast([128, 2 * NT_FULL, 128]), op=ALU.mult),
                        [qT, dq[di]], [qsc[di]])
                seq = [1, 0] + list(range(NT_ALL - 1, 1, -1))
                S.op("vector", lambda e: e.memset(snap.ap[:, seq[0], :], 0.0), [], [snap])
                plan_kv(seq[0], 1, (0, 6))
                plan_kv(seq[1], 1, (0, 6))
                for idx, t in enumerate(seq[:-1]):
                    nt = seq[idx + 1]
                    dst = None
                    if nt < NT_FULL:
                        dst = ([snap, snap], [snap.ap[:, nt, 0:512], snap.ap[:, nt, 512:1024]])
                    if idx + 2 < len(seq) - 1:
                        plan_kv(seq[idx + 2], 1, (0, 6))
                    state_update(t, 1, dst)
                pend = []
                plan_kv(0, 0)
                for t in range(NT_FULL):
                    pi = next_ps(4, 7)
                    for dc in range(2):
                        S.op("tensor", lambda e, pi=pi, dc=dc, t=t: e.matmul(
                            PS[pi].ap[:, 0:128], lhsT=kT.ap[:, dc, t * 128:(t + 1) * 128],
                            rhs=qT.ap[:, dc, t * 128:(t + 1) * 128], start=(dc == 0), stop=(dc == 1)),
                            [kT, qT], [PS[pi]])
                    pb = pT[cn["pT"] % 2]
                    cn["pT"] += 1
                    S.op("vector", lambda e, pi=pi, pb=pb: e.tensor_tensor(out=pb.ap, in0=PS[pi].ap[:, 0:128],
                                                                           in1=maskc.ap, op=ALU.mult),
                         [PS[pi], maskc], [pb])
                    qb = q_scaled(t, 0)
                    qbb = q_scaled(t, 1)
                    po = next_ps(4, 7)
                    sb = Sbf[0][par[0]]
                    S.op("tensor", lambda e, po=po, pb=pb, t=t: e.matmul(PS[po].ap, lhsT=pb.ap, rhs=vv.ap[:, t, :],
                                                                         start=True, stop=False), [pb, vv], [PS[po]])
                    for dc in range(2):
                        S.op("tensor", lambda e, po=po, dc=dc, qb=qb, sb=sb: e.matmul(
                            PS[po].ap, lhsT=qb.ap[:, dc, :], rhs=sb[dc].ap, start=False, stop=False),
                            [qb.buf, sb[dc]], [PS[po]])
                    for dc in range(2):
                        S.op("tensor", lambda e, po=po, dc=dc, qbb=qbb, t=t: e.matmul(
                            PS[po].ap, lhsT=qbb.ap[:, dc, :], rhs=snap.ap[:, t, dc * 512:(dc + 1) * 512], start=False,
                            stop=(dc == 1)), [qbb.buf, snap], [PS[po]])
                    if t != NT_FULL - 1:
                        np_ = 1 - par[0]
                        state_update(t, 0, (Sbf[0][np_], [Sbf[0][np_][0].ap, Sbf[0][np_][1].ap]))
                        par[0] = np_
                        if t + 1 != NT_FULL - 1:
                            plan_kv(t + 1, 0)
                    y = yb[cn["ow"] % 2]
                    sq = ssq[cn["ow"] % 2]
                    cn["ow"] += 1
                    o = PS[po]
                    S.op("scalar", lambda e, o=o, sq=sq: e.activation(out=junk.ap, in_=o.ap, func=AF.Square,
                                                                      scale=float(512 ** -0.5), accum_out=sq.ap),
                         [o], [junk, sq])
                    S.op("scalar", lambda e, sq=sq: e.activation(out=sq.ap, in_=sq.ap, func=AF.Sqrt,
                                                                 bias=P_eps.ap[:, 0:1]), [sq, P_eps], [sq])
                    while pend:
                        pend.pop(0)()

                    def tr(y=y, t=t, o=o, sq=sq):
                        S.op("vector", lambda e: e.reciprocal(out=sq.ap, in_=sq.ap), [sq], [sq])
                        S.op("vector", lambda e: e.scalar_tensor_tensor(
                            out=y.ap, in0=o.ap, scalar=sq.ap[:, 0:1], in1=sg.ap[:, t, :], op0=ALU.mult, op1=ALU.mult),
                            [o, sq, sg], [y])
                        pj = next_ps(7, 8)
                        for j in range(4):
                            S.op("tensor", lambda e, j=j, pj=pj, y=y: e.transpose(
                                ps_bf[pj][:, j * 128:(j + 1) * 128], y.ap[:, j * 128:(j + 1) * 128], P_identb.ap),
                                [y, P_identb], [PS[pj]])
                        S.op("scalar", lambda e, pj=pj, t=t: e.activation(
                            out=yT.ap[:, :, t * 128:(t + 1) * 128],
                            in_=ps_bf[pj][:, 0:512].rearrange("p (c t) -> p c t", c=4), func=AF.Copy), [PS[pj]], [yT])
                    pend.append(tr)
                while pend:
                    pend.pop(0)()
                S.dma("sync", yT_d[4 * h:4 * h + 4].rearrange("c p t -> p c t"), yT.ap, reads=[yT],
                      writes=[dbuf("yT_d")])
            S.barrier()

        def make_residual(layer, gcol0, x_src, x_dst, row_of, nxb=8):
            gb = [[A.alloc([128, 512], F32, f"gb{r}{i}") for i in range(2)] for r in range(2)]
            xb = [A.alloc([128, 512], F32, f"xb{i}") for i in range(nxb)]
            tmp = [A.alloc([128, 512], F32, f"rtmp{i}") for i in range(2)]
            st = {"gb": 0, "xb": 0, "tmp": 0, "cur": {}, "g": {}}

            def pre(grp, bi, c0, ncols):
                rows = sorted(set(row_of(t) for t in grp))
                for r in rows:
                    b = gb[r][st["gb"] % 2]
                    S.dma("sync", b.ap[:, 0:ncols],
                          mod_d[layer, r:r + 1, gcol0 + c0:gcol0 + c0 + ncols].partition_broadcast(128),
                          reads=[dbuf("mod_d")], writes=[b])
                    st["g"][r] = b
                st["gb"] += 1
                for gi, t in enumerate(grp):
                    b = xb[st["xb"] % nxb]
                    st["xb"] += 1
                    S.dma("sync", b.ap[:, 0:ncols], x_src(t)[:, c0:c0 + ncols], writes=[b])
                    st["cur"][gi] = b

            def evac(t, gi, bi, c0, ncols, pi):
                b = st["cur"][gi]
                g = st["g"][row_of(t)]
                tm = tmp[st["tmp"] % 2]
                st["tmp"] += 1
                S.op("vector", lambda e: e.tensor_tensor(out=tm.ap[:, 0:ncols], in0=PS[pi].ap[:, 0:ncols],
                                                         in1=g.ap[:, 0:ncols], op=ALU.mult), [PS[pi], g], [tm])
                S.op("vector", lambda e: e.tensor_tensor(out=b.ap[:, 0:ncols], in0=b.ap[:, 0:ncols],
                                                         in1=tm.ap[:, 0:ncols], op=ALU.add), [b, tm], [b])
                S.dma("sync", x_dst(t)[:, c0:c0 + ncols], b.ap[:, 0:ncols], reads=[b])

            return pre, evac

        def rows_full(dram):
            return lambda t: dram[t * 128:(t + 1) * 128, :]

        def rows_own(dram):
            return lambda t: dram[(t - 2) * 128:(t - 1) * 128, :]

        full_groups = [[0, 1, 2, 3], [4, 5, 6, 7], [8, 9, 10, 11], [12, 13, 14, 15], [16, 17, 18]]
        own_groups = [[2, 3, 4, 5], [6, 7, 8, 9], [10, 11, 12, 13], [14, 15, 16, 17]]
        blocks4 = lambda grp: [(i * 512, 512) for i in range(4)]

        def phase_ret_out():
            A.off = persist_off
            aTgs = [A.alloc([128, 32, 512], BF16, f"yTg{i}") for i in range(2)]
            wbufs = [A.alloc([128, 16, 512], BF16, f"wo{i}") for i in range(3)]
            pre, evac = make_residual(0, 2 * D, rows_full(xin), rows_full(x1_d), lambda t: 1 if t < 2 else 0)

            def prep(grp, aTg):
                r0 = grp[0] * 128
                n = len(grp) * 128
                S.dma("sync", aTg.ap[:, :, 0:n], yT_d[:, :, r0:r0 + n].rearrange("c p t -> p c t"),
                      reads=[dbuf("yT_d")], writes=[aTg])

            gemm_tok(full_groups, prep, WS["ret_w_out"], ret_w_out, 32, blocks4, evac, wbufs, aTgs, kparts=2,
                     pre=pre)
            S.barrier()

        def phase_mlp(layer, groups, x_src, x_dst, row_of):
            A.off = persist_off
            h1T = A.alloc([128, 64, 512], BF16, "h1T")
            wk = alloc_norm_work()
            xts = [A.alloc([128, D], F32, f"mxt{i}") for i in range(2)]
            a16 = A.alloc([128, 16, 512], BF16, "a16")
            w1b = [A.alloc([128, 16, 256], BF16, f"w1b{i}") for i in range(2)]
            w2b = [A.alloc([128, 16, 512], BF16, f"w2b{i}") for i in range(3)]
            rl = [A.alloc([128, 512], F32, f"rl{i}") for i in range(2)]
            pre, evac = make_residual(layer, 5 * D, x_src, x_dst, row_of, nxb=6)
            cn = {"x": 0, "w1": 0, "rl": 0}
            s1 = WStream(WS[f"mlp_w1_{layer}"], mlp_w1[layer], w1b, [(0, fb) for g in groups for fb in range(32)])

            def prep_norm(grp, h1):
                for gi, t in enumerate(grp):
                    xt = xts[cn["x"] % 2]
                    cn["x"] += 1
                    S.dma("sync", xt.ap, x_src(t), writes=[xt])
                    norm_mod_T(xt, row_of(t), layer * 2 + 1, a16, gi, wk)

            def prep(grp, h1):
                n = len(grp) * 128
                for fb in range(32):
                    w = s1.get(cn["w1"])
                    cn["w1"] += 1
                    for sub in range(2):
                        pi = next_ps(0, 4)
                        for kc in range(16):
                            S.op("tensor", lambda e, pi=pi, kc=kc, w=w, sub=sub: e.matmul(
                                PS[pi].ap[:, 0:n], lhsT=w.ap[:, kc, sub * 128:(sub + 1) * 128],
                                rhs=a16.ap[:, kc, 0:n], start=(kc == 0), stop=(kc == 15)), [w, a16], [PS[pi]])
                        r = rl[cn["rl"] % 2]
                        cn["rl"] += 1
                        S.op("scalar", lambda e, pi=pi, r=r: e.activation(out=r.ap[:, 0:n], in_=PS[pi].ap[:, 0:n],
                                                                          func=AF.Relu), [PS[pi]], [r])
                        S.op("vector", lambda e, r=r, fb=fb, sub=sub: e.tensor_tensor(
                            out=h1.ap[:, fb * 2 + sub, 0:n], in0=r.ap[:, 0:n], in1=r.ap[:, 0:n], op=ALU.mult),
                            [r], [h1])

            gemm_tok(groups, prep, WS[f"mlp_w2_{layer}"], mlp_w2[layer], 64, blocks4, evac, w2b, [h1T], kparts=4,
                     pre=pre, prep_early=prep_norm)
            S.barrier()

        def phase_att_proj():
            A.off = persist_off
            wk = alloc_norm_work()
            xts = [A.alloc([128, D], F32, f"xt{i}") for i in range(2)]
            aTgs = [A.alloc([128, 16, 512], BF16, f"aTg{i}") for i in range(2)]
            wbufs = [A.alloc([128, 16, 512], BF16, f"w{i}") for i in range(3)]
            tabs = [A.alloc([128, 1024], F32, f"tab{i}") for i in range(4)]
            xs_b = [A.alloc([128, 512], F32, f"xs{i}") for i in range(2)]
            rw = {"t1": A.alloc([128, 512], F32, "t1"), "t2": A.alloc([128, 512], F32, "t2")}
            ob = [A.alloc([128, 512], BF16, f"ob{i}") for i in range(6)]
            qTg = A.alloc([128, 16, 512], BF16, "qTg")
            kTg = A.alloc([128, 4, 512], BF16, "kTg")
            cnt = {"x": 0, "xs": 0, "ob": 0}
            tabmap = {}

            def prep(grp, aTg):
                for gi, t in enumerate(grp):
                    xt = xts[cnt["x"] % 2]
                    cnt["x"] += 1
                    S.dma("sync", xt.ap, x2_d[t * 128:(t + 1) * 128, :], writes=[xt])
                    norm_mod_T(xt, 1 if t < 2 else 0, 2, aTg, gi, wk)
                    tb = tabs[gi]
                    S.dma("sync", tb.ap, rope_a[t * 128:(t + 1) * 128, :], writes=[tb])
                    tabmap[t] = tb

            def blocks(grp):
                return [(i * 512, 512) for i in range(6)]

            def evac(t, gi, bi, c0, ncols, pi):
                r0 = t * 128
                o = ob[cnt["ob"] % 6]
                cnt["ob"] += 1
                if c0 < 2560:
                    isq = c0 < 2048
                    xs = xs_b[cnt["xs"] % 2]
                    cnt["xs"] += 1
                    S.op("scalar", lambda e: e.activation(out=xs.ap, in_=PS[pi].ap, func=AF.Copy), [PS[pi]], [xs])
                    rope_block(xs, tabmap[t], gi, (o, o.ap), 32, rw)
                    tg = qTg if isq else kTg
                    cb = (c0 // 128) if isq else 0

                    def tr(o=o, tg=tg, cb=cb, gi=gi):
                        pj = next_ps(4, 8)
                        for j in range(4):
                            S.op("tensor", lambda e, j=j, pj=pj, o=o: e.transpose(
                                ps_bf[pj][:, j * 128:(j + 1) * 128], o.ap[:, j * 128:(j + 1) * 128], P_identb.ap),
                                [o, P_identb], [PS[pj]])
                        S.op("scalar", lambda e, pj=pj, tg=tg, cb=cb, gi=gi: e.activation(
                            out=tg.ap[:, cb:cb + 4, gi * 128:(gi + 1) * 128],
                            in_=ps_bf[pj][:, 0:512].rearrange("p (c t) -> p c t", c=4), func=AF.Copy), [PS[pj]], [tg])
                    GCTX["defer"](tr)
                else:
                    S.op("scalar", lambda e: e.activation(out=o.ap, in_=PS[pi].ap, func=AF.Copy), [PS[pi]], [o])
                    S.dma("sync", av_d[r0:r0 + 128, :], o.ap, reads=[o], writes=[dbuf("av_d")])

            def post(grp):
                r0 = grp[0] * 128
                n = len(grp) * 128
                S.dma("sync", aq_d[:, :, r0:r0 + n].rearrange("c p t -> p c t"), qTg.ap[:, :, 0:n], reads=[qTg],
                      writes=[dbuf("aq_d")])
                S.dma("sync", akT_d[:, :, r0:r0 + n].rearrange("c p t -> p c t"), kTg.ap[:, :, 0:n], reads=[kTg],
                      writes=[dbuf("akT_d")])

            gemm_tok(full_groups, prep, WS["attn_w_in"], attn_w_in, 16, blocks, evac, wbufs, aTgs, post)
            S.barrier()

        def phase_att():
            A.off = persist_off
            SCALE = float(128 ** -0.5)
            qT = A.alloc([128, 4, R_FULL], BF16, "aqT")
            kT = A.alloc([128, R_FULL], BF16, "akT")
            vv = A.alloc([128, NT_FULL, 128], BF16, "avv")
            oTh = A.alloc([128, 4, 2048], BF16, "oTh")
            am = A.alloc([128, 768], F32, "am")
            sm = [A.alloc([128, 640], F32, f"sm{i}") for i in range(4)]
            pn = [A.alloc([128, 640], BF16, f"pn{i}") for i in range(4)]
            pTa = [A.alloc([128, 5, 512], BF16, f"pTa{i}") for i in range(2)]
            sc = [A.alloc([128, 8], F32, f"asc{i}") for i in range(4)]
            cn = {"i": 0, "pa": 0}
            stage = [A.alloc([128, 16, 512], BF16, f"stg{i}") for i in range(3)]
            wconvert(WS["attn_w_out"], wview(attn_w_out), stage)
            wconvert(WS["mlp_w1_1"], wview(mlp_w1[1]), stage)
            wconvert(WS["mlp_w2_1"], wview(mlp_w2[1]), stage)
            S.dma("sync", am.ap, amask, writes=[am])
            v_v = av_d.rearrange("(t p) c -> p t c", p=128)
            for kvh in range(4):
                S.dma("sync", qT.ap, aq_d[kvh * 4:kvh * 4 + 4].rearrange("c p t -> p c t"), reads=[dbuf("aq_d")],
                      writes=[qT])
                S.dma("sync", kT.ap, akT_d[kvh], reads=[dbuf("akT_d")], writes=[kT])
                S.dma("sync", vv.ap, v_v[:, :, kvh * 128:(kvh + 1) * 128], reads=[dbuf("av_d")], writes=[vv],
                      slow=True)
                stA, stB, stC = [], [], []
                for n in range(16):
                    t = n + 2
                    r0 = t * 128
                    pa = pTa[cn["pa"] % 2]
                    cn["pa"] += 1
                    for g in range(4):
                        hq = kvh * 4 + g
                        i2 = cn["i"] % 4
                        cn["i"] += 1
                        smb, pnb, scb = sm[i2], pn[i2], sc[i2]
                        m0 = 384 if n == 0 else 0

                        def fa(g=g, r0=r0, smb=smb, scb=scb, hq=hq, m0=m0):
                            p1 = next_ps(0, 4)
                            p2 = next_ps(0, 4)
                            S.op("tensor", lambda e: e.matmul(
                                PS[p1].ap[:, 0:384], lhsT=qT.ap[:, g, r0:r0 + 128], rhs=kT.ap[:, r0 - 128:r0 + 256],
                                start=True, stop=True), [qT, kT], [PS[p1]])
                            S.op("tensor", lambda e: e.matmul(
                                PS[p2].ap[:, 0:256], lhsT=qT.ap[:, g, r0:r0 + 128], rhs=kT.ap[:, 0:256],
                                start=True, stop=True), [qT, kT], [PS[p2]])
                            S.op("vector", lambda e: e.tensor_tensor(
                                out=smb.ap[:, 0:384], in0=PS[p1].ap[:, 0:384], in1=am.ap[:, m0:m0 + 384], op=ALU.add),
                                [PS[p1], am], [smb])
                            S.op("scalar", lambda e: e.activation(out=smb.ap[:, 384:640], in_=PS[p2].ap[:, 0:256],
                                                                  func=AF.Copy), [PS[p2]], [smb])
                            S.op("vector", lambda e: e.reduce_max(out=scb.ap[:, 0:1], in_=smb.ap, axis=AX.X),
                                 [smb], [scb])
                            S.op("vector", lambda e: e.tensor_scalar(
                                out=scb.ap[:, 1:2], in0=scb.ap[:, 0:1], scalar1=SCALE, scalar2=P_sink.ap[:, hq:hq + 1],
                                op0=ALU.mult, op1=ALU.max), [scb, P_sink], [scb])
                            S.op("vector", lambda e: e.tensor_scalar(
                                out=scb.ap[:, 2:3], in0=scb.ap[:, 1:2], scalar1=-1.0, scalar2=None, op0=ALU.mult),
                                [scb], [scb])
                            S.op("scalar", lambda e: e.activation(
                                out=smb.ap, in_=smb.ap, func=AF.Exp, bias=scb.ap[:, 2:3], scale=SCALE,
                                accum_out=scb.ap[:, 3:4]), [smb, scb], [smb, scb])
                            S.op("scalar", lambda e: e.activation(
                                out=scb.ap[:, 4:5], in_=P_sink.ap[:, hq:hq + 1], func=AF.Exp, bias=scb.ap[:, 2:3]),
                                [scb, P_sink], [scb])

                        def fb(smb=smb, pnb=pnb, scb=scb):
                            S.op("vector", lambda e: e.tensor_tensor(out=scb.ap[:, 5:6], in0=scb.ap[:, 3:4],
                                                                     in1=scb.ap[:, 4:5], op=ALU.add), [scb], [scb])
                            S.op("vector", lambda e: e.reciprocal(out=scb.ap[:, 5:6], in_=scb.ap[:, 5:6]),
                                 [scb], [scb])
                            S.op("vector", lambda e: e.tensor_scalar(
                                out=pnb.ap, in0=smb.ap, scalar1=scb.ap[:, 5:6], scalar2=None, op0=ALU.mult),
                                [smb, scb], [pnb])

                        def fc(pnb=pnb, pa=pa, g=g, n=n, t=t):
                            pj = next_ps(4, 8)
                            for j in range(5):
                                S.op("tensor", lambda e, j=j: e.transpose(
                                    ps_bf[pj][:, j * 128:(j + 1) * 128], pnb.ap[:, j * 128:(j + 1) * 128],
                                    P_identb.ap), [pnb, P_identb], [PS[pj]])
                            S.op("scalar", lambda e: e.activation(
                                out=pa.ap[:, :, g * 128:(g + 1) * 128],
                                in_=ps_bf[pj][:, 0:640].rearrange("p (c t) -> p c t", c=5), func=AF.Copy),
                                [PS[pj]], [pa])
                            if g == 3:
                                po = next_ps(0, 4)
                                vt = [t - 1, t, t + 1, 0, 1]
                                for j in range(5):
                                    S.op("tensor", lambda e, j=j: e.matmul(
                                        PS[po].ap, lhsT=vv.ap[:, vt[j], :], rhs=pa.ap[:, j, :], start=(j == 0),
                                        stop=(j == 4)), [vv, pa], [PS[po]])
                                S.op("scalar", lambda e: e.activation(
                                    out=oTh.ap[:, :, n * 128:(n + 1) * 128],
                                    in_=PS[po].ap.rearrange("p (c t) -> p c t", c=4), func=AF.Copy), [PS[po]], [oTh])
                        stA.append(fa)
                        stB.append(fb)
                        stC.append(fc)
                NI = len(stA)
                for st_ in range(NI + 2):
                    if st_ < NI:
                        stA[st_]()
                    if 0 <= st_ - 1 < NI:
                        stB[st_ - 1]()
                    if 0 <= st_ - 2 < NI:
                        stC[st_ - 2]()
                S.dma("sync", oT_d[kvh * 4:kvh * 4 + 4].rearrange("c p t -> p c t"), oTh.ap, reads=[oTh],
                      writes=[dbuf("oT_d")])
            S.barrier()

        def phase_att_out():
            A.off = persist_off
            aTgs = [A.alloc([128, 16, 512], BF16, f"oTg{i}") for i in range(2)]
            wbufs = [A.alloc([128, 16, 512], BF16, f"wo{i}") for i in range(3)]
            pre, evac = make_residual(1, 2 * D, rows_full(x2_d), rows_own(x3_d), lambda t: 0)

            def prep(grp, aTg):
                c0 = (grp[0] - 2) * 128
                n = len(grp) * 128
                S.dma("sync", aTg.ap[:, :, 0:n], oT_d[:, :, c0:c0 + n].rearrange("c p t -> p c t"),
                      reads=[dbuf("oT_d")], writes=[aTg])

            gemm_tok(own_groups, prep, WS["attn_w_out"], attn_w_out, 16, blocks4, evac, wbufs, aTgs, pre=pre)
            S.barrier()

        def phase_final():
            A.off = persist_off
            fg = A.alloc([128, D], F32, "fg")
            xts = [A.alloc([128, D], F32, f"fx{i}") for i in range(3)]
            junk = A.alloc([128, D], BF16, "fjunk")
            ssq = [A.alloc([128, 1], F32, f"fssq{i}") for i in range(2)]
            S.dma("sync", fg.ap, final_g.partition_broadcast(128), writes=[fg])
            for n in range(16):
                xt = xts[n % 3]
                sq = ssq[n % 2]
                S.dma("sync", xt.ap, x4_d[n * 128:(n + 1) * 128, :], writes=[xt])
                S.op("scalar", lambda e, xt=xt, sq=sq: e.activation(out=junk.ap, in_=xt.ap, func=AF.Square,
                                                                    scale=float(D ** -0.5), accum_out=sq.ap),
                     [xt], [junk, sq])
                S.op("vector", lambda e, sq=sq: e.tensor_scalar(out=sq.ap, in0=sq.ap, scalar1=EPS, scalar2=None,
                                                                op0=ALU.add), [sq], [sq])
                S.op("scalar", lambda e, sq=sq: e.activation(out=sq.ap, in_=sq.ap, func=AF.Sqrt), [sq], [sq])
                S.op("vector", lambda e, sq=sq: e.reciprocal(out=sq.ap, in_=sq.ap), [sq], [sq])
                S.op("vector", lambda e, xt=xt, sq=sq: e.scalar_tensor_tensor(
                    out=xt.ap, in0=xt.ap, scalar=sq.ap[:, 0:1], in1=fg.ap, op0=ALU.mult, op1=ALU.mult),
                    [xt, sq, fg], [xt])
                S.dma("sync", out[n * 128:(n + 1) * 128, :], xt.ap, reads=[xt])
            S.barrier()

        def phase_bench(mode):
            A.off = persist_off
            aTg = A.alloc([128, 16, 512], BF16, "b_aTg")
            wb = [A.alloc([128, 16, 512], BF16, f"b_w{i}") for i in range(2)]
            ob = [A.alloc([128, 512], BF16, f"b_o{i}") for i in range(2)]
            S.op("vector", lambda e: e.memset(aTg.ap, 0.5), [], [aTg])
            for w in wb:
                S.op("vector", lambda e, w=w: e.memset(w.ap, 0.25), [], [w])
            for it in range(160):
                w = wb[it % 2]
                if mode >= 2:
                    S.dma("gpsimd", w.ap, ret_w_in[:, (it % 24) * 512:(it % 24 + 1) * 512].rearrange(
                        "(c p) n -> p c n", p=128), writes=[w])
                pi = next_ps(0, 4)
                for kc in range(16):
                    S.op("tensor", lambda e, pi=pi, kc=kc, w=w, it=it: e.matmul(
                        PS[pi].ap, lhsT=aTg.ap[:, kc, (it % 4) * 128:(it % 4 + 1) * 128], rhs=w.ap[:, kc, :],
                        start=(kc == 0) or mode == 0, stop=(kc == 15) or mode == 0), [aTg, w], [PS[pi]])
                o = ob[it % 2]
                S.op("scalar", lambda e, pi=pi, o=o: e.activation(out=o.ap, in_=PS[pi].ap, func=AF.Copy), [PS[pi]], [o])
            S.barrier()

        phases = {
            "ada": phase_ada, "ret_proj": phase_ret_proj, "ret": phase_ret, "ret_out": phase_ret_out,
            "mlp0": lambda: phase_mlp(0, full_groups, rows_full(x1_d), rows_full(x2_d), lambda t: 1 if t < 2 else 0),
            "att_proj": phase_att_proj, "att": phase_att, "att_out": phase_att_out,
            "mlp1": lambda: phase_mlp(1, own_groups, rows_own(x3_d), rows_own(x4_d), lambda t: 0),
            "final": phase_final,
        }
        order = ["ada", "ret_proj", "ret", "ret_out", "mlp0", "att_proj", "att", "att_out", "mlp1", "final"]
        stop_after = None
        for d in dbg:
            if d.startswith("stop:"):
                stop_after = d[5:]
        for d in dbg:
            if d.startswith("bench:"):
                order = []
                phase_bench(int(d[6:]))
        for ph in order:
            phases[ph]()
            if stop_after == ph:
                break

        S.barrier()
        for e in Sched.ENGS:
            S.op(e, lambda en: en.nop(), [], [])

        S.finalize(eng_sems, dma_sems)
        with nc.Block() as block:
            @block.sync
            def _(e):
                S.emit("sync", e)

            @block.gpsimd
            def _(e):
                S.emit("gpsimd", e)

            @block.tensor
            def _(e):
                S.emit("tensor", e)

            @block.vector
            def _(e):
                S.emit("vector", e)

            @block.scalar
            def _(e):
                S.emit("scalar", e)

    return nc


NEG_MASK = -30000.0


def _const_tables():
    c = np.zeros((128, 1032), np.float32)
    p = np.arange(128, dtype=np.float32)
    jj = p[:, None]
    ii = p[None, :]
    c[:, 0:128] = np.eye(128, dtype=np.float32)
    c[:, 128:256] = np.maximum(ii - jj, 0.0)
    c[:, 256:384] = (ii >= jj).astype(np.float32)
    c[:, 384:512] = np.maximum(jj - ii, 0.0)
    c[:, 512:640] = (jj >= ii).astype(np.float32)
    c[:, 640:768] = ii + 1.0
    c[:, 768:896] = 128.0 - ii
    c[:, 1024] = 127.0 - p
    c[:, 1025] = p
    c[:, 1026] = 128.0
    am = np.zeros((128, 768), np.float32)
    prev = np.where(jj.T >= ii.T, 0.0, NEG_MASK)
    i_ = p[:, None]
    j_ = p[None, :]
    prev = np.where(j_ >= i_, 0.0, NEG_MASK).astype(np.float32)
    nxt = np.where(j_ <= i_, 0.0, NEG_MASK).astype(np.float32)
    am[:, 0:128] = prev
    am[:, 256:384] = nxt
    am[:, 384:512] = NEG_MASK
    am[:, 640:768] = nxt
    return c, am


def _rope_tables(pos):
    f32 = np.float32
    L = pos.shape[0]
    inv_r = (f32(10000.0) ** (-np.arange(0, 256, 2, dtype=f32) / f32(256))).astype(f32)
    ang = pos.astype(f32)[:, None] * inv_r[None, :]
    cr, sr = np.cos(ang).astype(f32), np.sin(ang).astype(f32)
    rr = np.zeros((R_ALL, 1024), f32)
    rr[:NCTX, 0:512] = 1.0
    rr[NCTX:, 0:512] = np.tile(cr, (1, 4))
    rr[NCTX:, 512:1024] = np.tile(np.concatenate([-sr, sr], axis=1), (1, 2))
    inv_a = (f32(10000.0) ** (-np.arange(0, 64, 2, dtype=f32) / f32(64))).astype(f32)
    rows = (pos // 64).astype(f32)
    cols = (pos % 64).astype(f32)
    ar = rows[:, None] * inv_a[None, :]
    ac = cols[:, None] * inv_a[None, :]
    c128 = np.concatenate([np.cos(ar), np.cos(ar), np.cos(ac), np.cos(ac)], axis=1).astype(f32)
    s128 = np.concatenate([-np.sin(ar), np.sin(ar), -np.sin(ac), np.sin(ac)], axis=1).astype(f32)
    ra = np.zeros((R_FULL, 1024), f32)
    ra[:NCTX, 0:512] = 1.0
    n = R_FULL - NCTX
    ra[NCTX:, 0:512] = np.tile(c128[:n], (1, 4))
    ra[NCTX:, 512:1024] = np.tile(s128[:n], (1, 4))
    return rr, ra


def make_in_maps(inp, cores=range(8)):
    f32 = np.float32
    g = {k: np.asarray(v) for k, v in inp.items()}
    consts, amask = _const_tables()
    shared = {
        "ada_w": np.ascontiguousarray(g["ada_w"], f32), "ada_b": np.ascontiguousarray(g["ada_b"], f32),
        "norm_mix_g": np.ascontiguousarray(g["norm_mix_g"], f32),
        "norm_mlp_g": np.ascontiguousarray(g["norm_mlp_g"], f32),
        "mlp_w1": np.ascontiguousarray(g["mlp_w1"], f32), "mlp_w2": np.ascontiguousarray(g["mlp_w2"], f32),
        "ret_w_in": np.ascontiguousarray(g["ret_w_in"][0], f32),
        "ret_w_out": np.ascontiguousarray(g["ret_w_out"][0], f32),
        "attn_w_in": np.ascontiguousarray(g["attn_w_in"][0], f32),
        "attn_w_out": np.ascontiguousarray(g["attn_w_out"][0], f32),
        "attn_sink": np.ascontiguousarray(g["attn_sink"], f32).reshape(1, 16),
        "final_g": np.ascontiguousarray(g["final_norm_g"], f32).reshape(1, D),
        "consts": consts, "amask": amask,
    }
    tabs = {}
    for h in range(2):
        pos = np.arange(4096) if h == 0 else np.arange(4095, -1, -1)
        tabs[h] = _rope_tables(pos)
    maps = []
    for core in cores:
        b, h = core // 2, core % 2
        x = g["x"][b]
        cx = g["ctx"][b]
        if h == 1:
            x = x[::-1]
            cx = cx[::-1]
        xin = np.ascontiguousarray(np.concatenate([cx, x], axis=0), f32)
        cvec = np.ascontiguousarray(np.stack([g["c"][b], g["c_ctx"]], axis=0), f32)
        df, db = g["ret_decay_fwd"][0], g["ret_decay_bwd"][0]
        dec = np.concatenate([df, db] if h == 0 else [db, df]).astype(f32).reshape(1, 16)
        m = dict(shared)
        m.update({"xin": xin, "cvec": cvec, "dec": dec, "rope_r": tabs[h][0], "rope_a": tabs[h][1]})
        maps.append(m)
    return maps


def assemble(results, B=4):
    out = np.zeros((B, 4096, D), np.float32)
    for core, r in enumerate(results):
        b, h = core // 2, core % 2
        o = np.asarray(r["out"])
        if h == 0:
            out[b, :2048] = o
        else:
            out[b, 2048:] = o[::-1]
    return out


_NC_CACHE = {}


def kernel(**inputs):
    if "nc" not in _NC_CACHE:
        _NC_CACHE["nc"] = build_program()
    nc = _NC_CACHE["nc"]
    in_maps = make_in_maps(inputs)
    res = run_bass_kernel_spmd(nc, in_maps, core_ids=list(range(8)))
    return assemble(res.results)
```

```python
import numpy as np
from contextlib import ExitStack
import concourse.bass as bass
import concourse.mybir as mybir
from concourse.bass_utils import run_bass_kernel_spmd

F32 = mybir.dt.float32
BF16 = mybir.dt.bfloat16
AF = mybir.ActivationFunctionType
ALU = mybir.AluOpType
AX = mybir.AxisListType

D = 2048
KC = 16
DFF = 8192
NCTX = 256
R_ALL = 4352
R_FULL = 2432
NT_ALL = 34
NT_FULL = 19
EPS = 1e-6
NS_DMA = 8
ARENA_BYTES = 211968


class Ins:
    __slots__ = ("eng", "fn", "waits", "signal", "sem", "val", "prev_val", "is_dma")

    def __init__(self, eng, fn, is_dma):
        self.eng = eng
        self.fn = fn
        self.waits = []
        self.signal = is_dma
        self.sem = None
        self.val = 0
        self.prev_val = 0
        self.is_dma = is_dma


class Buf:
    __slots__ = ("ap", "w", "r", "name")

    def __init__(self, ap, name=""):
        self.ap = ap
        self.w = None
        self.r = []
        self.name = name


class Sched:
    ENGS = ("sync", "gpsimd", "tensor", "vector", "scalar")

    def __init__(self):
        self.streams = {e: [] for e in self.ENGS}
        self.bar = {e: [] for e in self.ENGS}

    def op(self, eng, fn, reads=(), writes=(), dma=False):
        ins = Ins(eng, fn, dma)
        deps = []
        for b in reads:
            if b.w is not None:
                deps.append(b.w)
        for b in writes:
            if b.w is not None:
                deps.append(b.w)
            deps.extend(b.r)
        if self.bar[eng]:
            deps.extend(self.bar[eng])
            self.bar[eng] = []
        seen = set()
        for d in deps:
            if id(d) in seen or d is ins:
                continue
            seen.add(id(d))
            if d.eng == eng and not d.is_dma and eng == "tensor":
                continue
            ins.waits.append(d)
            d.signal = True
        for b in reads:
            if not dma:
                b.r = [x for x in b.r if x.is_dma or x.eng != eng]
            b.r.append(ins)
        for b in writes:
            b.w = ins
            b.r = []
        self.streams[eng].append(ins)
        return ins

    def dma(self, q, out, in_, reads=(), writes=(), slow=False):
        if slow:
            return self.op(q, lambda e: e.dma_start(out=out, in_=in_, allow_slow_non_contiguous=True),
                           reads, writes, dma=True)
        return self.op(q, lambda e: e.dma_start(out=out, in_=in_), reads, writes, dma=True)

    def barrier(self):
        deps = []
        for e in self.ENGS:
            st = self.streams[e]
            nd = 0
            last_c = None
            for ins in reversed(st):
                if ins.is_dma:
                    if nd < NS_DMA:
                        deps.append(ins)
                        nd += 1
                elif last_c is None:
                    last_c = ins
                    deps.append(ins)
                if nd >= NS_DMA and last_c is not None:
                    break
        for e in self.ENGS:
            self.bar[e] = list(deps)

    def finalize(self, eng_sems, dma_sems):
        for e in self.ENGS:
            cnt = 0
            di = 0
            for ins in self.streams[e]:
                if ins.is_dma:
                    s = di % NS_DMA
                    k = di // NS_DMA
                    ins.sem = dma_sems[e][s]
                    ins.val = 16 * (k + 1)
                    ins.prev_val = 16 * k
                    di += 1
                elif ins.signal:
                    cnt += 1
                    ins.sem = eng_sems[e]
                    ins.val = cnt

    def emit(self, ename, eng):
        known = {}
        for ins in self.streams[ename]:
            waits = {}
            for d in ins.waits:
                k = id(d.sem)
                if k not in waits or waits[k][1] < d.val:
                    waits[k] = (d.sem, d.val)
            if ins.is_dma and ins.prev_val > 0:
                k = id(ins.sem)
                if k not in waits or waits[k][1] < ins.prev_val:
                    waits[k] = (ins.sem, ins.prev_val)
            for k, (sem, val) in waits.items():
                if known.get(k, 0) >= val:
                    continue
                eng.wait_ge(sem, val)
                known[k] = val
            bi = ins.fn(eng)
            if ins.is_dma:
                bi.then_inc(ins.sem, 16)
            elif ins.signal:
                bi.then_inc(ins.sem, 1)


class Arena:
    def __init__(self, ap):
        self.base = ap
        self.off = 0

    def reset(self):
        self.off = 0

    def alloc(self, shape, dt, name=""):
        esz = 4 if dt == F32 else 2
        n = int(np.prod(shape[1:]))
        nbytes = (n * esz + 31) // 32 * 32
        assert self.off + nbytes <= ARENA_BYTES, (name, self.off, nbytes)
        a = self.base[:, self.off // 2: self.off // 2 + n * esz // 2]
        self.off += nbytes
        if dt != BF16:
            a = a.bitcast(dt)
        if len(shape) == 3:
            a = a.rearrange("p (a b) -> p a b", a=shape[1])
        elif len(shape) == 4:
            a = a.rearrange("p (a b c) -> p a b c", a=shape[1], b=shape[2])
        if shape[0] != 128:
            a = a[0:shape[0]]
        return Buf(a, name)


def build_program(debug=None):
    nc = bass.Bass("TRN2", target_bir_lowering=False)
    dbg = set(debug or [])

    def din(name, shape, dt=F32):
        return nc.dram_tensor(name, list(shape), dt, kind="ExternalInput").ap()

    def dscr(name, shape, dt):
        kind = "ExternalOutput" if name in dbg else "Internal"
        return nc.dram_tensor(name, list(shape), dt, kind=kind).ap()

    xin = din("xin", [R_ALL, D])
    cvec = din("cvec", [2, D])
    ada_w = din("ada_w", [2, D, 6 * D])
    ada_b = din("ada_b", [2, 6 * D])
    norm_mix_g = din("norm_mix_g", [2, D])
    norm_mlp_g = din("norm_mlp_g", [2, D])
    mlp_w1 = din("mlp_w1", [2, D, DFF])
    mlp_w2 = din("mlp_w2", [2, DFF, D])
    ret_w_in = din("ret_w_in", [D, 12288])
    ret_w_out = din("ret_w_out", [4096, D])
    dec = din("dec", [1, 16])
    attn_w_in = din("attn_w_in", [D, 3072])
    attn_w_out = din("attn_w_out", [D, D])
    attn_sink = din("attn_sink", [1, 16])
    final_g = din("final_g", [1, D])
    consts = din("consts", [128, 1024 + 8])
    rope_r = din("rope_r", [R_ALL, 1024])
    rope_a = din("rope_a", [R_FULL, 1024])
    amask = din("amask", [128, 768])
    out = nc.dram_tensor("out", [2048, D], F32, kind="ExternalOutput").ap()

    mod_d = dscr("mod_d", [2, 2, 6 * D], F32)
    qT_d = dscr("qT_d", [16, 128, R_FULL], BF16)
    kT_d = dscr("kT_d", [16, 128, R_FULL], BF16)
    k_d = dscr("k_d", [R_ALL, 2048], BF16)
    v_d = dscr("v_d", [R_ALL, 4096], BF16)
    sg_d = dscr("sg_d", [R_FULL, 4096], BF16)
    yT_d = dscr("yT_d", [32, 128, R_FULL], BF16)
    x1_d = dscr("x1_d", [R_FULL, D], F32)
    x2_d = dscr("x2_d", [R_FULL, D], F32)
    aq_d = dscr("aq_d", [16, 128, R_FULL], BF16)
    akT_d = dscr("akT_d", [4, 128, R_FULL], BF16)
    av_d = dscr("av_d", [R_FULL, 512], BF16)
    oT_d = dscr("oT_d", [16, 128, 2048], BF16)
    x3_d = dscr("x3_d", [2048, D], F32)
    x4_d = dscr("x4_d", [2048, D], F32)

    S = Sched()

    with ExitStack() as es:
        arena_t = es.enter_context(nc.sbuf_tensor("arena", [128, ARENA_BYTES // 2], BF16))
        psA = es.enter_context(nc.psum_tensor("psA", [128, 8, 512], F32))
        eng_sems = {e: es.enter_context(nc.semaphore(f"sem_{e}")) for e in Sched.ENGS}
        dma_sems = {e: [es.enter_context(nc.semaphore(f"dsem_{e}{i}")) for i in range(NS_DMA)]
                    for e in ("sync", "gpsimd")}
        A = Arena(arena_t)
        PS = [Buf(psA[:, i, :], f"ps{i}") for i in range(8)]
        ps_bf = [psA[:, i, :].bitcast(BF16) for i in range(8)]

        DRAMB = {}

        def dbuf(name):
            if name not in DRAMB:
                DRAMB[name] = Buf(None, name)
            return DRAMB[name]

        rr = {"ps": 0}

        def next_ps(lo=0, hi=4):
            i = lo + rr.setdefault((lo, hi), 0) % (hi - lo)
            rr[(lo, hi)] += 1
            return i

        P_ident = A.alloc([128, 128], F32, "ident")
        P_identb = A.alloc([128, 128], BF16, "identb")
        P_c = A.alloc([128, 1024 + 8], F32, "consts")
        P_modF = A.alloc([128, 2, 2, 96], F32, "modF")
        P_gF = A.alloc([128, 2, 2, 16], F32, "gF")
        P_AB = A.alloc([128, 4, 2, 32], F32, "AB")
        P_dec = A.alloc([128, 16], F32, "dec")
        P_lg = A.alloc([128, 16], F32, "lg")
        P_sink = A.alloc([128, 16], F32, "sink")
        P_eps = A.alloc([128, 1], F32, "eps")
        persist_off = A.off

        S.dma("sync", P_c.ap, consts, writes=[P_c])
        S.dma("sync", P_ident.ap, consts[:, 0:128], writes=[P_ident])
        S.dma("sync", P_dec.ap, dec.partition_broadcast(128), writes=[P_dec])
        S.dma("sync", P_sink.ap, attn_sink.partition_broadcast(128), writes=[P_sink])
        S.op("vector", lambda e: e.tensor_copy(out=P_identb.ap, in_=P_ident.ap), [P_ident], [P_identb])
        S.op("vector", lambda e: e.memset(P_eps.ap, EPS), [], [P_eps])
        for l in range(2):
            S.dma("sync", P_gF.ap[:, l, 0, :], norm_mix_g[l].rearrange("(c p) -> p c", p=128), writes=[P_gF], slow=True)
            S.dma("sync", P_gF.ap[:, l, 1, :], norm_mlp_g[l].rearrange("(c p) -> p c", p=128), writes=[P_gF], slow=True)

        def cst(lo, hi):
            return P_c.ap[:, lo:hi]

        def phase_ada():
            A.off = persist_off
            cT = A.alloc([128, 16, 2], F32, "cT")
            cTb = A.alloc([128, 16, 2], BF16, "cTb")
            brow = A.alloc([2, 6 * D], F32, "brow")
            mrow = A.alloc([2, 6 * D], F32, "mrow")
            wb = [A.alloc([128, 16, 512], BF16, f"adaw{i}") for i in range(4)]
            for r in range(2):
                S.dma("sync", cT.ap[:, :, r], cvec[r].rearrange("(c p) -> p c", p=128), writes=[cT], slow=True)
            S.op("scalar", lambda e: e.activation(out=cTb.ap, in_=cT.ap, func=AF.Silu), [cT], [cTb])
            asteps = [(l, nb) for l in range(2) for nb in range(24)]
            st = {"p": 0}

            def aload(upto):
                while st["p"] <= min(upto, len(asteps) - 1):
                    l2, nb2 = asteps[st["p"]]
                    w2 = wb[st["p"] % 4]
                    S.dma("gpsimd", w2.ap, ada_w[l2, :, nb2 * 512:(nb2 + 1) * 512].rearrange("(c p) n -> p c n", p=128),
                          writes=[w2])
                    st["p"] += 1

            for l in range(2):
                S.dma("sync", brow.ap, ada_b[l:l + 1, :].partition_broadcast(2), writes=[brow])
                for nb in range(24):
                    i = l * 24 + nb
                    aload(i + 3)
                    w = wb[i % 4]
                    pi = next_ps(0, 4)
                    for kc in range(16):
                        S.op("tensor", lambda e, pi=pi, kc=kc, w=w: e.matmul(
                            PS[pi].ap[0:2, :], lhsT=cTb.ap[:, kc, :], rhs=w.ap[:, kc, :],
                            start=(kc == 0), stop=(kc == 15)), [cTb, w], [PS[pi]])
                    S.op("vector", lambda e, pi=pi, nb=nb: e.tensor_tensor(
                        out=mrow.ap[:, nb * 512:(nb + 1) * 512], in0=PS[pi].ap[0:2, :],
                        in1=brow.ap[:, nb * 512:(nb + 1) * 512], op=ALU.add), [PS[pi], brow], [mrow])
                S.dma("sync", mod_d[l], mrow.ap, reads=[mrow], writes=[dbuf("mod_d")])
            for l in range(2):
                for r in range(2):
                    S.dma("sync", P_modF.ap[:, l, r, :], mod_d[l, r].rearrange("(c p) -> p c", p=128),
                          reads=[dbuf("mod_d")], writes=[P_modF], slow=True)
            for l in range(2):
                for sub in range(2):
                    sh0 = 0 if sub == 0 else 48
                    sc0 = sh0 + 16
                    for r in range(2):
                        S.op("vector", lambda e, l=l, sub=sub, r=r, sc0=sc0: e.scalar_tensor_tensor(
                            out=P_AB.ap[:, l * 2 + sub, r, 0:16], in0=P_modF.ap[:, l, r, sc0:sc0 + 16], scalar=1.0,
                            in1=P_gF.ap[:, l, sub, :], op0=ALU.add, op1=ALU.mult), [P_modF, P_gF], [P_AB])
                        S.op("vector", lambda e, l=l, sub=sub, r=r, sh0=sh0: e.tensor_copy(
                            out=P_AB.ap[:, l * 2 + sub, r, 16:32], in_=P_modF.ap[:, l, r, sh0:sh0 + 16]),
                            [P_modF], [P_AB])
            S.op("scalar", lambda e: e.activation(out=P_lg.ap, in_=P_dec.ap, func=AF.Exp, scale=-1.0), [P_dec], [P_lg])
            S.op("scalar", lambda e: e.activation(out=P_lg.ap, in_=P_lg.ap, func=AF.Ln, bias=1.0), [P_lg], [P_lg])
            S.op("vector", lambda e: e.tensor_scalar(out=P_lg.ap, in0=P_lg.ap, scalar1=-1.0, scalar2=None,
                                                     op0=ALU.mult), [P_lg], [P_lg])
            S.barrier()

        def norm_mod_T(xt, row, lsub, aTg, gi, work):
            junk, ssq, xn = work["junk"], work["ssq"], work["xn"]
            S.op("scalar", lambda e: e.activation(out=junk.ap, in_=xt.ap, func=AF.Square,
                                                  scale=float(D ** -0.5), accum_out=ssq.ap), [xt], [xn, ssq])
            S.op("vector", lambda e: e.tensor_scalar(out=ssq.ap, in0=ssq.ap, scalar1=EPS, scalar2=None,
                                                     op0=ALU.add), [ssq], [ssq])
            S.op("scalar", lambda e: e.activation(out=ssq.ap, in_=ssq.ap, func=AF.Sqrt), [ssq], [ssq])
            S.op("vector", lambda e: e.reciprocal(out=ssq.ap, in_=ssq.ap), [ssq], [ssq])
            S.op("scalar", lambda e: e.activation(out=xn.ap, in_=xt.ap, func=AF.Copy, scale=ssq.ap[:, 0:1]),
                 [xt, ssq], [xn])
            for b4 in range(4):
                pi = next_ps(4, 8)
                for j in range(4):
                    kc = b4 * 4 + j
                    S.op("tensor", lambda e, pi=pi, j=j, kc=kc: e.transpose(
                        PS[pi].ap[:, j * 128:(j + 1) * 128], xn.ap[:, kc * 128:(kc + 1) * 128], P_ident.ap),
                        [xn, P_ident], [PS[pi]])
                for j in range(4):
                    kc = b4 * 4 + j
                    S.op("vector", lambda e, pi=pi, j=j, kc=kc: e.tensor_scalar(
                        out=aTg.ap[:, kc, gi * 128:(gi + 1) * 128], in0=PS[pi].ap[:, j * 128:(j + 1) * 128],
                        scalar1=P_AB.ap[:, lsub, row, kc:kc + 1], scalar2=P_AB.ap[:, lsub, row, 16 + kc:17 + kc],
                        op0=ALU.mult, op1=ALU.add), [PS[pi], P_AB], [aTg])

        def alloc_norm_work():
            xn = A.alloc([128, D], F32, "xn")
            return {"junk": xn, "ssq": A.alloc([128, 1], F32, "ssq"), "xn": xn}

        converted = set()
        GCTX = {}

        def wscratch(name, K, N, kp, ncols):
            nkq = K // 128 // kp
            nbi = N // ncols
            t = nc.dram_tensor("ws_" + name, [nkq, nbi, 128, kp * ncols], BF16, kind="Internal").ap()
            return {"ap": t, "kp": kp, "ncols": ncols, "name": name}

        def wload(ws, Wv, kq, bi, w):
            key = (ws["name"], kq, bi)
            kp, ncols = ws["kp"], ws["ncols"]
            sc = ws["ap"][kq, bi].rearrange("p (c n) -> p c n", c=kp)
            if key in converted:
                S.dma("sync", w.ap, sc, reads=[dbuf(key)], writes=[w])
            else:
                S.dma("gpsimd", w.ap, Wv[:, kq * kp:(kq + 1) * kp, bi * ncols:(bi + 1) * ncols], writes=[w])
                S.dma("gpsimd", sc, w.ap, reads=[w], writes=[dbuf(key)])
                converted.add(key)

        def wconvert(ws, Wv, stage, nhalf=1):
            kp, ncols = ws["kp"], ws["ncols"]
            nkq, nbi = ws["ap"].shape[0], ws["ap"].shape[1]
            hk = kp // nhalf
            i = 0
            for kq in range(nkq):
                for bi in range(nbi):
                    key = (ws["name"], kq, bi)
                    if key in converted:
                        continue
                    sc = ws["ap"][kq, bi].rearrange("p (c n) -> p c n", c=kp)
                    for hf in range(nhalf):
                        st = stage[i % len(stage)]
                        i += 1
                        sv = st.ap.rearrange("p a b -> p (a b)")[:, 0:hk * ncols].rearrange("p (c n) -> p c n", c=hk)
                        S.dma("gpsimd", sv,
                              Wv[:, kq * kp + hf * hk:kq * kp + (hf + 1) * hk, bi * ncols:(bi + 1) * ncols],
                              writes=[st])
                        S.dma("gpsimd", sc[:, hf * hk:(hf + 1) * hk, :], sv, reads=[st],
                              writes=[dbuf(key)])
                    converted.add(key)

        class WStream:
            def __init__(self, ws, W, wbufs, steps):
                self.ws, self.wbufs, self.steps = ws, wbufs, steps
                self.Wv = W.rearrange("(c p) n -> p c n", p=128)
                self.planned = 0
                self.pf = len(wbufs) - 1

            def get(self, i):
                upto = min(i + self.pf, len(self.steps) - 1)
                while self.planned <= upto:
                    kq, bi = self.steps[self.planned]
                    wload(self.ws, self.Wv, kq, bi, self.wbufs[self.planned % len(self.wbufs)])
                    self.planned += 1
                return self.wbufs[i % len(self.wbufs)]

        EARLY = set()

        def gemm_tok(groups, prep, ws, W, kcn, blocks, evac, wbufs, aTgs, post=None, kparts=1, pre=None,
                     prep_early=None, early=True):
            kp = kcn // kparts
            steps = []
            for g, grp in enumerate(groups):
                for bi, (c0, ncols) in enumerate(blocks(grp)):
                    for kq in range(kparts):
                        steps.append((g, bi, c0, ncols, kq))
            stream = WStream(ws, W, wbufs, [(kq, c0 // ncols) for (g, bi, c0, ncols, kq) in steps])
            pend = []
            GCTX["defer"] = lambda fn: pend.append([0, fn])
            prepped = set()
            nblk = {}
            for (g, bi, c0, ncols, kq) in steps:
                nblk[g] = max(nblk.get(g, 0), bi + 1)
            for i, (g, bi, c0, ncols, kq) in enumerate(steps):
                grp = groups[g]
                aTg = aTgs[g % len(aTgs)]
                w = stream.get(i)
                if bi == 0 and kq == 0 and g not in prepped:
                    if prep_early is not None:
                        prep_early(grp, aTg)
                    prep(grp, aTg)
                    prepped.add(g)
                if early and bi == nblk[g] - 1 and kq == 0 and g + 1 < len(groups) and (g + 1) not in prepped:
                    if prep_early is not None:
                        prep_early(groups[g + 1], aTgs[(g + 1) % len(aTgs)])
                        prepped.add(g + 1)
                        EARLY.add(g + 1)
                    elif len(aTgs) > 1:
                        prep(groups[g + 1], aTgs[(g + 1) % len(aTgs)])
                        prepped.add(g + 1)
                if bi == 0 and kq == 0 and g in EARLY:
                    EARLY.discard(g)
                    prep(grp, aTg)
                if kq == 0 and pre is not None:
                    pre(grp, bi, c0, ncols)
                for gi, t in enumerate(grp):
                    if kparts == 1:
                        pi = next_ps(0, 4)
                    else:
                        pi = (bi % 2) * 4 + gi
                    for k2 in range(kp):
                        kc = kq * kp + k2
                        S.op("tensor", lambda e, pi=pi, kc=kc, k2=k2, w=w, aTg=aTg, gi=gi, ncols=ncols: e.matmul(
                            PS[pi].ap[:, 0:ncols], lhsT=aTg.ap[:, kc, gi * 128:(gi + 1) * 128],
                            rhs=w.ap[:, k2, 0:ncols], start=(kc == 0), stop=(kc == kcn - 1)),
                            [aTg, w], [PS[pi]])
                    for p in pend:
                        p[0] += 1
                    while pend and pend[0][0] >= 2:
                        pend.pop(0)[1]()
                    if kq == kparts - 1:
                        evac(t, gi, bi, c0, ncols, pi)
                last = (i + 1 == len(steps)) or steps[i + 1][0] != g
                if last:
                    while pend:
                        pend.pop(0)[1]()
                    if post is not None:
                        post(grp)

        WS = {
            "ret_w_in": wscratch("ret_w_in", D, 12288, 16, 512),
            "ret_w_out": wscratch("ret_w_out", 4096, D, 16, 512),
            "mlp_w1_0": wscratch("mlp_w1_0", D, DFF, 16, 256),
            "mlp_w2_0": wscratch("mlp_w2_0", DFF, D, 16, 512),
            "mlp_w1_1": wscratch("mlp_w1_1", D, DFF, 16, 256),
            "mlp_w2_1": wscratch("mlp_w2_1", DFF, D, 16, 512),
            "attn_w_in": wscratch("attn_w_in", D, 3072, 16, 512),
            "attn_w_out": wscratch("attn_w_out", D, D, 16, 512),
        }

        def wview(W):
            return W.rearrange("(c p) n -> p c n", p=128)

        def rope_block(xs, tab, gi_unused, outb, B, work):
            t1, t2 = work["t1"], work["t2"]
            n = 512 // (2 * B)
            xv = xs.ap.rearrange("p (n two b) -> p n two b", two=2, b=B)
            sv = tab.ap[:, 512:1024].rearrange("p (n two b) -> p n two b", two=2, b=B)
            t2v = t2.ap.rearrange("p (n two b) -> p n two b", two=2, b=B)
            S.op("vector", lambda e: e.tensor_tensor(out=t1.ap, in0=xs.ap, in1=tab.ap[:, 0:512], op=ALU.mult),
                 [xs, tab], [t1])
            S.op("vector", lambda e: e.tensor_tensor(out=t2v[:, :, 0, :], in0=xv[:, :, 1, :], in1=sv[:, :, 0, :],
                                                     op=ALU.mult), [xs, tab], [t2])
            S.op("vector", lambda e: e.tensor_tensor(out=t2v[:, :, 1, :], in0=xv[:, :, 0, :], in1=sv[:, :, 1, :],
                                                     op=ALU.mult), [xs, tab], [t2])
            S.op("vector", lambda e: e.tensor_tensor(out=outb[1], in0=t1.ap, in1=t2.ap, op=ALU.add),
                 [t1, t2], [outb[0]])

        def phase_ret_proj():
            A.off = persist_off
            wk = alloc_norm_work()
            xts = [A.alloc([128, D], F32, f"xt{i}") for i in range(2)]
            aTgs = [A.alloc([128, 16, 512], BF16, f"aTg{i}") for i in range(2)]
            wbufs = [A.alloc([128, 16, 512], BF16, f"w{i}") for i in range(3)]
            tabs = [A.alloc([128, 1024], F32, f"tab{i}") for i in range(4)]
            xs_b = [A.alloc([128, 512], F32, f"xs{i}") for i in range(2)]
            rw = {"t1": A.alloc([128, 512], F32, "t1"), "t2": A.alloc([128, 512], F32, "t2")}
            ob = [A.alloc([128, 512], BF16, f"ob{i}") for i in range(6)]
            qTg = A.alloc([128, 16, 512], BF16, "qTg")
            kTg = A.alloc([128, 16, 512], BF16, "kTg")
            cnt = {"x": 0, "xs": 0, "ob": 0}
            full_groups = [[0, 1, 2, 3], [4, 5, 6, 7], [8, 9, 10, 11], [12, 13, 14, 15], [16, 17, 18]]
            far_groups = [[19, 20, 21, 22], [23, 24, 25, 26], [27, 28, 29, 30], [31, 32, 33]]
            groups = full_groups + far_groups
            tabmap = {}

            def prep(grp, aTg):
                for gi, t in enumerate(grp):
                    xt = xts[cnt["x"] % 2]
                    cnt["x"] += 1
                    S.dma("sync", xt.ap, xin[t * 128:(t + 1) * 128, :], writes=[xt])
                    norm_mod_T(xt, 1 if t < 2 else 0, 0, aTg, gi, wk)
                    tb = tabs[gi]
                    S.dma("sync", tb.ap, rope_r[t * 128:(t + 1) * 128, :], writes=[tb])
                    tabmap[t] = tb

            def blocks(grp):
                if grp[0] >= NT_FULL:
                    return [(2048 + i * 512, 512) for i in range(4)] + [(4096 + i * 512, 512) for i in range(8)]
                return [(i * 512, 512) for i in range(24)]

            def evac(t, gi, bi, c0, ncols, pi):
                full = t < NT_FULL
                r0 = t * 128
                if c0 < 4096:
                    isq = c0 < 2048
                    xs = xs_b[cnt["xs"] % 2]
                    cnt["xs"] += 1
                    S.op("scalar", lambda e: e.activation(out=xs.ap, in_=PS[pi].ap, func=AF.Copy,
                                                          scale=1.0 if isq else 0.0625), [PS[pi]], [xs])
                    o = ob[cnt["ob"] % 6]
                    cnt["ob"] += 1
                    rope_block(xs, tabmap[t], gi, (o, o.ap), 128, rw)
                    cb = (c0 % 2048) // 128
                    if not isq:
                        S.dma("sync", k_d[r0:r0 + 128, c0 - 2048:c0 - 2048 + 512], o.ap, reads=[o],
                              writes=[dbuf("k_d")])
                    if full:
                        tg = qTg if isq else kTg

                        def tr(o=o, tg=tg, cb=cb, gi=gi):
                            pj = next_ps(4, 8)
                            for j in range(4):
                                S.op("tensor", lambda e, j=j, pj=pj, o=o: e.transpose(
                                    ps_bf[pj][:, j * 128:(j + 1) * 128], o.ap[:, j * 128:(j + 1) * 128], P_identb.ap),
                                    [o, P_identb], [PS[pj]])
                            S.op("scalar", lambda e, pj=pj, tg=tg, cb=cb, gi=gi: e.activation(
                                out=tg.ap[:, cb:cb + 4, gi * 128:(gi + 1) * 128],
                                in_=ps_bf[pj][:, 0:512].rearrange("p (c t) -> p c t", c=4), func=AF.Copy),
                                [PS[pj]], [tg])
                        GCTX["defer"](tr)
                elif c0 < 8192:
                    o = ob[cnt["ob"] % 6]
                    cnt["ob"] += 1
                    S.op("scalar", lambda e: e.activation(out=o.ap, in_=PS[pi].ap, func=AF.Copy), [PS[pi]], [o])
                    S.dma("sync", v_d[r0:r0 + 128, c0 - 4096:c0 - 4096 + 512], o.ap, reads=[o], writes=[dbuf("v_d")])
                else:
                    o = ob[cnt["ob"] % 6]
                    cnt["ob"] += 1
                    S.op("scalar", lambda e: e.activation(out=o.ap, in_=PS[pi].ap, func=AF.Silu), [PS[pi]], [o])
                    S.dma("sync", sg_d[r0:r0 + 128, c0 - 8192:c0 - 8192 + 512], o.ap, reads=[o],
                          writes=[dbuf("sg_d")])

            def post(grp):
                if grp[0] >= NT_FULL:
                    return
                r0 = grp[0] * 128
                n = len(grp) * 128
                S.dma("sync", qT_d[:, :, r0:r0 + n].rearrange("c p t -> p c t"), qTg.ap[:, :, 0:n], reads=[qTg],
                      writes=[dbuf("qT_d")])
                S.dma("sync", kT_d[:, :, r0:r0 + n].rearrange("c p t -> p c t"), kTg.ap[:, :, 0:n], reads=[kTg],
                      writes=[dbuf("kT_d")])

            gemm_tok(groups, prep, WS["ret_w_in"], ret_w_in, 16, blocks, evac, wbufs, aTgs, post)
            S.barrier()

        def phase_ret():
            A.off = persist_off
            qT = A.alloc([128, 2, R_FULL], BF16, "qT")
            kT = A.alloc([128, 2, R_FULL], BF16, "kT")
            kk = A.alloc([128, NT_ALL, 256], BF16, "kk")
            vv = A.alloc([128, NT_ALL, 512], BF16, "vv")
            sg = A.alloc([128, NT_FULL, 512], BF16, "sg")
            snap = A.alloc([128, NT_FULL, 1024], BF16, "snap")
            yT = A.alloc([128, 4, R_FULL], BF16, "yT")
            Sst = [[A.alloc([128, 512], F32, f"S{i}{dc}") for dc in range(2)] for i in range(2)]
            Sbf = [[[A.alloc([128, 512], BF16, f"Sbf{i}{p}{dc}") for dc in range(2)] for p in range(2)]
                   for i in range(2)]
            maskc = A.alloc([128, 128], F32, "maskc")
            mtmp = A.alloc([128, 128], F32, "mtmp")
            dq = [A.alloc([128, 128], F32, f"dq{i}") for i in range(2)]
            dsc = A.alloc([128, 4], F32, "dsc")
            qs = [A.alloc([128, 2, 128], BF16, f"qs{i}") for i in range(4)]
            ks = [A.alloc([128, 256], BF16, f"ks{i}") for i in range(5)]
            pT = [A.alloc([128, 128], BF16, f"pT{i}") for i in range(2)]
            qsc = [A.alloc([128, 2, R_FULL], BF16, f"qsc{i}") for i in range(2)]
            yb = [A.alloc([128, 512], BF16, f"yb{i}") for i in range(2)]
            junk = A.alloc([128, 512], BF16, "junkr")
            ssq = [A.alloc([128, 1], F32, f"ssqr{i}") for i in range(2)]
            cn = {"qs": 0, "ks": 0, "pT": 0, "ow": 0}
            k_v = k_d.rearrange("(t p) c -> p t c", p=128)
            v_v = v_d.rearrange("(t p) c -> p t c", p=128)
            sg_v = sg_d.rearrange("(t p) c -> p t c", p=128)
            stage = [A.alloc([128, 4, 512], BF16, f"stg{i}") for i in range(2)]
            wconvert(WS["ret_w_out"], wview(ret_w_out), stage, nhalf=4)
            wconvert(WS["mlp_w1_0"], wview(mlp_w1[0]), stage, nhalf=2)
            wconvert(WS["mlp_w2_0"], wview(mlp_w2[0]), stage, nhalf=4)

            par = {0: 0, 1: 0}
            kvp = {}

            def plan_kv(t, di, banks=(0, 4)):
                kb = ks[cn["ks"] % 5]
                cn["ks"] += 1
                S.op("vector", lambda e: e.tensor_scalar(out=kb.ap, in0=kk.ap[:, t, :], scalar1=dsc.ap[:, di:di + 1],
                                                         scalar2=None, op0=ALU.mult), [kk, dsc], [kb])
                pis = []
                for dc in range(2):
                    pi = next_ps(*banks)
                    pis.append(pi)
                    S.op("tensor", lambda e, pi=pi, dc=dc: e.matmul(
                        PS[pi].ap, lhsT=kb.ap[:, dc * 128:(dc + 1) * 128], rhs=vv.ap[:, t, :], start=True, stop=True),
                        [kb, vv], [PS[pi]])
                kvp[(t, di)] = pis

            def state_update(t, di, dst):
                pis = kvp.pop((t, di))
                for dc in range(2):
                    pi = pis[dc]
                    S.op("vector", lambda e, pi=pi, dc=dc: e.scalar_tensor_tensor(
                        out=Sst[di][dc].ap, in0=Sst[di][dc].ap, scalar=dsc.ap[:, 2 + di:3 + di],
                        in1=PS[pi].ap, op0=ALU.mult, op1=ALU.add), [Sst[di][dc], dsc, PS[pi]], [Sst[di][dc]])
                    if dst is not None:
                        S.op("scalar", lambda e, dc=dc: e.activation(out=dst[1][dc], in_=Sst[di][dc].ap,
                                                                     func=AF.Copy), [Sst[di][dc]], [dst[0][dc]])

            class _QV:
                def __init__(self, buf, ap):
                    self.buf, self.ap = buf, ap

            def q_scaled(t, di):
                return _QV(qsc[di], qsc[di].ap[:, :, t * 128:(t + 1) * 128])

            for h in range(8):
                S.dma("sync", qT.ap, qT_d[2 * h:2 * h + 2].rearrange("c p t -> p c t"), reads=[dbuf("qT_d")],
                      writes=[qT])
                S.dma("sync", kT.ap, kT_d[2 * h:2 * h + 2].rearrange("c p t -> p c t"), reads=[dbuf("kT_d")],
                      writes=[kT])
                for t0 in range(0, NT_ALL, 9):
                    t1 = min(NT_ALL, t0 + 9)
                    S.dma("sync", kk.ap[:, t0:t1, :], k_v[:, t0:t1, h * 256:(h + 1) * 256], reads=[dbuf("k_d")],
                          writes=[kk])
                    S.dma("sync", vv.ap[:, t0:t1, :], v_v[:, t0:t1, h * 512:(h + 1) * 512], reads=[dbuf("v_d")],
                          writes=[vv])
                for t0 in range(0, NT_FULL, 10):
                    t1 = min(NT_FULL, t0 + 10)
                    S.dma("sync", sg.ap[:, t0:t1, :], sg_v[:, t0:t1, h * 512:(h + 1) * 512], reads=[dbuf("sg_d")],
                          writes=[sg])
                lgf = P_lg.ap[:, h:h + 1]
                lgb = P_lg.ap[:, 8 + h:9 + h]
                S.op("scalar", lambda e, lgf=lgf: e.activation(out=maskc.ap, in_=cst(128, 256), func=AF.Exp, scale=lgf),
                     [P_c, P_lg], [maskc])
                S.op("vector", lambda e: e.tensor_tensor(out=maskc.ap, in0=maskc.ap, in1=cst(256, 384), op=ALU.mult),
                     [maskc, P_c], [maskc])
                S.op("scalar", lambda e, lgb=lgb: e.activation(out=mtmp.ap, in_=cst(384, 512), func=AF.Exp, scale=lgb),
                     [P_c, P_lg], [mtmp])
                S.op("vector", lambda e: e.tensor_tensor(out=mtmp.ap, in0=mtmp.ap, in1=cst(512, 640), op=ALU.mult),
                     [mtmp, P_c], [mtmp])
                S.op("vector", lambda e: e.tensor_tensor(out=maskc.ap, in0=maskc.ap, in1=mtmp.ap, op=ALU.add),
                     [maskc, mtmp], [maskc])
                S.op("scalar", lambda e, lgf=lgf: e.activation(out=dq[0].ap, in_=cst(640, 768), func=AF.Exp, scale=lgf),
                     [P_c, P_lg], [dq[0]])
                S.op("scalar", lambda e, lgb=lgb: e.activation(out=dq[1].ap, in_=cst(768, 896), func=AF.Exp, scale=lgb),
                     [P_c, P_lg], [dq[1]])
                S.op("scalar", lambda e, lgf=lgf: e.activation(out=dsc.ap[:, 0:1], in_=cst(1024, 1025), func=AF.Exp,
                                                               scale=lgf), [P_c, P_lg], [dsc])
                S.op("scalar", lambda e, lgb=lgb: e.activation(out=dsc.ap[:, 1:2], in_=cst(1025, 1026), func=AF.Exp,
                                                               scale=lgb), [P_c, P_lg], [dsc])
                S.op("scalar", lambda e, lgf=lgf: e.activation(out=dsc.ap[:, 2:3], in_=cst(1026, 1027), func=AF.Exp,
                                                               scale=lgf), [P_c, P_lg], [dsc])
                S.op("scalar", lambda e, lgb=lgb: e.activation(out=dsc.ap[:, 3:4], in_=cst(1026, 1027), func=AF.Exp,
                                                               scale=lgb), [P_c, P_lg], [dsc])
                for di in range(2):
                    for dc in range(2):
                        S.op("vector", lambda e, di=di, dc=dc: e.memset(Sst[di][dc].ap, 0.0), [], [Sst[di][dc]])
                for dc in range(2):
                    sb0 = Sbf[0][par[0]][dc]
                    S.op("vector", lambda e, sb0=sb0: e.memset(sb0.ap, 0.0), [], [sb0])
                for di in range(2):
                    S.op("vector", lambda e, di=di: e.tensor_tensor(
                        out=qsc[di].ap.rearrange("p c (t i) -> p (c t) i", i=128),
                        in0=qT.ap.rearrange("p c (t i) -> p (c t) i", i=128),
                        in1=dq[di].ap.unsqueeze(1).to_broadcast([128, 2 * NT_FULL, 128]), op=ALU.mult),
                        [qT, dq[di]], [qsc[di]])
                seq = [1, 0] + list(range(NT_ALL - 1, 1, -1))
                S.op("vector", lambda e: e.memset(snap.ap[:, seq[0], :], 0.0), [], [snap])
                plan_kv(seq[0], 1, (0, 6))
                plan_kv(seq[1], 1, (0, 6))
                for idx, t in enumerate(seq[:-1]):
                    nt = seq[idx + 1]
                    dst = None
                    if nt < NT_FULL:
                        dst = ([snap, snap], [snap.ap[:, nt, 0:512], snap.ap[:, nt, 512:1024]])
                    if idx + 2 < len(seq) - 1:
                        plan_kv(seq[idx + 2], 1, (0, 6))
                    state_update(t, 1, dst)
                pend = []
                plan_kv(0, 0)
                for t in range(NT_FULL):
                    pi = next_ps(4, 7)
                    for dc in range(2):
                        S.op("tensor", lambda e, pi=pi, dc=dc, t=t: e.matmul(
                            PS[pi].ap[:, 0:128], lhsT=kT.ap[:, dc, t * 128:(t + 1) * 128],
                            rhs=qT.ap[:, dc, t * 128:(t + 1) * 128], start=(dc == 0), stop=(dc == 1)),
                            [kT, qT], [PS[pi]])
                    pb = pT[cn["pT"] % 2]
                    cn["pT"] += 1
                    S.op("vector", lambda e, pi=pi, pb=pb: e.tensor_tensor(out=pb.ap, in0=PS[pi].ap[:, 0:128],
                                                                           in1=maskc.ap, op=ALU.mult),
                         [PS[pi], maskc], [pb])
                    qb = q_scaled(t, 0)
                    qbb = q_scaled(t, 1)
                    po = next_ps(4, 7)
                    sb = Sbf[0][par[0]]
                    S.op("tensor", lambda e, po=po, pb=pb, t=t: e.matmul(PS[po].ap, lhsT=pb.ap, rhs=vv.ap[:, t, :],
                                                                         start=True, stop=False), [pb, vv], [PS[po]])
                    for dc in range(2):
                        S.op("tensor", lambda e, po=po, dc=dc, qb=qb, sb=sb: e.matmul(
                            PS[po].ap, lhsT=qb.ap[:, dc, :], rhs=sb[dc].ap, start=False, stop=False),
                            [qb.buf, sb[dc]], [PS[po]])
                    for dc in range(2):
                        S.op("tensor", lambda e, po=po, dc=dc, qbb=qbb, t=t: e.matmul(
                            PS[po].ap, lhsT=qbb.ap[:, dc, :], rhs=snap.ap[:, t, dc * 512:(dc + 1) * 512], start=False,
                            stop=(dc == 1)), [qbb.buf, snap], [PS[po]])
                    if t != NT_FULL - 1:
                        np_ = 1 - par[0]
                        state_update(t, 0, (Sbf[0][np_], [Sbf[0][np_][0].ap, Sbf[0][np_][1].ap]))
                        par[0] = np_
                        if t + 1 != NT_FULL - 1:
                            plan_kv(t + 1, 0)
                    y = yb[cn["ow"] % 2]
                    sq = ssq[cn["ow"] % 2]
                    cn["ow"] += 1
                    o = PS[po]
                    S.op("scalar", lambda e, o=o, sq=sq: e.activation(out=junk.ap, in_=o.ap, func=AF.Square,
                                                                      scale=float(512 ** -0.5), accum_out=sq.ap),
                         [o], [junk, sq])
                    S.op("scalar", lambda e, sq=sq: e.activation(out=sq.ap, in_=sq.ap, func=AF.Sqrt,
                                                                 bias=P_eps.ap[:, 0:1]), [sq, P_eps], [sq])
                    while pend:
                        pend.pop(0)()

                    def tr(y=y, t=t, o=o, sq=sq):
                        S.op("vector", lambda e: e.reciprocal(out=sq.ap, in_=sq.ap), [sq], [sq])
                        S.op("vector", lambda e: e.scalar_tensor_tensor(
                            out=y.ap, in0=o.ap, scalar=sq.ap[:, 0:1], in1=sg.ap[:, t, :], op0=ALU.mult, op1=ALU.mult),
                            [o, sq, sg], [y])
                        pj = next_ps(7, 8)
                        for j in range(4):
                            S.op("tensor", lambda e, j=j, pj=pj, y=y: e.transpose(
                                ps_bf[pj][:, j * 128:(j + 1) * 128], y.ap[:, j * 128:(j + 1) * 128], P_identb.ap),
                                [y, P_identb], [PS[pj]])
                        S.op("scalar", lambda e, pj=pj, t=t: e.activation(
                            out=yT.ap[:, :, t * 128:(t + 1) * 128],
                            in_=ps_bf[pj][:, 0:512].rearrange("p (c t) -> p c t", c=4), func=AF.Copy), [PS[pj]], [yT])
                    pend.append(tr)
                while pend:
                    pend.pop(0)()
                S.dma("sync", yT_d[4 * h:4 * h + 4].rearrange("c p t -> p c t"), yT.ap, reads=[yT],
                      writes=[dbuf("yT_d")])
            S.barrier()

        def make_residual(layer, gcol0, x_src, x_dst, row_of, nxb=8):
            gb = [[A.alloc([128, 512], F32, f"gb{r}{i}") for i in range(2)] for r in range(2)]
            xb = [A.alloc([128, 512], F32, f"xb{i}") for i in range(nxb)]
            tmp = [A.alloc([128, 512], F32, f"rtmp{i}") for i in range(2)]
            st = {"gb": 0, "xb": 0, "tmp": 0, "cur": {}, "g": {}}

            def pre(grp, bi, c0, ncols):
                rows = sorted(set(row_of(t) for t in grp))
                for r in rows:
                    b = gb[r][st["gb"] % 2]
                    S.dma("sync", b.ap[:, 0:ncols],
                          mod_d[layer, r:r + 1, gcol0 + c0:gcol0 + c0 + ncols].partition_broadcast(128),
                          reads=[dbuf("mod_d")], writes=[b])
                    st["g"][r] = b
                st["gb"] += 1
                for gi, t in enumerate(grp):
                    b = xb[st["xb"] % nxb]
                    st["xb"] += 1
                    S.dma("sync", b.ap[:, 0:ncols], x_src(t)[:, c0:c0 + ncols], writes=[b])
                    st["cur"][gi] = b

            def evac(t, gi, bi, c0, ncols, pi):
                b = st["cur"][gi]
                g = st["g"][row_of(t)]
                tm = tmp[st["tmp"] % 2]
                st["tmp"] += 1
                S.op("vector", lambda e: e.tensor_tensor(out=tm.ap[:, 0:ncols], in0=PS[pi].ap[:, 0:ncols],
                                                         in1=g.ap[:, 0:ncols], op=ALU.mult), [PS[pi], g], [tm])
                S.op("vector", lambda e: e.tensor_tensor(out=b.ap[:, 0:ncols], in0=b.ap[:, 0:ncols],
                                                         in1=tm.ap[:, 0:ncols], op=ALU.add), [b, tm], [b])
                S.dma("sync", x_dst(t)[:, c0:c0 + ncols], b.ap[:, 0:ncols], reads=[b])

            return pre, evac

        def rows_full(dram):
            return lambda t: dram[t * 128:(t + 1) * 128, :]

        def rows_own(dram):
            return lambda t: dram[(t - 2) * 128:(t - 1) * 128, :]

        full_groups = [[0, 1, 2, 3], [4, 5, 6, 7], [8, 9, 10, 11], [12, 13, 14, 15], [16, 17, 18]]
        own_groups = [[2, 3, 4, 5], [6, 7, 8, 9], [10, 11, 12, 13], [14, 15, 16, 17]]
        blocks4 = lambda grp: [(i * 512, 512) for i in range(4)]

        def phase_ret_out():
            A.off = persist_off
            aTgs = [A.alloc([128, 32, 512], BF16, f"yTg{i}") for i in range(2)]
            wbufs = [A.alloc([128, 16, 512], BF16, f"wo{i}") for i in range(3)]
            pre, evac = make_residual(0, 2 * D, rows_full(xin), rows_full(x1_d), lambda t: 1 if t < 2 else 0)

            def prep(grp, aTg):
                r0 = grp[0] * 128
                n = len(grp) * 128
                S.dma("sync", aTg.ap[:, :, 0:n], yT_d[:, :, r0:r0 + n].rearrange("c p t -> p c t"),
                      reads=[dbuf("yT_d")], writes=[aTg])

            gemm_tok(full_groups, prep, WS["ret_w_out"], ret_w_out, 32, blocks4, evac, wbufs, aTgs, kparts=2,
                     pre=pre)
            S.barrier()

        def phase_mlp(layer, groups, x_src, x_dst, row_of):
            A.off = persist_off
            h1T = A.alloc([128, 64, 512], BF16, "h1T")
            wk = alloc_norm_work()
            xts = [A.alloc([128, D], F32, f"mxt{i}") for i in range(2)]
            a16 = A.alloc([128, 16, 512], BF16, "a16")
            w1b = [A.alloc([128, 16, 256], BF16, f"w1b{i}") for i in range(2)]
            w2b = [A.alloc([128, 16, 512], BF16, f"w2b{i}") for i in range(3)]
            rl = [A.alloc([128, 512], F32, f"rl{i}") for i in range(2)]
            pre, evac = make_residual(layer, 5 * D, x_src, x_dst, row_of, nxb=6)
            cn = {"x": 0, "w1": 0, "rl": 0}
            s1 = WStream(WS[f"mlp_w1_{layer}"], mlp_w1[layer], w1b, [(0, fb) for g in groups for fb in range(32)])

            def prep_norm(grp, h1):
                for gi, t in enumerate(grp):
                    xt = xts[cn["x"] % 2]
                    cn["x"] += 1
                    S.dma("sync", xt.ap, x_src(t), writes=[xt])
                    norm_mod_T(xt, row_of(t), layer * 2 + 1, a16, gi, wk)

            def prep(grp, h1):
                n = len(grp) * 128
                for fb in range(32):
                    w = s1.get(cn["w1"])
                    cn["w1"] += 1
                    for sub in range(2):
                        pi = next_ps(0, 4)
                        for kc in range(16):
                            S.op("tensor", lambda e, pi=pi, kc=kc, w=w, sub=sub: e.matmul(
                                PS[pi].ap[:, 0:n], lhsT=w.ap[:, kc, sub * 128:(sub + 1) * 128],
                                rhs=a16.ap[:, kc, 0:n], start=(kc == 0), stop=(kc == 15)), [w, a16], [PS[pi]])
                        r = rl[cn["rl"] % 2]
                        cn["rl"] += 1
                        S.op("scalar", lambda e, pi=pi, r=r: e.activation(out=r.ap[:, 0:n], in_=PS[pi].ap[:, 0:n],
                                                                          func=AF.Relu), [PS[pi]], [r])
                        S.op("vector", lambda e, r=r, fb=fb, sub=sub: e.tensor_tensor(
                            out=h1.ap[:, fb * 2 + sub, 0:n], in0=r.ap[:, 0:n], in1=r.ap[:, 0:n], op=ALU.mult),
                            [r], [h1])

            gemm_tok(groups, prep, WS[f"mlp_w2_{layer}"], mlp_w2[layer], 64, blocks4, evac, w2b, [h1T], kparts=4,
                     pre=pre, prep_early=prep_norm)
            S.barrier()

        def phase_att_proj():
            A.off = persist_off
            wk = alloc_norm_work()
            xts = [A.alloc([128, D], F32, f"xt{i}") for i in range(2)]
            aTgs = [A.alloc([128, 16, 512], BF16, f"aTg{i}") for i in range(2)]
            wbufs = [A.alloc([128, 16, 512], BF16, f"w{i}") for i in range(3)]
            tabs = [A.alloc([128, 1024], F32, f"tab{i}") for i in range(4)]
            xs_b = [A.alloc([128, 512], F32, f"xs{i}") for i in range(2)]
            rw = {"t1": A.alloc([128, 512], F32, "t1"), "t2": A.alloc([128, 512], F32, "t2")}
            ob = [A.alloc([128, 512], BF16, f"ob{i}") for i in range(6)]
            qTg = A.alloc([128, 16, 512], BF16, "qTg")
            kTg = A.alloc([128, 4, 512], BF16, "kTg")
            cnt = {"x": 0, "xs": 0, "ob": 0}
            tabmap = {}

            def prep(grp, aTg):
                for gi, t in enumerate(grp):
                    xt = xts[cnt["x"] % 2]
                    cnt["x"] += 1
                    S.dma("sync", xt.ap, x2_d[t * 128:(t + 1) * 128, :], writes=[xt])
                    norm_mod_T(xt, 1 if t < 2 else 0, 2, aTg, gi, wk)
                    tb = tabs[gi]
                    S.dma("sync", tb.ap, rope_a[t * 128:(t + 1) * 128, :], writes=[tb])
                    tabmap[t] = tb

            def blocks(grp):
                return [(i * 512, 512) for i in range(6)]

            def evac(t, gi, bi, c0, ncols, pi):
                r0 = t * 128
                o = ob[cnt["ob"] % 6]
                cnt["ob"] += 1
                if c0 < 2560:
                    isq = c0 < 2048
                    xs = xs_b[cnt["xs"] % 2]
                    cnt["xs"] += 1
                    S.op("scalar", lambda e: e.activation(out=xs.ap, in_=PS[pi].ap, func=AF.Copy), [PS[pi]], [xs])
                    rope_block(xs, tabmap[t], gi, (o, o.ap), 32, rw)
                    tg = qTg if isq else kTg
                    cb = (c0 // 128) if isq else 0

                    def tr(o=o, tg=tg, cb=cb, gi=gi):
                        pj = next_ps(4, 8)
                        for j in range(4):
                            S.op("tensor", lambda e, j=j, pj=pj, o=o: e.transpose(
                                ps_bf[pj][:, j * 128:(j + 1) * 128], o.ap[:, j * 128:(j + 1) * 128], P_identb.ap),
                                [o, P_identb], [PS[pj]])
                        S.op("scalar", lambda e, pj=pj, tg=tg, cb=cb, gi=gi: e.activation(
                            out=tg.ap[:, cb:cb + 4, gi * 128:(gi + 1) * 128],
                            in_=ps_bf[pj][:, 0:512].rearrange("p (c t) -> p c t", c=4), func=AF.Copy), [PS[pj]], [tg])
                    GCTX["defer"](tr)
                else:
                    S.op("scalar", lambda e: e.activation(out=o.ap, in_=PS[pi].ap, func=AF.Copy), [PS[pi]], [o])
                    S.dma("sync", av_d[r0:r0 + 128, :], o.ap, reads=[o], writes=[dbuf("av_d")])

            def post(grp):
                r0 = grp[0] * 128
                n = len(grp) * 128
                S.dma("sync", aq_d[:, :, r0:r0 + n].rearrange("c p t -> p c t"), qTg.ap[:, :, 0:n], reads=[qTg],
                      writes=[dbuf("aq_d")])
                S.dma("sync", akT_d[:, :, r0:r0 + n].rearrange("c p t -> p c t"), kTg.ap[:, :, 0:n], reads=[kTg],
                      writes=[dbuf("akT_d")])

            gemm_tok(full_groups, prep, WS["attn_w_in"], attn_w_in, 16, blocks, evac, wbufs, aTgs, post)
            S.barrier()

        def phase_att():
            A.off = persist_off
            SCALE = float(128 ** -0.5)
            qT = A.alloc([128, 4, R_FULL], BF16, "aqT")
            kT = A.alloc([128, R_FULL], BF16, "akT")
            vv = A.alloc([128, NT_FULL, 128], BF16, "avv")
            oTh = A.alloc([128, 4, 2048], BF16, "oTh")
            am = A.alloc([128, 768], F32, "am")
            sm = [A.alloc([128, 640], F32, f"sm{i}") for i in range(4)]
            pn = [A.alloc([128, 640], BF16, f"pn{i}") for i in range(4)]
            pTa = [A.alloc([128, 5, 512], BF16, f"pTa{i}") for i in range(2)]
            sc = [A.alloc([128, 8], F32, f"asc{i}") for i in range(4)]
            cn = {"i": 0, "pa": 0}
            stage = [A.alloc([128, 16, 512], BF16, f"stg{i}") for i in range(3)]
            wconvert(WS["attn_w_out"], wview(attn_w_out), stage)
            wconvert(WS["mlp_w1_1"], wview(mlp_w1[1]), stage)
            wconvert(WS["mlp_w2_1"], wview(mlp_w2[1]), stage)
            S.dma("sync", am.ap, amask, writes=[am])
            v_v = av_d.rearrange("(t p) c -> p t c", p=128)
            for kvh in range(4):
                S.dma("sync", qT.ap, aq_d[kvh * 4:kvh * 4 + 4].rearrange("c p t -> p c t"), reads=[dbuf("aq_d")],
                      writes=[qT])
                S.dma("sync", kT.ap, akT_d[kvh], reads=[dbuf("akT_d")], writes=[kT])
                S.dma("sync", vv.ap, v_v[:, :, kvh * 128:(kvh + 1) * 128], reads=[dbuf("av_d")], writes=[vv],
                      slow=True)
                stA, stB, stC = [], [], []
                for n in range(16):
                    t = n + 2
                    r0 = t * 128
                    pa = pTa[cn["pa"] % 2]
                    cn["pa"] += 1
                    for g in range(4):
                        hq = kvh * 4 + g
                        i2 = cn["i"] % 4
                        cn["i"] += 1
                        smb, pnb, scb = sm[i2], pn[i2], sc[i2]
                        m0 = 384 if n == 0 else 0

                        def fa(g=g, r0=r0, smb=smb, scb=scb, hq=hq, m0=m0):
                            p1 = next_ps(0, 4)
                            p2 = next_ps(0, 4)
                            S.op("tensor", lambda e: e.matmul(
                                PS[p1].ap[:, 0:384], lhsT=qT.ap[:, g, r0:r0 + 128], rhs=kT.ap[:, r0 - 128:r0 + 256],
                                start=True, stop=True), [qT, kT], [PS[p1]])
                            S.op("tensor", lambda e: e.matmul(
                                PS[p2].ap[:, 0:256], lhsT=qT.ap[:, g, r0:r0 + 128], rhs=kT.ap[:, 0:256],
                                start=True, stop=True), [qT, kT], [PS[p2]])
                            S.op("vector", lambda e: e.tensor_tensor(
                                out=smb.ap[:, 0:384], in0=PS[p1].ap[:, 0:384], in1=am.ap[:, m0:m0 + 384], op=ALU.add),
                                [PS[p1], am], [smb])
                            S.op("scalar", lambda e: e.activation(out=smb.ap[:, 384:640], in_=PS[p2].ap[:, 0:256],
                                                                  func=AF.Copy), [PS[p2]], [smb])
                            S.op("vector", lambda e: e.reduce_max(out=scb.ap[:, 0:1], in_=smb.ap, axis=AX.X),
                                 [smb], [scb])
                            S.op("vector", lambda e: e.tensor_scalar(
                                out=scb.ap[:, 1:2], in0=scb.ap[:, 0:1], scalar1=SCALE, scalar2=P_sink.ap[:, hq:hq + 1],
                                op0=ALU.mult, op1=ALU.max), [scb, P_sink], [scb])
                            S.op("vector", lambda e: e.tensor_scalar(
                                out=scb.ap[:, 2:3], in0=scb.ap[:, 1:2], scalar1=-1.0, scalar2=None, op0=ALU.mult),
                                [scb], [scb])
                            S.op("scalar", lambda e: e.activation(
                                out=smb.ap, in_=smb.ap, func=AF.Exp, bias=scb.ap[:, 2:3], scale=SCALE,
                                accum_out=scb.ap[:, 3:4]), [smb, scb], [smb, scb])
                            S.op("scalar", lambda e: e.activation(
                                out=scb.ap[:, 4:5], in_=P_sink.ap[:, hq:hq + 1], func=AF.Exp, bias=scb.ap[:, 2:3]),
                                [scb, P_sink], [scb])

                        def fb(smb=smb, pnb=pnb, scb=scb):
                            S.op("vector", lambda e: e.tensor_tensor(out=scb.ap[:, 5:6], in0=scb.ap[:, 3:4],
                                                                     in1=scb.ap[:, 4:5], op=ALU.add), [scb], [scb])
                            S.op("vector", lambda e: e.reciprocal(out=scb.ap[:, 5:6], in_=scb.ap[:, 5:6]),
                                 [scb], [scb])
                            S.op("vector", lambda e: e.tensor_scalar(
                                out=pnb.ap, in0=smb.ap, scalar1=scb.ap[:, 5:6], scalar2=None, op0=ALU.mult),
                                [smb, scb], [pnb])

                        def fc(pnb=pnb, pa=pa, g=g, n=n, t=t):
                            pj = next_ps(4, 8)
                            for j in range(5):
                                S.op("tensor", lambda e, j=j: e.transpose(
                                    ps_bf[pj][:, j * 128:(j + 1) * 128], pnb.ap[:, j * 128:(j + 1) * 128],
                                    P_identb.ap), [pnb, P_identb], [PS[pj]])
                            S.op("scalar", lambda e: e.activation(
                                out=pa.ap[:, :, g * 128:(g + 1) * 128],
                                in_=ps_bf[pj][:, 0:640].rearrange("p (c t) -> p c t", c=5), func=AF.Copy),
                                [PS[pj]], [pa])
                            if g == 3:
                                po = next_ps(0, 4)
                                vt = [t - 1, t, t + 1, 0, 1]
                                for j in range(5):
                                    S.op("tensor", lambda e, j=j: e.matmul(
                                        PS[po].ap, lhsT=vv.ap[:, vt[j], :], rhs=pa.ap[:, j, :], start=(j == 0),
                                        stop=(j == 4)), [vv, pa], [PS[po]])
                                S.op("scalar", lambda e: e.activation(
                                    out=oTh.ap[:, :, n * 128:(n + 1) * 128],
                                    in_=PS[po].ap.rearrange("p (c t) -> p c t", c=4), func=AF.Copy), [PS[po]], [oTh])
                        stA.append(fa)
                        stB.append(fb)
                        stC.append(fc)
                NI = len(stA)
                for st_ in range(NI + 2):
                    if st_ < NI:
                        stA[st_]()
                    if 0 <= st_ - 1 < NI:
                        stB[st_ - 1]()
                    if 0 <= st_ - 2 < NI:
                        stC[st_ - 2]()
                S.dma("sync", oT_d[kvh * 4:kvh * 4 + 4].rearrange("c p t -> p c t"), oTh.ap, reads=[oTh],
                      writes=[dbuf("oT_d")])
            S.barrier()

        def phase_att_out():
            A.off = persist_off
            aTgs = [A.alloc([128, 16, 512], BF16, f"oTg{i}") for i in range(2)]
            wbufs = [A.alloc([128, 16, 512], BF16, f"wo{i}") for i in range(3)]
            pre, evac = make_residual(1, 2 * D, rows_full(x2_d), rows_own(x3_d), lambda t: 0)

            def prep(grp, aTg):
                c0 = (grp[0] - 2) * 128
                n = len(grp) * 128
                S.dma("sync", aTg.ap[:, :, 0:n], oT_d[:, :, c0:c0 + n].rearrange("c p t -> p c t"),
                      reads=[dbuf("oT_d")], writes=[aTg])

            gemm_tok(own_groups, prep, WS["attn_w_out"], attn_w_out, 16, blocks4, evac, wbufs, aTgs, pre=pre)
            S.barrier()

        def phase_final():
            A.off = persist_off
            fg = A.alloc([128, D], F32, "fg")
            xts = [A.alloc([128, D], F32, f"fx{i}") for i in range(3)]
            junk = A.alloc([128, D], BF16, "fjunk")
            ssq = [A.alloc([128, 1], F32, f"fssq{i}") for i in range(2)]
            S.dma("sync", fg.ap, final_g.partition_broadcast(128), writes=[fg])
            for n in range(16):
                xt = xts[n % 3]
                sq = ssq[n % 2]
                S.dma("sync", xt.ap, x4_d[n * 128:(n + 1) * 128, :], writes=[xt])
                S.op("scalar", lambda e, xt=xt, sq=sq: e.activation(out=junk.ap, in_=xt.ap, func=AF.Square,
                                                                    scale=float(D ** -0.5), accum_out=sq.ap),
                     [xt], [junk, sq])
                S.op("vector", lambda e, sq=sq: e.tensor_scalar(out=sq.ap, in0=sq.ap, scalar1=EPS, scalar2=None,
                                                                op0=ALU.add), [sq], [sq])
                S.op("scalar", lambda e, sq=sq: e.activation(out=sq.ap, in_=sq.ap, func=AF.Sqrt), [sq], [sq])
                S.op("vector", lambda e, sq=sq: e.reciprocal(out=sq.ap, in_=sq.ap), [sq], [sq])
                S.op("vector", lambda e, xt=xt, sq=sq: e.scalar_tensor_tensor(
                    out=xt.ap, in0=xt.ap, scalar=sq.ap[:, 0:1], in1=fg.ap, op0=ALU.mult, op1=ALU.mult),
                    [xt, sq, fg], [xt])
                S.dma("sync", out[n * 128:(n + 1) * 128, :], xt.ap, reads=[xt])
            S.barrier()

        def phase_bench(mode):
            A.off = persist_off
            aTg = A.alloc([128, 16, 512], BF16, "b_aTg")
            wb = [A.alloc([128, 16, 512], BF16, f"b_w{i}") for i in range(2)]
            ob = [A.alloc([128, 512], BF16, f"b_o{i}") for i in range(2)]
            S.op("vector", lambda e: e.memset(aTg.ap, 0.5), [], [aTg])
            for w in wb:
                S.op("vector", lambda e, w=w: e.memset(w.ap, 0.25), [], [w])
            for it in range(160):
                w = wb[it % 2]
                if mode >= 2:
                    S.dma("gpsimd", w.ap, ret_w_in[:, (it % 24) * 512:(it % 24 + 1) * 512].rearrange(
                        "(c p) n -> p c n", p=128), writes=[w])
                pi = next_ps(0, 4)
                for kc in range(16):
                    S.op("tensor", lambda e, pi=pi, kc=kc, w=w, it=it: e.matmul(
                        PS[pi].ap, lhsT=aTg.ap[:, kc, (it % 4) * 128:(it % 4 + 1) * 128], rhs=w.ap[:, kc, :],
                        start=(kc == 0) or mode == 0, stop=(kc == 15) or mode == 0), [aTg, w], [PS[pi]])
                o = ob[it % 2]
                S.op("scalar", lambda e, pi=pi, o=o: e.activation(out=o.ap, in_=PS[pi].ap, func=AF.Copy), [PS[pi]], [o])
            S.barrier()

        phases = {
            "ada": phase_ada, "ret_proj": phase_ret_proj, "ret": phase_ret, "ret_out": phase_ret_out,
            "mlp0": lambda: phase_mlp(0, full_groups, rows_full(x1_d), rows_full(x2_d), lambda t: 1 if t < 2 else 0),
            "att_proj": phase_att_proj, "att": phase_att, "att_out": phase_att_out,
            "mlp1": lambda: phase_mlp(1, own_groups, rows_own(x3_d), rows_own(x4_d), lambda t: 0),
            "final": phase_final,
        }
        order = ["ada", "ret_proj", "ret", "ret_out", "mlp0", "att_proj", "att", "att_out", "mlp1", "final"]
        stop_after = None
        for d in dbg:
            if d.startswith("stop:"):
                stop_after = d[5:]
        for d in dbg:
            if d.startswith("bench:"):
                order = []
                phase_bench(int(d[6:]))
        for ph in order:
            phases[ph]()
            if stop_after == ph:
                break

        S.barrier()
        for e in Sched.ENGS:
            S.op(e, lambda en: en.nop(), [], [])

        S.finalize(eng_sems, dma_sems)
        with nc.Block() as block:
            @block.sync
            def _(e):
                S.emit("sync", e)

            @block.gpsimd
            def _(e):
                S.emit("gpsimd", e)

            @block.tensor
            def _(e):
                S.emit("tensor", e)

            @block.vector
            def _(e):
                S.emit("vector", e)

            @block.scalar
            def _(e):
                S.emit("scalar", e)

    return nc


NEG_MASK = -30000.0


def _const_tables():
    c = np.zeros((128, 1032), np.float32)
    p = np.arange(128, dtype=np.float32)
    jj = p[:, None]
    ii = p[None, :]
    c[:, 0:128] = np.eye(128, dtype=np.float32)
    c[:, 128:256] = np.maximum(ii - jj, 0.0)
    c[:, 256:384] = (ii >= jj).astype(np.float32)
    c[:, 384:512] = np.maximum(jj - ii, 0.0)
    c[:, 512:640] = (jj >= ii).astype(np.float32)
    c[:, 640:768] = ii + 1.0
    c[:, 768:896] = 128.0 - ii
    c[:, 1024] = 127.0 - p
    c[:, 1025] = p
    c[:, 1026] = 128.0
    am = np.zeros((128, 768), np.float32)
    prev = np.where(jj.T >= ii.T, 0.0, NEG_MASK)
    i_ = p[:, None]
    j_ = p[None, :]
    prev = np.where(j_ >= i_, 0.0, NEG_MASK).astype(np.float32)
    nxt = np.where(j_ <= i_, 0.0, NEG_MASK).astype(np.float32)
    am[:, 0:128] = prev
    am[:, 256:384] = nxt
    am[:, 384:512] = NEG_MASK
    am[:, 640:768] = nxt
    return c, am


def _rope_tables(pos):
    f32 = np.float32
    L = pos.shape[0]
    inv_r = (f32(10000.0) ** (-np.arange(0, 256, 2, dtype=f32) / f32(256))).astype(f32)
    ang = pos.astype(f32)[:, None] * inv_r[None, :]
    cr, sr = np.cos(ang).astype(f32), np.sin(ang).astype(f32)
    rr = np.zeros((R_ALL, 1024), f32)
    rr[:NCTX, 0:512] = 1.0
    rr[NCTX:, 0:512] = np.tile(cr, (1, 4))
    rr[NCTX:, 512:1024] = np.tile(np.concatenate([-sr, sr], axis=1), (1, 2))
    inv_a = (f32(10000.0) ** (-np.arange(0, 64, 2, dtype=f32) / f32(64))).astype(f32)
    rows = (pos // 64).astype(f32)
    cols = (pos % 64).astype(f32)
    ar = rows[:, None] * inv_a[None, :]
    ac = cols[:, None] * inv_a[None, :]
    c128 = np.concatenate([np.cos(ar), np.cos(ar), np.cos(ac), np.cos(ac)], axis=1).astype(f32)
    s128 = np.concatenate([-np.sin(ar), np.sin(ar), -np.sin(ac), np.sin(ac)], axis=1).astype(f32)
    ra = np.zeros((R_FULL, 1024), f32)
    ra[:NCTX, 0:512] = 1.0
    n = R_FULL - NCTX
    ra[NCTX:, 0:512] = np.tile(c128[:n], (1, 4))
    ra[NCTX:, 512:1024] = np.tile(s128[:n], (1, 4))
    return rr, ra


def make_in_maps(inp, cores=range(8)):
    f32 = np.float32
    g = {k: np.asarray(v) for k, v in inp.items()}
    consts, amask = _const_tables()
    shared = {
        "ada_w": np.ascontiguousarray(g["ada_w"], f32), "ada_b": np.ascontiguousarray(g["ada_b"], f32),
        "norm_mix_g": np.ascontiguousarray(g["norm_mix_g"], f32),
        "norm_mlp_g": np.ascontiguousarray(g["norm_mlp_g"], f32),
        "mlp_w1": np.ascontiguousarray(g["mlp_w1"], f32), "mlp_w2": np.ascontiguousarray(g["mlp_w2"], f32),
        "ret_w_in": np.ascontiguousarray(g["ret_w_in"][0], f32),
        "ret_w_out": np.ascontiguousarray(g["ret_w_out"][0], f32),
        "attn_w_in": np.ascontiguousarray(g["attn_w_in"][0], f32),
        "attn_w_out": np.ascontiguousarray(g["attn_w_out"][0], f32),
        "attn_sink": np.ascontiguousarray(g["attn_sink"], f32).reshape(1, 16),
        "final_g": np.ascontiguousarray(g["final_norm_g"], f32).reshape(1, D),
        "consts": consts, "amask": amask,
    }
    tabs = {}
    for h in range(2):
        pos = np.arange(4096) if h == 0 else np.arange(4095, -1, -1)
        tabs[h] = _rope_tables(pos)
    maps = []
    for core in cores:
        b, h = core // 2, core % 2
        x = g["x"][b]
        cx = g["ctx"][b]
        if h == 1:
            x = x[::-1]
            cx = cx[::-1]
        xin = np.ascontiguousarray(np.concatenate([cx, x], axis=0), f32)
        cvec = np.ascontiguousarray(np.stack([g["c"][b], g["c_ctx"]], axis=0), f32)
        df, db = g["ret_decay_fwd"][0], g["ret_decay_bwd"][0]
        dec = np.concatenate([df, db] if h == 0 else [db, df]).astype(f32).reshape(1, 16)
        m = dict(shared)
        m.update({"xin": xin, "cvec": cvec, "dec": dec, "rope_r": tabs[h][0], "rope_a": tabs[h][1]})
        maps.append(m)
    return maps


def assemble(results, B=4):
    out = np.zeros((B, 4096, D), np.float32)
    for core, r in enumerate(results):
        b, h = core // 2, core % 2
        o = np.asarray(r["out"])
        if h == 0:
            out[b, :2048] = o
        else:
            out[b, 2048:] = o[::-1]
    return out


_NC_CACHE = {}


def kernel(**inputs):
    if "nc" not in _NC_CACHE:
        _NC_CACHE["nc"] = build_program()
    nc = _NC_CACHE["nc"]
    in_maps = make_in_maps(inputs)
    res = run_bass_kernel_spmd(nc, in_maps, core_ids=list(range(8)))
    return assemble(res.results)
```

```python
import numpy as np
from contextlib import ExitStack
import concourse.bass as bass
import concourse.mybir as mybir
from concourse.bass_utils import run_bass_kernel_spmd

F32 = mybir.dt.float32
BF16 = mybir.dt.bfloat16
AF = mybir.ActivationFunctionType
ALU = mybir.AluOpType
AX = mybir.AxisListType

D = 2048
KC = 16
DFF = 8192
NCTX = 256
R_ALL = 4352
R_FULL = 2432
NT_ALL = 34
NT_FULL = 19
EPS = 1e-6
NS_DMA = 8
ARENA_BYTES = 211968


class Ins:
    __slots__ = ("eng", "fn", "waits", "signal", "sem", "val", "prev_val", "is_dma")

    def __init__(self, eng, fn, is_dma):
        self.eng = eng
        self.fn = fn
        self.waits = []
        self.signal = is_dma
        self.sem = None
        self.val = 0
        self.prev_val = 0
        self.is_dma = is_dma


class Buf:
    __slots__ = ("ap", "w", "r", "name")

    def __init__(self, ap, name=""):
        self.ap = ap
        self.w = None
        self.r = []
        self.name = name


class Sched:
    ENGS = ("sync", "gpsimd", "tensor", "vector", "scalar")

    def __init__(self):
        self.streams = {e: [] for e in self.ENGS}
        self.bar = {e: [] for e in self.ENGS}

    def op(self, eng, fn, reads=(), writes=(), dma=False):
        ins = Ins(eng, fn, dma)
        deps = []
        for b in reads:
            if b.w is not None:
                deps.append(b.w)
        for b in writes:
            if b.w is not None:
                deps.append(b.w)
            deps.extend(b.r)
        if self.bar[eng]:
            deps.extend(self.bar[eng])
            self.bar[eng] = []
        seen = set()
        for d in deps:
            if id(d) in seen or d is ins:
                continue
            seen.add(id(d))
            if d.eng == eng and not d.is_dma and eng == "tensor":
                continue
            ins.waits.append(d)
            d.signal = True
        for b in reads:
            if not dma:
                b.r = [x for x in b.r if x.is_dma or x.eng != eng]
            b.r.append(ins)
        for b in writes:
            b.w = ins
            b.r = []
        self.streams[eng].append(ins)
        return ins

    def dma(self, q, out, in_, reads=(), writes=(), slow=False):
        if slow:
            return self.op(q, lambda e: e.dma_start(out=out, in_=in_, allow_slow_non_contiguous=True),
                           reads, writes, dma=True)
        return self.op(q, lambda e: e.dma_start(out=out, in_=in_), reads, writes, dma=True)

    def barrier(self):
        deps = []
        for e in self.ENGS:
            st = self.streams[e]
            nd = 0
            last_c = None
            for ins in reversed(st):
                if ins.is_dma:
                    if nd < NS_DMA:
                        deps.append(ins)
                        nd += 1
                elif last_c is None:
                    last_c = ins
                    deps.append(ins)
                if nd >= NS_DMA and last_c is not None:
                    break
        for e in self.ENGS:
            self.bar[e] = list(deps)

    def finalize(self, eng_sems, dma_sems):
        for e in self.ENGS:
            cnt = 0
            di = 0
            for ins in self.streams[e]:
                if ins.is_dma:
                    s = di % NS_DMA
                    k = di // NS_DMA
                    ins.sem = dma_sems[e][s]
                    ins.val = 16 * (k + 1)
                    ins.prev_val = 16 * k
                    di += 1
                elif ins.signal:
                    cnt += 1
                    ins.sem = eng_sems[e]
                    ins.val = cnt

    def emit(self, ename, eng):
        known = {}
        for ins in self.streams[ename]:
            waits = {}
            for d in ins.waits:
                k = id(d.sem)
                if k not in waits or waits[k][1] < d.val:
                    waits[k] = (d.sem, d.val)
            if ins.is_dma and ins.prev_val > 0:
                k = id(ins.sem)
                if k not in waits or waits[k][1] < ins.prev_val:
                    waits[k] = (ins.sem, ins.prev_val)
            for k, (sem, val) in waits.items():
                if known.get(k, 0) >= val:
                    continue
                eng.wait_ge(sem, val)
                known[k] = val
            bi = ins.fn(eng)
            if ins.is_dma:
                bi.then_inc(ins.sem, 16)
            elif ins.signal:
                bi.then_inc(ins.sem, 1)


class Arena:
    def __init__(self, ap):
        self.base = ap
        self.off = 0

    def reset(self):
        self.off = 0

    def alloc(self, shape, dt, name=""):
        esz = 4 if dt == F32 else 2
        n = int(np.prod(shape[1:]))
        nbytes = (n * esz + 31) // 32 * 32
        assert self.off + nbytes <= ARENA_BYTES, (name, self.off, nbytes)
        a = self.base[:, self.off // 2: self.off // 2 + n * esz // 2]
        self.off += nbytes
        if dt != BF16:
            a = a.bitcast(dt)
        if len(shape) == 3:
            a = a.rearrange("p (a b) -> p a b", a=shape[1])
        elif len(shape) == 4:
            a = a.rearrange("p (a b c) -> p a b c", a=shape[1], b=shape[2])
        if shape[0] != 128:
            a = a[0:shape[0]]
        return Buf(a, name)


def build_program(debug=None):
    nc = bass.Bass("TRN2", target_bir_lowering=False)
    dbg = set(debug or [])

    def din(name, shape, dt=F32):
        return nc.dram_tensor(name, list(shape), dt, kind="ExternalInput").ap()

    def dscr(name, shape, dt):
        kind = "ExternalOutput" if name in dbg else "Internal"
        return nc.dram_tensor(name, list(shape), dt, kind=kind).ap()

    xin = din("xin", [R_ALL, D])
    cvec = din("cvec", [2, D])
    ada_w = din("ada_w", [2, D, 6 * D])
    ada_b = din("ada_b", [2, 6 * D])
    norm_mix_g = din("norm_mix_g", [2, D])
    norm_mlp_g = din("norm_mlp_g", [2, D])
    mlp_w1 = din("mlp_w1", [2, D, DFF])
    mlp_w2 = din("mlp_w2", [2, DFF, D])
    ret_w_in = din("ret_w_in", [D, 12288])
    ret_w_out = din("ret_w_out", [4096, D])
    dec = din("dec", [1, 16])
    attn_w_in = din("attn_w_in", [D, 3072])
    attn_w_out = din("attn_w_out", [D, D])
    attn_sink = din("attn_sink", [1, 16])
    final_g = din("final_g", [1, D])
    consts = din("consts", [128, 1024 + 8])
    rope_r = din("rope_r", [R_ALL, 1024])
    rope_a = din("rope_a", [R_FULL, 1024])
    amask = din("amask", [128, 768])
    out = nc.dram_tensor("out", [2048, D], F32, kind="ExternalOutput").ap()

    mod_d = dscr("mod_d", [2, 2, 6 * D], F32)
    qT_d = dscr("qT_d", [16, 128, R_FULL], BF16)
    kT_d = dscr("kT_d", [16, 128, R_FULL], BF16)
    k_d = dscr("k_d", [R_ALL, 2048], BF16)
    v_d = dscr("v_d", [R_ALL, 4096], BF16)
    sg_d = dscr("sg_d", [R_FULL, 4096], BF16)
    yT_d = dscr("yT_d", [32, 128, R_FULL], BF16)
    x1_d = dscr("x1_d", [R_FULL, D], F32)
    x2_d = dscr("x2_d", [R_FULL, D], F32)
    aq_d = dscr("aq_d", [16, 128, R_FULL], BF16)
    akT_d = dscr("akT_d", [4, 128, R_FULL], BF16)
    av_d = dscr("av_d", [R_FULL, 512], BF16)
    oT_d = dscr("oT_d", [16, 128, 2048], BF16)
    x3_d = dscr("x3_d", [2048, D], F32)
    x4_d = dscr("x4_d", [2048, D], F32)

    S = Sched()

    with ExitStack() as es:
        arena_t = es.enter_context(nc.sbuf_tensor("arena", [128, ARENA_BYTES // 2], BF16))
        psA = es.enter_context(nc.psum_tensor("psA", [128, 8, 512], F32))
        eng_sems = {e: es.enter_context(nc.semaphore(f"sem_{e}")) for e in Sched.ENGS}
        dma_sems = {e: [es.enter_context(nc.semaphore(f"dsem_{e}{i}")) for i in range(NS_DMA)]
                    for e in ("sync", "gpsimd")}
        A = Arena(arena_t)
        PS = [Buf(psA[:, i, :], f"ps{i}") for i in range(8)]
        ps_bf = [psA[:, i, :].bitcast(BF16) for i in range(8)]

        DRAMB = {}

        def dbuf(name):
            if name not in DRAMB:
                DRAMB[name] = Buf(None, name)
            return DRAMB[name]

        rr = {"ps": 0}

        def next_ps(lo=0, hi=4):
            i = lo + rr.setdefault((lo, hi), 0) % (hi - lo)
            rr[(lo, hi)] += 1
            return i

        P_ident = A.alloc([128, 128], F32, "ident")
        P_identb = A.alloc([128, 128], BF16, "identb")
        P_c = A.alloc([128, 1024 + 8], F32, "consts")
        P_modF = A.alloc([128, 2, 2, 96], F32, "modF")
        P_gF = A.alloc([128, 2, 2, 16], F32, "gF")
        P_AB = A.alloc([128, 4, 2, 32], F32, "AB")
        P_dec = A.alloc([128, 16], F32, "dec")
        P_lg = A.alloc([128, 16], F32, "lg")
        P_sink = A.alloc([128, 16], F32, "sink")
        P_eps = A.alloc([128, 1], F32, "eps")
        persist_off = A.off

        S.dma("sync", P_c.ap, consts, writes=[P_c])
        S.dma("sync", P_ident.ap, consts[:, 0:128], writes=[P_ident])
        S.dma("sync", P_dec.ap, dec.partition_broadcast(128), writes=[P_dec])
        S.dma("sync", P_sink.ap, attn_sink.partition_broadcast(128), writes=[P_sink])
        S.op("vector", lambda e: e.tensor_copy(out=P_identb.ap, in_=P_ident.ap), [P_ident], [P_identb])
        S.op("vector", lambda e: e.memset(P_eps.ap, EPS), [], [P_eps])
        for l in range(2):
            S.dma("sync", P_gF.ap[:, l, 0, :], norm_mix_g[l].rearrange("(c p) -> p c", p=128), writes=[P_gF], slow=True)
            S.dma("sync", P_gF.ap[:, l, 1, :], norm_mlp_g[l].rearrange("(c p) -> p c", p=128), writes=[P_gF], slow=True)

        def cst(lo, hi):
            return P_c.ap[:, lo:hi]

        def phase_ada():
            A.off = persist_off
            cT = A.alloc([128, 16, 2], F32, "cT")
            cTb = A.alloc([128, 16, 2], BF16, "cTb")
            brow = A.alloc([2, 6 * D], F32, "brow")
            mrow = A.alloc([2, 6 * D], F32, "mrow")
            wb = [A.alloc([128, 16, 512], BF16, f"adaw{i}") for i in range(4)]
            for r in range(2):
                S.dma("sync", cT.ap[:, :, r], cvec[r].rearrange("(c p) -> p c", p=128), writes=[cT], slow=True)
            S.op("scalar", lambda e: e.activation(out=cTb.ap, in_=cT.ap, func=AF.Silu), [cT], [cTb])
            asteps = [(l, nb) for l in range(2) for nb in range(24)]
            st = {"p": 0}

            def aload(upto):
                while st["p"] <= min(upto, len(asteps) - 1):
                    l2, nb2 = asteps[st["p"]]
                    w2 = wb[st["p"] % 4]
                    S.dma("gpsimd", w2.ap, ada_w[l2, :, nb2 * 512:(nb2 + 1) * 512].rearrange("(c p) n -> p c n", p=128),
                          writes=[w2])
                    st["p"] += 1

            for l in range(2):
                S.dma("sync", brow.ap, ada_b[l:l + 1, :].partition_broadcast(2), writes=[brow])
                for nb in range(24):
                    i = l * 24 + nb
                    aload(i + 3)
                    w = wb[i % 4]
                    pi = next_ps(0, 4)
                    for kc in range(16):
                        S.op("tensor", lambda e, pi=pi, kc=kc, w=w: e.matmul(
                            PS[pi].ap[0:2, :], lhsT=cTb.ap[:, kc, :], rhs=w.ap[:, kc, :],
                            start=(kc == 0), stop=(kc == 15)), [cTb, w], [PS[pi]])
                    S.op("vector", lambda e, pi=pi, nb=nb: e.tensor_tensor(
                        out=mrow.ap[:, nb * 512:(nb + 1) * 512], in0=PS[pi].ap[0:2, :],
                        in1=brow.ap[:, nb * 512:(nb + 1) * 512], op=ALU.add), [PS[pi], brow], [mrow])
                S.dma("sync", mod_d[l], mrow.ap, reads=[mrow], writes=[dbuf("mod_d")])
            for l in range(2):
                for r in range(2):
                    S.dma("sync", P_modF.ap[:, l, r, :], mod_d[l, r].rearrange("(c p) -> p c", p=128),
                          reads=[dbuf("mod_d")], writes=[P_modF], slow=True)
            for l in range(2):
                for sub in range(2):
                    sh0 = 0 if sub == 0 else 48
                    sc0 = sh0 + 16
                    for r in range(2):
                        S.op("vector", lambda e, l=l, sub=sub, r=r, sc0=sc0: e.scalar_tensor_tensor(
                            out=P_AB.ap[:, l * 2 + sub, r, 0:16], in0=P_modF.ap[:, l, r, sc0:sc0 + 16], scalar=1.0,
                            in1=P_gF.ap[:, l, sub, :], op0=ALU.add, op1=ALU.mult), [P_modF, P_gF], [P_AB])
                        S.op("vector", lambda e, l=l, sub=sub, r=r, sh0=sh0: e.tensor_copy(
                            out=P_AB.ap[:, l * 2 + sub, r, 16:32], in_=P_modF.ap[:, l, r, sh0:sh0 + 16]),
                            [P_modF], [P_AB])
            S.op("scalar", lambda e: e.activation(out=P_lg.ap, in_=P_dec.ap, func=AF.Exp, scale=-1.0), [P_dec], [P_lg])
            S.op("scalar", lambda e: e.activation(out=P_lg.ap, in_=P_lg.ap, func=AF.Ln, bias=1.0), [P_lg], [P_lg])
            S.op("vector", lambda e: e.tensor_scalar(out=P_lg.ap, in0=P_lg.ap, scalar1=-1.0, scalar2=None,
                                                     op0=ALU.mult), [P_lg], [P_lg])
            S.barrier()

        def norm_pre(xt, work):
            junk, ssq, xn = work["junk"], work["ssq"], work["xn"]
            S.op("scalar", lambda e: e.activation(out=junk.ap, in_=xt.ap, func=AF.Square,
                                                  scale=float(D ** -0.5), accum_out=ssq.ap), [xt], [xn, ssq])
            S.op("scalar", lambda e: e.activation(out=ssq.ap, in_=ssq.ap, func=AF.Sqrt, bias=P_eps.ap[:, 0:1]),
                 [ssq, P_eps], [ssq])
            S.op("vector", lambda e: e.reciprocal(out=ssq.ap, in_=ssq.ap), [ssq], [ssq])
            S.op("scalar", lambda e: e.activation(out=xn.ap, in_=xt.ap, func=AF.Copy, scale=ssq.ap[:, 0:1]),
                 [xt, ssq], [xn])

        def norm_tr(row, lsub, aTg, gi, work):
            xn = work["xn"]
            for b4 in range(4):
                pi = next_ps(*GCTX.get("tr_banks", (4, 8)))
                for j in range(4):
                    kc = b4 * 4 + j
                    S.op("tensor", lambda e, pi=pi, j=j, kc=kc: e.transpose(
                        PS[pi].ap[:, j * 128:(j + 1) * 128], xn.ap[:, kc * 128:(kc + 1) * 128], P_ident.ap),
                        [xn, P_ident], [PS[pi]])
                for j in range(4):
                    kc = b4 * 4 + j
                    S.op("vector", lambda e, pi=pi, j=j, kc=kc: e.tensor_scalar(
                        out=aTg.ap[:, kc, gi * 128:(gi + 1) * 128], in0=PS[pi].ap[:, j * 128:(j + 1) * 128],
                        scalar1=P_AB.ap[:, lsub, row, kc:kc + 1], scalar2=P_AB.ap[:, lsub, row, 16 + kc:17 + kc],
                        op0=ALU.mult, op1=ALU.add), [PS[pi], P_AB], [aTg])

        def norm_mod_T(xt, row, lsub, aTg, gi, work):
            norm_pre(xt, work)
            norm_tr(row, lsub, aTg, gi, work)

        def alloc_norm_work():
            xn = A.alloc([128, D], F32, "xn")
            return {"junk": xn, "ssq": A.alloc([128, 1], F32, "ssq"), "xn": xn}

        converted = set()
        GCTX = {}

        def wscratch(name, K, N, kp, ncols):
            nkq = K // 128 // kp
            nbi = N // ncols
            t = nc.dram_tensor("ws_" + name, [nkq, nbi, 128, kp * ncols], BF16, kind="Internal").ap()
            return {"ap": t, "kp": kp, "ncols": ncols, "name": name}

        def wload(ws, Wv, kq, bi, w):
            key = (ws["name"], kq, bi)
            kp, ncols = ws["kp"], ws["ncols"]
            sc = ws["ap"][kq, bi].rearrange("p (c n) -> p c n", c=kp)
            if key in converted:
                S.dma("sync", w.ap, sc, reads=[dbuf(key)], writes=[w])
            else:
                S.dma("gpsimd", w.ap, Wv[:, kq * kp:(kq + 1) * kp, bi * ncols:(bi + 1) * ncols], writes=[w])
                S.dma("gpsimd", sc, w.ap, reads=[w], writes=[dbuf(key)])
                converted.add(key)

        def wconvert(ws, Wv, stage, nhalf=1):
            kp, ncols = ws["kp"], ws["ncols"]
            nkq, nbi = ws["ap"].shape[0], ws["ap"].shape[1]
            hk = kp // nhalf
            i = 0
            for kq in range(nkq):
                for bi in range(nbi):
                    key = (ws["name"], kq, bi)
                    if key in converted:
                        continue
                    sc = ws["ap"][kq, bi].rearrange("p (c n) -> p c n", c=kp)
                    for hf in range(nhalf):
                        st = stage[i % len(stage)]
                        i += 1
                        sv = st.ap.rearrange("p a b -> p (a b)")[:, 0:hk * ncols].rearrange("p (c n) -> p c n", c=hk)
                        S.dma("gpsimd", sv,
                              Wv[:, kq * kp + hf * hk:kq * kp + (hf + 1) * hk, bi * ncols:(bi + 1) * ncols],
                              writes=[st])
                        S.dma("gpsimd", sc[:, hf * hk:(hf + 1) * hk, :], sv, reads=[st],
                              writes=[dbuf(key)])
                    converted.add(key)

        class WStream:
            def __init__(self, ws, W, wbufs, steps):
                self.ws, self.wbufs, self.steps = ws, wbufs, steps
                self.Wv = W.rearrange("(c p) n -> p c n", p=128)
                self.planned = 0
                self.pf = len(wbufs) - 1

            def get(self, i):
                upto = min(i + self.pf, len(self.steps) - 1)
                while self.planned <= upto:
                    kq, bi = self.steps[self.planned]
                    wload(self.ws, self.Wv, kq, bi, self.wbufs[self.planned % len(self.wbufs)])
                    self.planned += 1
                return self.wbufs[i % len(self.wbufs)]

        EARLY = set()

        def gemm_tok(groups, prep, ws, W, kcn, blocks, evac, wbufs, aTgs, post=None, kparts=1, pre=None,
                     prep_main=None, early_blocks=1, pops=1):
            kp = kcn // kparts
            steps = []
            for g, grp in enumerate(groups):
                for bi, (c0, ncols) in enumerate(blocks(grp)):
                    for kq in range(kparts):
                        steps.append((g, bi, c0, ncols, kq))
            stream = WStream(ws, W, wbufs, [(kq, c0 // ncols) for (g, bi, c0, ncols, kq) in steps])
            pend = []
            GCTX["defer"] = lambda fn: pend.append([0, fn])
            prepped = set()
            nblk = {}
            bgq = []
            for (g, bi, c0, ncols, kq) in steps:
                nblk[g] = max(nblk.get(g, 0), bi + 1)
            for i, (g, bi, c0, ncols, kq) in enumerate(steps):
                grp = groups[g]
                aTg = aTgs[g % len(aTgs)]
                w = stream.get(i)
                if bi == 0 and kq == 0:
                    GCTX["tr_banks"] = (4, 8)
                    if g not in prepped:
                        bgq.extend(prep(grp, aTg))
                        prepped.add(g)
                    while bgq:
                        bgq.pop(0)()
                    if prep_main is not None:
                        prep_main(grp, aTg)
                if (bi == max(nblk[g] - early_blocks, 0) and kq == 0 and g + 1 < len(groups)
                        and (g + 1) not in prepped and (len(aTgs) > 1 or prep_main is not None)):
                    bgq.extend(prep(groups[g + 1], aTgs[(g + 1) % len(aTgs)]))
                    prepped.add(g + 1)
                    if kparts > 1:
                        GCTX["tr_banks"] = (0, 4) if bi % 2 == 1 else (4, 8)
                if kq == 0 and pre is not None:
                    pre(grp, bi, c0, ncols)
                for gi, t in enumerate(grp):
                    if kparts == 1:
                        pi = next_ps(0, 4)
                    else:
                        pi = (bi % 2) * 4 + gi
                    for k2 in range(kp):
                        kc = kq * kp + k2
                        S.op("tensor", lambda e, pi=pi, kc=kc, k2=k2, w=w, aTg=aTg, gi=gi, ncols=ncols: e.matmul(
                            PS[pi].ap[:, 0:ncols], lhsT=aTg.ap[:, kc, gi * 128:(gi + 1) * 128],
                            rhs=w.ap[:, k2, 0:ncols], start=(kc == 0), stop=(kc == kcn - 1)),
                            [aTg, w], [PS[pi]])
                    for p in pend:
                        p[0] += 1
                    while pend and pend[0][0] >= 2:
                        pend.pop(0)[1]()
                    for _ in range(pops):
                        if bgq:
                            bgq.pop(0)()
                    if kq == kparts - 1:
                        evac(t, gi, bi, c0, ncols, pi)
                last = (i + 1 == len(steps)) or steps[i + 1][0] != g
                if last:
                    while pend:
                        pend.pop(0)[1]()
                    if post is not None:
                        post(grp)

        WS = {
            "ret_w_in": wscratch("ret_w_in", D, 12288, 16, 512),
            "ret_w_out": wscratch("ret_w_out", 4096, D, 16, 512),
            "mlp_w1_0": wscratch("mlp_w1_0", D, DFF, 16, 256),
            "mlp_w2_0": wscratch("mlp_w2_0", DFF, D, 16, 512),
            "mlp_w1_1": wscratch("mlp_w1_1", D, DFF, 16, 256),
            "mlp_w2_1": wscratch("mlp_w2_1", DFF, D, 16, 512),
            "attn_w_in": wscratch("attn_w_in", D, 3072, 16, 512),
            "attn_w_out": wscratch("attn_w_out", D, D, 16, 512),
        }

        def wview(W):
            return W.rearrange("(c p) n -> p c n", p=128)

        def rope_block(xs, tab, gi_unused, outb, B, work):
            t1, t2 = work["t1"], work["t2"]
            n = 512 // (2 * B)
            xv = xs.ap.rearrange("p (n two b) -> p n two b", two=2, b=B)
            sv = tab.ap[:, 512:1024].rearrange("p (n two b) -> p n two b", two=2, b=B)
            t2v = t2.ap.rearrange("p (n two b) -> p n two b", two=2, b=B)
            S.op("vector", lambda e: e.tensor_tensor(out=t1.ap, in0=xs.ap, in1=tab.ap[:, 0:512], op=ALU.mult),
                 [xs, tab], [t1])
            S.op("vector", lambda e: e.tensor_tensor(out=t2v[:, :, 0, :], in0=xv[:, :, 1, :], in1=sv[:, :, 0, :],
                                                     op=ALU.mult), [xs, tab], [t2])
            S.op("vector", lambda e: e.tensor_tensor(out=t2v[:, :, 1, :], in0=xv[:, :, 0, :], in1=sv[:, :, 1, :],
                                                     op=ALU.mult), [xs, tab], [t2])
            S.op("vector", lambda e: e.tensor_tensor(out=outb[1], in0=t1.ap, in1=t2.ap, op=ALU.add),
                 [t1, t2], [outb[0]])

        def phase_ret_proj():
            A.off = persist_off
            wk = alloc_norm_work()
            xts = [A.alloc([128, D], F32, f"xt{i}") for i in range(2)]
            aTgs = [A.alloc([128, 16, 512], BF16, f"aTg{i}") for i in range(2)]
            wbufs = [A.alloc([128, 16, 512], BF16, f"w{i}") for i in range(3)]
            tabs = [A.alloc([128, 1024], F32, f"tab{i}") for i in range(4)]
            xs_b = [A.alloc([128, 512], F32, f"xs{i}") for i in range(2)]
            rw = {"t1": A.alloc([128, 512], F32, "t1"), "t2": A.alloc([128, 512], F32, "t2")}
            ob = [A.alloc([128, 512], BF16, f"ob{i}") for i in range(6)]
            qTg = A.alloc([128, 16, 512], BF16, "qTg")
            kTg = A.alloc([128, 16, 512], BF16, "kTg")
            cnt = {"x": 0, "xs": 0, "ob": 0}
            full_groups = [[0, 1, 2, 3], [4, 5, 6, 7], [8, 9, 10, 11], [12, 13, 14, 15], [16, 17, 18]]
            far_groups = [[19, 20, 21, 22], [23, 24, 25, 26], [27, 28, 29, 30], [31, 32, 33]]
            groups = full_groups + far_groups
            tabmap = {}

            def prep(grp, aTg):
                tasks = []
                for gi, t in enumerate(grp):
                    def t1(gi=gi, t=t):
                        xt = xts[cnt["x"] % 2]
                        cnt["x"] += 1
                        S.dma("sync", xt.ap, xin[t * 128:(t + 1) * 128, :], writes=[xt])
                        tb = tabs[gi]
                        S.dma("sync", tb.ap, rope_r[t * 128:(t + 1) * 128, :], writes=[tb])
                        tabmap[t] = tb
                        norm_pre(xt, wk)

                    def t2(gi=gi, t=t):
                        norm_tr(1 if t < 2 else 0, 0, aTg, gi, wk)
                    tasks += [t1, t2]
                return tasks

            def blocks(grp):
                if grp[0] >= NT_FULL:
                    return [(2048 + i * 512, 512) for i in range(4)] + [(4096 + i * 512, 512) for i in range(8)]
                return [(i * 512, 512) for i in range(24)]

            def evac(t, gi, bi, c0, ncols, pi):
                full = t < NT_FULL
                r0 = t * 128
                if c0 < 4096:
                    isq = c0 < 2048
                    xs = xs_b[cnt["xs"] % 2]
                    cnt["xs"] += 1
                    S.op("scalar", lambda e: e.activation(out=xs.ap, in_=PS[pi].ap, func=AF.Copy,
                                                          scale=1.0 if isq else 0.0625), [PS[pi]], [xs])
                    o = ob[cnt["ob"] % 6]
                    cnt["ob"] += 1
                    rope_block(xs, tabmap[t], gi, (o, o.ap), 128, rw)
                    cb = (c0 % 2048) // 128
                    if not isq:
                        S.dma("sync", k_d[r0:r0 + 128, c0 - 2048:c0 - 2048 + 512], o.ap, reads=[o],
                              writes=[dbuf("k_d")])
                    if full:
                        tg = qTg if isq else kTg

                        def tr(o=o, tg=tg, cb=cb, gi=gi):
                            pj = next_ps(4, 8)
                            for j in range(4):
                                S.op("tensor", lambda e, j=j, pj=pj, o=o: e.transpose(
                                    ps_bf[pj][:, j * 128:(j + 1) * 128], o.ap[:, j * 128:(j + 1) * 128], P_identb.ap),
                                    [o, P_identb], [PS[pj]])
                            S.op("scalar", lambda e, pj=pj, tg=tg, cb=cb, gi=gi: e.activation(
                                out=tg.ap[:, cb:cb + 4, gi * 128:(gi + 1) * 128],
                                in_=ps_bf[pj][:, 0:512].rearrange("p (c t) -> p c t", c=4), func=AF.Copy),
                                [PS[pj]], [tg])
                        GCTX["defer"](tr)
                elif c0 < 8192:
                    o = ob[cnt["ob"] % 6]
                    cnt["ob"] += 1
                    S.op("scalar", lambda e: e.activation(out=o.ap, in_=PS[pi].ap, func=AF.Copy), [PS[pi]], [o])
                    S.dma("sync", v_d[r0:r0 + 128, c0 - 4096:c0 - 4096 + 512], o.ap, reads=[o], writes=[dbuf("v_d")])
                else:
                    o = ob[cnt["ob"] % 6]
                    cnt["ob"] += 1
                    S.op("scalar", lambda e: e.activation(out=o.ap, in_=PS[pi].ap, func=AF.Silu), [PS[pi]], [o])
                    S.dma("sync", sg_d[r0:r0 + 128, c0 - 8192:c0 - 8192 + 512], o.ap, reads=[o],
                          writes=[dbuf("sg_d")])

            def post(grp):
                if grp[0] >= NT_FULL:
                    return
                r0 = grp[0] * 128
                n = len(grp) * 128
                S.dma("sync", qT_d[:, :, r0:r0 + n].rearrange("c p t -> p c t"), qTg.ap[:, :, 0:n], reads=[qTg],
                      writes=[dbuf("qT_d")])
                S.dma("sync", kT_d[:, :, r0:r0 + n].rearrange("c p t -> p c t"), kTg.ap[:, :, 0:n], reads=[kTg],
                      writes=[dbuf("kT_d")])

            gemm_tok(groups, prep, WS["ret_w_in"], ret_w_in, 16, blocks, evac, wbufs, aTgs, post, early_blocks=2)
            S.barrier()

        def phase_ret():
            A.off = persist_off
            qT = A.alloc([128, 2, R_FULL], BF16, "qT")
            kT = A.alloc([128, 2, R_FULL], BF16, "kT")
            kk = A.alloc([128, NT_ALL, 256], BF16, "kk")
            vv = A.alloc([128, NT_ALL, 512], BF16, "vv")
            sg = A.alloc([128, NT_FULL, 512], BF16, "sg")
            snap = A.alloc([128, NT_FULL, 1024], BF16, "snap")
            yT = A.alloc([128, 4, R_FULL], BF16, "yT")
            Sst = [[A.alloc([128, 512], F32, f"S{i}{dc}") for dc in range(2)] for i in range(2)]
            Sbf = [[[A.alloc([128, 512], BF16, f"Sbf{i}{p}{dc}") for dc in range(2)] for p in range(2)]
                   for i in range(2)]
            maskc = A.alloc([128, 128], F32, "maskc")
            mtmp = A.alloc([128, 128], F32, "mtmp")
            dq = [A.alloc([128, 128], F32, f"dq{i}") for i in range(2)]
            dsc = A.alloc([128, 4], F32, "dsc")
            qs = [A.alloc([128, 2, 128], BF16, f"qs{i}") for i in range(4)]
            ks = [A.alloc([128, 256], BF16, f"ks{i}") for i in range(5)]
            pT = [A.alloc([128, 128], BF16, f"pT{i}") for i in range(2)]
            qsc = [A.alloc([128, 2, R_FULL], BF16, f"qsc{i}") for i in range(2)]
            yb = [A.alloc([128, 512], BF16, f"yb{i}") for i in range(2)]
            junk = A.alloc([128, 512], BF16, "junkr")
            ssq = [A.alloc([128, 1], F32, f"ssqr{i}") for i in range(2)]
            cn = {"qs": 0, "ks": 0, "pT": 0, "ow": 0}
            k_v = k_d.rearrange("(t p) c -> p t c", p=128)
            v_v = v_d.rearrange("(t p) c -> p t c", p=128)
            sg_v = sg_d.rearrange("(t p) c -> p t c", p=128)
            stage = [A.alloc([128, 4, 512], BF16, f"stg{i}") for i in range(2)]
            wconvert(WS["ret_w_out"], wview(ret_w_out), stage, nhalf=4)
            wconvert(WS["mlp_w1_0"], wview(mlp_w1[0]), stage, nhalf=2)
            wconvert(WS["mlp_w2_0"], wview(mlp_w2[0]), stage, nhalf=4)

            par = {0: 0, 1: 0}
            kvp = {}

            def plan_kv(t, di, banks=(0, 4)):
                kb = ks[cn["ks"] % 5]
                cn["ks"] += 1
                S.op("vector", lambda e: e.tensor_scalar(out=kb.ap, in0=kk.ap[:, t, :], scalar1=dsc.ap[:, di:di + 1],
                                                         scalar2=None, op0=ALU.mult), [kk, dsc], [kb])
                pis = []
                for dc in range(2):
                    pi = next_ps(*banks)
                    pis.append(pi)
                    S.op("tensor", lambda e, pi=pi, dc=dc: e.matmul(
                        PS[pi].ap, lhsT=kb.ap[:, dc * 128:(dc + 1) * 128], rhs=vv.ap[:, t, :], start=True, stop=True),
                        [kb, vv], [PS[pi]])
                kvp[(t, di)] = pis

            def state_update(t, di, dst):
                pis = kvp.pop((t, di))
                for dc in range(2):
                    pi = pis[dc]
                    S.op("vector", lambda e, pi=pi, dc=dc: e.scalar_tensor_tensor(
                        out=Sst[di][dc].ap, in0=Sst[di][dc].ap, scalar=dsc.ap[:, 2 + di:3 + di],
                        in1=PS[pi].ap, op0=ALU.mult, op1=ALU.add), [Sst[di][dc], dsc, PS[pi]], [Sst[di][dc]])
                    if dst is not None:
                        S.op("scalar", lambda e, dc=dc: e.activation(out=dst[1][dc], in_=Sst[di][dc].ap,
                                                                     func=AF.Copy), [Sst[di][dc]], [dst[0][dc]])

            class _QV:
                def __init__(self, buf, ap):
                    self.buf, self.ap = buf, ap

            def q_scaled(t, di):
                return _QV(qsc[di], qsc[di].ap[:, :, t * 128:(t + 1) * 128])

            for h in range(8):
                S.dma("sync", qT.ap, qT_d[2 * h:2 * h + 2].rearrange("c p t -> p c t"), reads=[dbuf("qT_d")],
                      writes=[qT])
                S.dma("sync", kT.ap, kT_d[2 * h:2 * h + 2].rearrange("c p t -> p c t"), reads=[dbuf("kT_d")],
                      writes=[kT])
                for t0 in range(0, NT_ALL, 9):
                    t1 = min(NT_ALL, t0 + 9)
                    S.dma("sync", kk.ap[:, t0:t1, :], k_v[:, t0:t1, h * 256:(h + 1) * 256], reads=[dbuf("k_d")],
                          writes=[kk])
                    S.dma("sync", vv.ap[:, t0:t1, :], v_v[:, t0:t1, h * 512:(h + 1) * 512], reads=[dbuf("v_d")],
                          writes=[vv])
                for t0 in range(0, NT_FULL, 10):
                    t1 = min(NT_FULL, t0 + 10)
                    S.dma("sync", sg.ap[:, t0:t1, :], sg_v[:, t0:t1, h * 512:(h + 1) * 512], reads=[dbuf("sg_d")],
                          writes=[sg])
                lgf = P_lg.ap[:, h:h + 1]
                lgb = P_lg.ap[:, 8 + h:9 + h]
                S.op("scalar", lambda e, lgf=lgf: e.activation(out=maskc.ap, in_=cst(128, 256), func=AF.Exp, scale=lgf),
                     [P_c, P_lg], [maskc])
                S.op("vector", lambda e: e.tensor_tensor(out=maskc.ap, in0=maskc.ap, in1=cst(256, 384), op=ALU.mult),
                     [maskc, P_c], [maskc])
                S.op("scalar", lambda e, lgb=lgb: e.activation(out=mtmp.ap, in_=cst(384, 512), func=AF.Exp, scale=lgb),
                     [P_c, P_lg], [mtmp])
                S.op("vector", lambda e: e.tensor_tensor(out=mtmp.ap, in0=mtmp.ap, in1=cst(512, 640), op=ALU.mult),
                     [mtmp, P_c], [mtmp])
                S.op("vector", lambda e: e.tensor_tensor(out=maskc.ap, in0=maskc.ap, in1=mtmp.ap, op=ALU.add),
                     [maskc, mtmp], [maskc])
                S.op("scalar", lambda e, lgf=lgf: e.activation(out=dq[0].ap, in_=cst(640, 768), func=AF.Exp, scale=lgf),
                     [P_c, P_lg], [dq[0]])
                S.op("scalar", lambda e, lgb=lgb: e.activation(out=dq[1].ap, in_=cst(768, 896), func=AF.Exp, scale=lgb),
                     [P_c, P_lg], [dq[1]])
                S.op("scalar", lambda e, lgf=lgf: e.activation(out=dsc.ap[:, 0:1], in_=cst(1024, 1025), func=AF.Exp,
                                                               scale=lgf), [P_c, P_lg], [dsc])
                S.op("scalar", lambda e, lgb=lgb: e.activation(out=dsc.ap[:, 1:2], in_=cst(1025, 1026), func=AF.Exp,
                                                               scale=lgb), [P_c, P_lg], [dsc])
                S.op("scalar", lambda e, lgf=lgf: e.activation(out=dsc.ap[:, 2:3], in_=cst(1026, 1027), func=AF.Exp,
                                                               scale=lgf), [P_c, P_lg], [dsc])
                S.op("scalar", lambda e, lgb=lgb: e.activation(out=dsc.ap[:, 3:4], in_=cst(1026, 1027), func=AF.Exp,
                                                               scale=lgb), [P_c, P_lg], [dsc])
                for di in range(2):
                    for dc in range(2):
                        S.op("vector", lambda e, di=di, dc=dc: e.memset(Sst[di][dc].ap, 0.0), [], [Sst[di][dc]])
                for dc in range(2):
                    sb0 = Sbf[0][par[0]][dc]
                    S.op("vector", lambda e, sb0=sb0: e.memset(sb0.ap, 0.0), [], [sb0])
                for di in range(2):
                    S.op("vector", lambda e, di=di: e.tensor_tensor(
                        out=qsc[di].ap.rearrange("p c (t i) -> p (c t) i", i=128),
                        in0=qT.ap.rearrange("p c (t i) -> p (c t) i", i=128),
                        in1=dq[di].ap.unsqueeze(1).to_broadcast([128, 2 * NT_FULL, 128]), op=ALU.mult),
                        [qT, dq[di]], [qsc[di]])
                seq = [1, 0] + list(range(NT_ALL - 1, 1, -1))
                S.op("vector", lambda e: e.memset(snap.ap[:, seq[0], :], 0.0), [], [snap])
                plan_kv(seq[0], 1, (0, 6))
                plan_kv(seq[1], 1, (0, 6))
                for idx, t in enumerate(seq[:-1]):
                    nt = seq[idx + 1]
                    dst = None
                    if nt < NT_FULL:
                        dst = ([snap, snap], [snap.ap[:, nt, 0:512], snap.ap[:, nt, 512:1024]])
                    if idx + 2 < len(seq) - 1:
                        plan_kv(seq[idx + 2], 1, (0, 6))
                    state_update(t, 1, dst)
                pend = []
                plan_kv(0, 0)
                for t in range(NT_FULL):
                    pi = next_ps(4, 7)
                    for dc in range(2):
                        S.op("tensor", lambda e, pi=pi, dc=dc, t=t: e.matmul(
                            PS[pi].ap[:, 0:128], lhsT=kT.ap[:, dc, t * 128:(t + 1) * 128],
                            rhs=qT.ap[:, dc, t * 128:(t + 1) * 128], start=(dc == 0), stop=(dc == 1)),
                            [kT, qT], [PS[pi]])
                    pb = pT[cn["pT"] % 2]
                    cn["pT"] += 1
                    S.op("vector", lambda e, pi=pi, pb=pb: e.tensor_tensor(out=pb.ap, in0=PS[pi].ap[:, 0:128],
                                                                           in1=maskc.ap, op=ALU.mult),
                         [PS[pi], maskc], [pb])
                    qb = q_scaled(t, 0)
                    qbb = q_scaled(t, 1)
                    po = next_ps(4, 7)
                    sb = Sbf[0][par[0]]
                    S.op("tensor", lambda e, po=po, pb=pb, t=t: e.matmul(PS[po].ap, lhsT=pb.ap, rhs=vv.ap[:, t, :],
                                                                         start=True, stop=False), [pb, vv], [PS[po]])
                    for dc in range(2):
                        S.op("tensor", lambda e, po=po, dc=dc, qb=qb, sb=sb: e.matmul(
                            PS[po].ap, lhsT=qb.ap[:, dc, :], rhs=sb[dc].ap, start=False, stop=False),
                            [qb.buf, sb[dc]], [PS[po]])
                    for dc in range(2):
                        S.op("tensor", lambda e, po=po, dc=dc, qbb=qbb, t=t: e.matmul(
                            PS[po].ap, lhsT=qbb.ap[:, dc, :], rhs=snap.ap[:, t, dc * 512:(dc + 1) * 512], start=False,
                            stop=(dc == 1)), [qbb.buf, snap], [PS[po]])
                    if t != NT_FULL - 1:
                        np_ = 1 - par[0]
                        state_update(t, 0, (Sbf[0][np_], [Sbf[0][np_][0].ap, Sbf[0][np_][1].ap]))
                        par[0] = np_
                        if t + 1 != NT_FULL - 1:
                            plan_kv(t + 1, 0)
                    y = yb[cn["ow"] % 2]
                    sq = ssq[cn["ow"] % 2]
                    cn["ow"] += 1
                    o = PS[po]
                    S.op("scalar", lambda e, o=o, sq=sq: e.activation(out=junk.ap, in_=o.ap, func=AF.Square,
                                                                      scale=float(512 ** -0.5), accum_out=sq.ap),
                         [o], [junk, sq])
                    S.op("scalar", lambda e, sq=sq: e.activation(out=sq.ap, in_=sq.ap, func=AF.Sqrt,
                                                                 bias=P_eps.ap[:, 0:1]), [sq, P_eps], [sq])
                    while pend:
                        pend.pop(0)()

                    def tr(y=y, t=t, o=o, sq=sq):
                        S.op("vector", lambda e: e.reciprocal(out=sq.ap, in_=sq.ap), [sq], [sq])
                        S.op("vector", lambda e: e.scalar_tensor_tensor(
                            out=y.ap, in0=o.ap, scalar=sq.ap[:, 0:1], in1=sg.ap[:, t, :], op0=ALU.mult, op1=ALU.mult),
                            [o, sq, sg], [y])
                        pj = next_ps(7, 8)
                        for j in range(4):
                            S.op("tensor", lambda e, j=j, pj=pj, y=y: e.transpose(
                                ps_bf[pj][:, j * 128:(j + 1) * 128], y.ap[:, j * 128:(j + 1) * 128], P_identb.ap),
                                [y, P_identb], [PS[pj]])
                        S.op("scalar", lambda e, pj=pj, t=t: e.activation(
                            out=yT.ap[:, :, t * 128:(t + 1) * 128],
                            in_=ps_bf[pj][:, 0:512].rearrange("p (c t) -> p c t", c=4), func=AF.Copy), [PS[pj]], [yT])
                    pend.append(tr)
                while pend:
                    pend.pop(0)()
                S.dma("sync", yT_d[4 * h:4 * h + 4].rearrange("c p t -> p c t"), yT.ap, reads=[yT],
                      writes=[dbuf("yT_d")])
            S.barrier()

        def make_residual(layer, gcol0, x_src, x_dst, row_of, nxb=8):
            gb = [[A.alloc([128, 512], F32, f"gb{r}{i}") for i in range(2)] for r in range(2)]
            xb = [A.alloc([128, 512], F32, f"xb{i}") for i in range(nxb)]
            tmp = [A.alloc([128, 512], F32, f"rtmp{i}") for i in range(2)]
            st = {"gb": 0, "xb": 0, "tmp": 0, "cur": {}, "g": {}}

            def pre(grp, bi, c0, ncols):
                rows = sorted(set(row_of(t) for t in grp))
                for r in rows:
                    b = gb[r][st["gb"] % 2]
                    S.dma("sync", b.ap[:, 0:ncols],
                          mod_d[layer, r:r + 1, gcol0 + c0:gcol0 + c0 + ncols].partition_broadcast(128),
                          reads=[dbuf("mod_d")], writes=[b])
                    st["g"][r] = b
                st["gb"] += 1
                for gi, t in enumerate(grp):
                    b = xb[st["xb"] % nxb]
                    st["xb"] += 1
                    S.dma("sync", b.ap[:, 0:ncols], x_src(t)[:, c0:c0 + ncols], writes=[b])
                    st["cur"][gi] = b

            def evac(t, gi, bi, c0, ncols, pi):
                b = st["cur"][gi]
                g = st["g"][row_of(t)]
                tm = tmp[st["tmp"] % 2]
                st["tmp"] += 1
                S.op("vector", lambda e: e.tensor_tensor(out=tm.ap[:, 0:ncols], in0=PS[pi].ap[:, 0:ncols],
                                                         in1=g.ap[:, 0:ncols], op=ALU.mult), [PS[pi], g], [tm])
                S.op("vector", lambda e: e.tensor_tensor(out=b.ap[:, 0:ncols], in0=b.ap[:, 0:ncols],
                                                         in1=tm.ap[:, 0:ncols], op=ALU.add), [b, tm], [b])
                S.dma("sync", x_dst(t)[:, c0:c0 + ncols], b.ap[:, 0:ncols], reads=[b])

            return pre, evac

        def rows_full(dram):
            return lambda t: dram[t * 128:(t + 1) * 128, :]

        def rows_own(dram):
            return lambda t: dram[(t - 2) * 128:(t - 1) * 128, :]

        full_groups = [[0, 1, 2, 3], [4, 5, 6, 7], [8, 9, 10, 11], [12, 13, 14, 15], [16, 17, 18]]
        own_groups = [[2, 3, 4, 5], [6, 7, 8, 9], [10, 11, 12, 13], [14, 15, 16, 17]]
        blocks4 = lambda grp: [(i * 512, 512) for i in range(4)]

        def phase_ret_out():
            A.off = persist_off
            aTgs = [A.alloc([128, 32, 512], BF16, f"yTg{i}") for i in range(2)]
            wbufs = [A.alloc([128, 16, 512], BF16, f"wo{i}") for i in range(3)]
            pre, evac = make_residual(0, 2 * D, rows_full(xin), rows_full(x1_d), lambda t: 1 if t < 2 else 0)

            def prep(grp, aTg):
                r0 = grp[0] * 128
                n = len(grp) * 128
                return [lambda: S.dma("sync", aTg.ap[:, :, 0:n], yT_d[:, :, r0:r0 + n].rearrange("c p t -> p c t"),
                                      reads=[dbuf("yT_d")], writes=[aTg])]

            gemm_tok(full_groups, prep, WS["ret_w_out"], ret_w_out, 32, blocks4, evac, wbufs, aTgs, kparts=2,
                     pre=pre)
            S.barrier()

        def phase_mlp(layer, groups, x_src, x_dst, row_of):
            A.off = persist_off
            h1T = A.alloc([128, 64, 512], BF16, "h1T")
            wk = alloc_norm_work()
            xts = [A.alloc([128, D], F32, f"mxt{i}") for i in range(2)]
            a16 = A.alloc([128, 16, 512], BF16, "a16")
            w1b = [A.alloc([128, 16, 256], BF16, f"w1b{i}") for i in range(2)]
            w2b = [A.alloc([128, 16, 512], BF16, f"w2b{i}") for i in range(3)]
            rl = [A.alloc([128, 512], F32, f"rl{i}") for i in range(2)]
            pre, evac = make_residual(layer, 5 * D, x_src, x_dst, row_of, nxb=6)
            cn = {"x": 0, "w1": 0, "rl": 0}
            s1 = WStream(WS[f"mlp_w1_{layer}"], mlp_w1[layer], w1b, [(0, fb) for g in groups for fb in range(32)])

            def prep_norm(grp, h1):
                tasks = []
                for gi, t in enumerate(grp):
                    def t1(gi=gi, t=t):
                        xt = xts[cn["x"] % 2]
                        cn["x"] += 1
                        S.dma("sync", xt.ap, x_src(t), writes=[xt])
                        norm_pre(xt, wk)

                    def t2(gi=gi, t=t):
                        norm_tr(row_of(t), layer * 2 + 1, a16, gi, wk)
                    tasks += [t1, t2]
                return tasks

            def prep(grp, h1):
                n = len(grp) * 128
                for fb in range(32):
                    w = s1.get(cn["w1"])
                    cn["w1"] += 1
                    for sub in range(2):
                        pi = next_ps(0, 4)
                        for kc in range(16):
                            S.op("tensor", lambda e, pi=pi, kc=kc, w=w, sub=sub: e.matmul(
                                PS[pi].ap[:, 0:n], lhsT=w.ap[:, kc, sub * 128:(sub + 1) * 128],
                                rhs=a16.ap[:, kc, 0:n], start=(kc == 0), stop=(kc == 15)), [w, a16], [PS[pi]])
                        r = rl[cn["rl"] % 2]
                        cn["rl"] += 1
                        S.op("scalar", lambda e, pi=pi, r=r: e.activation(out=r.ap[:, 0:n], in_=PS[pi].ap[:, 0:n],
                                                                          func=AF.Relu), [PS[pi]], [r])
                        S.op("vector", lambda e, r=r, fb=fb, sub=sub: e.tensor_tensor(
                            out=h1.ap[:, fb * 2 + sub, 0:n], in0=r.ap[:, 0:n], in1=r.ap[:, 0:n], op=ALU.mult),
                            [r], [h1])

            gemm_tok(groups, prep_norm, WS[f"mlp_w2_{layer}"], mlp_w2[layer], 64, blocks4, evac, w2b, [h1T], kparts=4,
                     pre=pre, prep_main=prep)
            S.barrier()

        def phase_att_proj():
            A.off = persist_off
            wk = alloc_norm_work()
            xts = [A.alloc([128, D], F32, f"xt{i}") for i in range(2)]
            aTgs = [A.alloc([128, 16, 512], BF16, f"aTg{i}") for i in range(2)]
            wbufs = [A.alloc([128, 16, 512], BF16, f"w{i}") for i in range(3)]
            tabs = [A.alloc([128, 1024], F32, f"tab{i}") for i in range(4)]
            xs_b = [A.alloc([128, 512], F32, f"xs{i}") for i in range(2)]
            rw = {"t1": A.alloc([128, 512], F32, "t1"), "t2": A.alloc([128, 512], F32, "t2")}
            ob = [A.alloc([128, 512], BF16, f"ob{i}") for i in range(6)]
            qTg = A.alloc([128, 16, 512], BF16, "qTg")
            kTg = A.alloc([128, 4, 512], BF16, "kTg")
            cnt = {"x": 0, "xs": 0, "ob": 0}
            tabmap = {}

            def prep(grp, aTg):
                tasks = []
                for gi, t in enumerate(grp):
                    def t1(gi=gi, t=t):
                        xt = xts[cnt["x"] % 2]
                        cnt["x"] += 1
                        S.dma("sync", xt.ap, x2_d[t * 128:(t + 1) * 128, :], writes=[xt])
                        tb = tabs[gi]
                        S.dma("sync", tb.ap, rope_a[t * 128:(t + 1) * 128, :], writes=[tb])
                        tabmap[t] = tb
                        norm_pre(xt, wk)

                    def t2(gi=gi, t=t):
                        norm_tr(1 if t < 2 else 0, 2, aTg, gi, wk)
                    tasks += [t1, t2]
                return tasks

            def blocks(grp):
                return [(i * 512, 512) for i in range(6)]

            def evac(t, gi, bi, c0, ncols, pi):
                r0 = t * 128
                o = ob[cnt["ob"] % 6]
                cnt["ob"] += 1
                if c0 < 2560:
                    isq = c0 < 2048
                    xs = xs_b[cnt["xs"] % 2]
                    cnt["xs"] += 1
                    S.op("scalar", lambda e: e.activation(out=xs.ap, in_=PS[pi].ap, func=AF.Copy), [PS[pi]], [xs])
                    rope_block(xs, tabmap[t], gi, (o, o.ap), 32, rw)
                    tg = qTg if isq else kTg
                    cb = (c0 // 128) if isq else 0

                    def tr(o=o, tg=tg, cb=cb, gi=gi):
                        pj = next_ps(4, 8)
                        for j in range(4):
                            S.op("tensor", lambda e, j=j, pj=pj, o=o: e.transpose(
                                ps_bf[pj][:, j * 128:(j + 1) * 128], o.ap[:, j * 128:(j + 1) * 128], P_identb.ap),
                                [o, P_identb], [PS[pj]])
                        S.op("scalar", lambda e, pj=pj, tg=tg, cb=cb, gi=gi: e.activation(
                            out=tg.ap[:, cb:cb + 4, gi * 128:(gi + 1) * 128],
                            in_=ps_bf[pj][:, 0:512].rearrange("p (c t) -> p c t", c=4), func=AF.Copy), [PS[pj]], [tg])
                    GCTX["defer"](tr)
                else:
                    S.op("scalar", lambda e: e.activation(out=o.ap, in_=PS[pi].ap, func=AF.Copy), [PS[pi]], [o])
                    S.dma("sync", av_d[r0:r0 + 128, :], o.ap, reads=[o], writes=[dbuf("av_d")])

            def post(grp):
                r0 = grp[0] * 128
                n = len(grp) * 128
                S.dma("sync", aq_d[:, :, r0:r0 + n].rearrange("c p t -> p c t"), qTg.ap[:, :, 0:n], reads=[qTg],
                      writes=[dbuf("aq_d")])
                S.dma("sync", akT_d[:, :, r0:r0 + n].rearrange("c p t -> p c t"), kTg.ap[:, :, 0:n], reads=[kTg],
                      writes=[dbuf("akT_d")])

            gemm_tok(full_groups, prep, WS["attn_w_in"], attn_w_in, 16, blocks, evac, wbufs, aTgs, post, pops=2)
            S.barrier()

        def phase_att():
            A.off = persist_off
            SCALE = float(128 ** -0.5)
            qT = A.alloc([128, 4, R_FULL], BF16, "aqT")
            kT = A.alloc([128, R_FULL], BF16, "akT")
            vv = A.alloc([128, NT_FULL, 128], BF16, "avv")
            oTh = A.alloc([128, 4, 2048], BF16, "oTh")
            am = A.alloc([128, 768], F32, "am")
            sm = [A.alloc([128, 640], F32, f"sm{i}") for i in range(4)]
            pn = [A.alloc([128, 640], BF16, f"pn{i}") for i in range(4)]
            pTa = [A.alloc([128, 5, 512], BF16, f"pTa{i}") for i in range(2)]
            sc = [A.alloc([128, 8], F32, f"asc{i}") for i in range(4)]
            cn = {"i": 0, "pa": 0}
            stage = [A.alloc([128, 16, 512], BF16, f"stg{i}") for i in range(3)]
            wconvert(WS["attn_w_out"], wview(attn_w_out), stage)
            wconvert(WS["mlp_w1_1"], wview(mlp_w1[1]), stage)
            wconvert(WS["mlp_w2_1"], wview(mlp_w2[1]), stage)
            S.dma("sync", am.ap, amask, writes=[am])
            v_v = av_d.rearrange("(t p) c -> p t c", p=128)
            for kvh in range(4):
                S.dma("sync", qT.ap, aq_d[kvh * 4:kvh * 4 + 4].rearrange("c p t -> p c t"), reads=[dbuf("aq_d")],
                      writes=[qT])
                S.dma("sync", kT.ap, akT_d[kvh], reads=[dbuf("akT_d")], writes=[kT])
                S.dma("sync", vv.ap, v_v[:, :, kvh * 128:(kvh + 1) * 128], reads=[dbuf("av_d")], writes=[vv],
                      slow=True)
                stA, stB, stC = [], [], []
                for n in range(16):
                    t = n + 2
                    r0 = t * 128
                    pa = pTa[cn["pa"] % 2]
                    cn["pa"] += 1
                    for g in range(4):
                        hq = kvh * 4 + g
                        i2 = cn["i"] % 4
                        cn["i"] += 1
                        smb, pnb, scb = sm[i2], pn[i2], sc[i2]
                        m0 = 384 if n == 0 else 0

                        def fa(g=g, r0=r0, smb=smb, scb=scb, hq=hq, m0=m0):
                            p1 = next_ps(0, 4)
                            p2 = next_ps(0, 4)
                            S.op("tensor", lambda e: e.matmul(
                                PS[p1].ap[:, 0:384], lhsT=qT.ap[:, g, r0:r0 + 128], rhs=kT.ap[:, r0 - 128:r0 + 256],
                                start=True, stop=True), [qT, kT], [PS[p1]])
                            S.op("tensor", lambda e: e.matmul(
                                PS[p2].ap[:, 0:256], lhsT=qT.ap[:, g, r0:r0 + 128], rhs=kT.ap[:, 0:256],
                                start=True, stop=True), [qT, kT], [PS[p2]])
                            S.op("vector", lambda e: e.tensor_tensor(
                                out=smb.ap[:, 0:384], in0=PS[p1].ap[:, 0:384], in1=am.ap[:, m0:m0 + 384], op=ALU.add),
                                [PS[p1], am], [smb])
                            S.op("scalar", lambda e: e.activation(out=smb.ap[:, 384:640], in_=PS[p2].ap[:, 0:256],
                                                                  func=AF.Copy), [PS[p2]], [smb])
                            S.op("vector", lambda e: e.reduce_max(out=scb.ap[:, 0:1], in_=smb.ap, axis=AX.X),
                                 [smb], [scb])
                            S.op("vector", lambda e: e.tensor_scalar(
                                out=scb.ap[:, 1:2], in0=scb.ap[:, 0:1], scalar1=SCALE, scalar2=P_sink.ap[:, hq:hq + 1],
                                op0=ALU.mult, op1=ALU.max), [scb, P_sink], [scb])
                            S.op("vector", lambda e: e.tensor_scalar(
                                out=scb.ap[:, 2:3], in0=scb.ap[:, 1:2], scalar1=-1.0, scalar2=None, op0=ALU.mult),
                                [scb], [scb])
                            S.op("scalar", lambda e: e.activation(
                                out=smb.ap, in_=smb.ap, func=AF.Exp, bias=scb.ap[:, 2:3], scale=SCALE,
                                accum_out=scb.ap[:, 3:4]), [smb, scb], [smb, scb])
                            S.op("scalar", lambda e: e.activation(
                                out=scb.ap[:, 4:5], in_=P_sink.ap[:, hq:hq + 1], func=AF.Exp, bias=scb.ap[:, 2:3]),
                                [scb, P_sink], [scb])

                        def fb(smb=smb, pnb=pnb, scb=scb):
                            S.op("vector", lambda e: e.tensor_tensor(out=scb.ap[:, 5:6], in0=scb.ap[:, 3:4],
                                                                     in1=scb.ap[:, 4:5], op=ALU.add), [scb], [scb])
                            S.op("vector", lambda e: e.reciprocal(out=scb.ap[:, 5:6], in_=scb.ap[:, 5:6]),
                                 [scb], [scb])
                            S.op("vector", lambda e: e.tensor_scalar(
                                out=pnb.ap, in0=smb.ap, scalar1=scb.ap[:, 5:6], scalar2=None, op0=ALU.mult),
                                [smb, scb], [pnb])

                        def fc(pnb=pnb, pa=pa, g=g, n=n, t=t):
                            pj = next_ps(4, 8)
                            for j in range(5):
                                S.op("tensor", lambda e, j=j: e.transpose(
                                    ps_bf[pj][:, j * 128:(j + 1) * 128], pnb.ap[:, j * 128:(j + 1) * 128],
                                    P_identb.ap), [pnb, P_identb], [PS[pj]])
                            S.op("scalar", lambda e: e.activation(
                                out=pa.ap[:, :, g * 128:(g + 1) * 128],
                                in_=ps_bf[pj][:, 0:640].rearrange("p (c t) -> p c t", c=5), func=AF.Copy),
                                [PS[pj]], [pa])
                            if g == 3:
                                po = next_ps(0, 4)
                                vt = [t - 1, t, t + 1, 0, 1]
                                for j in range(5):
                                    S.op("tensor", lambda e, j=j: e.matmul(
                                        PS[po].ap, lhsT=vv.ap[:, vt[j], :], rhs=pa.ap[:, j, :], start=(j == 0),
                                        stop=(j == 4)), [vv, pa], [PS[po]])
                                S.op("scalar", lambda e: e.activation(
                                    out=oTh.ap[:, :, n * 128:(n + 1) * 128],
                                    in_=PS[po].ap.rearrange("p (c t) -> p c t", c=4), func=AF.Copy), [PS[po]], [oTh])
                        stA.append(fa)
                        stB.append(fb)
                        stC.append(fc)
                NI = len(stA)
                for st_ in range(NI + 2):
                    if st_ < NI:
                        stA[st_]()
                    if 0 <= st_ - 1 < NI:
                        stB[st_ - 1]()
                    if 0 <= st_ - 2 < NI:
                        stC[st_ - 2]()
                S.dma("sync", oT_d[kvh * 4:kvh * 4 + 4].rearrange("c p t -> p c t"), oTh.ap, reads=[oTh],
                      writes=[dbuf("oT_d")])
            S.barrier()

        def phase_att_out():
            A.off = persist_off
            aTgs = [A.alloc([128, 16, 512], BF16, f"oTg{i}") for i in range(2)]
            wbufs = [A.alloc([128, 16, 512], BF16, f"wo{i}") for i in range(3)]
            pre, evac = make_residual(1, 2 * D, rows_full(x2_d), rows_own(x3_d), lambda t: 0)

            def prep(grp, aTg):
                c0 = (grp[0] - 2) * 128
                n = len(grp) * 128
                return [lambda: S.dma("sync", aTg.ap[:, :, 0:n], oT_d[:, :, c0:c0 + n].rearrange("c p t -> p c t"),
                                      reads=[dbuf("oT_d")], writes=[aTg])]

            gemm_tok(own_groups, prep, WS["attn_w_out"], attn_w_out, 16, blocks4, evac, wbufs, aTgs, pre=pre)
            S.barrier()

        def phase_final():
            A.off = persist_off
            fg = A.alloc([128, D], F32, "fg")
            xts = [A.alloc([128, D], F32, f"fx{i}") for i in range(3)]
            junk = A.alloc([128, D], BF16, "fjunk")
            ssq = [A.alloc([128, 1], F32, f"fssq{i}") for i in range(2)]
            S.dma("sync", fg.ap, final_g.partition_broadcast(128), writes=[fg])
            for n in range(16):
                xt = xts[n % 3]
                sq = ssq[n % 2]
                S.dma("sync", xt.ap, x4_d[n * 128:(n + 1) * 128, :], writes=[xt])
                S.op("scalar", lambda e, xt=xt, sq=sq: e.activation(out=junk.ap, in_=xt.ap, func=AF.Square,
                                                                    scale=float(D ** -0.5), accum_out=sq.ap),
                     [xt], [junk, sq])
                S.op("vector", lambda e, sq=sq: e.tensor_scalar(out=sq.ap, in0=sq.ap, scalar1=EPS, scalar2=None,
                                                                op0=ALU.add), [sq], [sq])
                S.op("scalar", lambda e, sq=sq: e.activation(out=sq.ap, in_=sq.ap, func=AF.Sqrt), [sq], [sq])
                S.op("vector", lambda e, sq=sq: e.reciprocal(out=sq.ap, in_=sq.ap), [sq], [sq])
                S.op("vector", lambda e, xt=xt, sq=sq: e.scalar_tensor_tensor(
                    out=xt.ap, in0=xt.ap, scalar=sq.ap[:, 0:1], in1=fg.ap, op0=ALU.mult, op1=ALU.mult),
                    [xt, sq, fg], [xt])
                S.dma("sync", out[n * 128:(n + 1) * 128, :], xt.ap, reads=[xt])
            S.barrier()

        def phase_bench(mode):
            A.off = persist_off
            aTg = A.alloc([128, 16, 512], BF16, "b_aTg")
            wb = [A.alloc([128, 16, 512], BF16, f"b_w{i}") for i in range(2)]
            ob = [A.alloc([128, 512], BF16, f"b_o{i}") for i in range(2)]
            S.op("vector", lambda e: e.memset(aTg.ap, 0.5), [], [aTg])
            for w in wb:
                S.op("vector", lambda e, w=w: e.memset(w.ap, 0.25), [], [w])
            for it in range(160):
                w = wb[it % 2]
                if mode >= 2:
                    S.dma("gpsimd", w.ap, ret_w_in[:, (it % 24) * 512:(it % 24 + 1) * 512].rearrange(
                        "(c p) n -> p c n", p=128), writes=[w])
                pi = next_ps(0, 4)
                for kc in range(16):
                    S.op("tensor", lambda e, pi=pi, kc=kc, w=w, it=it: e.matmul(
                        PS[pi].ap, lhsT=aTg.ap[:, kc, (it % 4) * 128:(it % 4 + 1) * 128], rhs=w.ap[:, kc, :],
                        start=(kc == 0) or mode == 0, stop=(kc == 15) or mode == 0), [aTg, w], [PS[pi]])
                o = ob[it % 2]
                S.op("scalar", lambda e, pi=pi, o=o: e.activation(out=o.ap, in_=PS[pi].ap, func=AF.Copy), [PS[pi]], [o])
            S.barrier()

        phases = {
            "ada": phase_ada, "ret_proj": phase_ret_proj, "ret": phase_ret, "ret_out": phase_ret_out,
            "mlp0": lambda: phase_mlp(0, full_groups, rows_full(x1_d), rows_full(x2_d), lambda t: 1 if t < 2 else 0),
            "att_proj": phase_att_proj, "att": phase_att, "att_out": phase_att_out,
            "mlp1": lambda: phase_mlp(1, own_groups, rows_own(x3_d), rows_own(x4_d), lambda t: 0),
            "final": phase_final,
        }
        order = ["ada", "ret_proj", "ret", "ret_out", "mlp0", "att_proj", "att", "att_out", "mlp1", "final"]
        stop_after = None
        for d in dbg:
            if d.startswith("stop:"):
                stop_after = d[5:]
        for d in dbg:
            if d.startswith("bench:"):
                order = []
                phase_bench(int(d[6:]))
        for ph in order:
            phases[ph]()
            if stop_after == ph:
                break

        S.barrier()
        for e in Sched.ENGS:
            S.op(e, lambda en: en.nop(), [], [])

        S.finalize(eng_sems, dma_sems)
        with nc.Block() as block:
            @block.sync
            def _(e):
                S.emit("sync", e)

            @block.gpsimd
            def _(e):
                S.emit("gpsimd", e)

            @block.tensor
            def _(e):
                S.emit("tensor", e)

            @block.vector
            def _(e):
                S.emit("vector", e)

            @block.scalar
            def _(e):
                S.emit("scalar", e)

    return nc


NEG_MASK = -30000.0


def _const_tables():
    c = np.zeros((128, 1032), np.float32)
    p = np.arange(128, dtype=np.float32)
    jj = p[:, None]
    ii = p[None, :]
    c[:, 0:128] = np.eye(128, dtype=np.float32)
    c[:, 128:256] = np.maximum(ii - jj, 0.0)
    c[:, 256:384] = (ii >= jj).astype(np.float32)
    c[:, 384:512] = np.maximum(jj - ii, 0.0)
    c[:, 512:640] = (jj >= ii).astype(np.float32)
    c[:, 640:768] = ii + 1.0
    c[:, 768:896] = 128.0 - ii
    c[:, 1024] = 127.0 - p
    c[:, 1025] = p
    c[:, 1026] = 128.0
    am = np.zeros((128, 768), np.float32)
    prev = np.where(jj.T >= ii.T, 0.0, NEG_MASK)
    i_ = p[:, None]
    j_ = p[None, :]
    prev = np.where(j_ >= i_, 0.0, NEG_MASK).astype(np.float32)
    nxt = np.where(j_ <= i_, 0.0, NEG_MASK).astype(np.float32)
    am[:, 0:128] = prev
    am[:, 256:384] = nxt
    am[:, 384:512] = NEG_MASK
    am[:, 640:768] = nxt
    return c, am


def _rope_tables(pos):
    f32 = np.float32
    L = pos.shape[0]
    inv_r = (f32(10000.0) ** (-np.arange(0, 256, 2, dtype=f32) / f32(256))).astype(f32)
    ang = pos.astype(f32)[:, None] * inv_r[None, :]
    cr, sr = np.cos(ang).astype(f32), np.sin(ang).astype(f32)
    rr = np.zeros((R_ALL, 1024), f32)
    rr[:NCTX, 0:512] = 1.0
    rr[NCTX:, 0:512] = np.tile(cr, (1, 4))
    rr[NCTX:, 512:1024] = np.tile(np.concatenate([-sr, sr], axis=1), (1, 2))
    inv_a = (f32(10000.0) ** (-np.arange(0, 64, 2, dtype=f32) / f32(64))).astype(f32)
    rows = (pos // 64).astype(f32)
    cols = (pos % 64).astype(f32)
    ar = rows[:, None] * inv_a[None, :]
    ac = cols[:, None] * inv_a[None, :]
    c128 = np.concatenate([np.cos(ar), np.cos(ar), np.cos(ac), np.cos(ac)], axis=1).astype(f32)
    s128 = np.concatenate([-np.sin(ar), np.sin(ar), -np.sin(ac), np.sin(ac)], axis=1).astype(f32)
    ra = np.zeros((R_FULL, 1024), f32)
    ra[:NCTX, 0:512] = 1.0
    n = R_FULL - NCTX
    ra[NCTX:, 0:512] = np.tile(c128[:n], (1, 4))
    ra[NCTX:, 512:1024] = np.tile(s128[:n], (1, 4))
    return rr, ra


def make_in_maps(inp, cores=range(8)):
    f32 = np.float32
    g = {k: np.asarray(v) for k, v in inp.items()}
    consts, amask = _const_tables()
    shared = {
        "ada_w": np.ascontiguousarray(g["ada_w"], f32), "ada_b": np.ascontiguousarray(g["ada_b"], f32),
        "norm_mix_g": np.ascontiguousarray(g["norm_mix_g"], f32),
        "norm_mlp_g": np.ascontiguousarray(g["norm_mlp_g"], f32),
        "mlp_w1": np.ascontiguousarray(g["mlp_w1"], f32), "mlp_w2": np.ascontiguousarray(g["mlp_w2"], f32),
        "ret_w_in": np.ascontiguousarray(g["ret_w_in"][0], f32),
        "ret_w_out": np.ascontiguousarray(g["ret_w_out"][0], f32),
        "attn_w_in": np.ascontiguousarray(g["attn_w_in"][0], f32),
        "attn_w_out": np.ascontiguousarray(g["attn_w_out"][0], f32),
        "attn_sink": np.ascontiguousarray(g["attn_sink"], f32).reshape(1, 16),
        "final_g": np.ascontiguousarray(g["final_norm_g"], f32).reshape(1, D),
        "consts": consts, "amask": amask,
    }
    tabs = {}
    for h in range(2):
        pos = np.arange(4096) if h == 0 else np.arange(4095, -1, -1)
        tabs[h] = _rope_tables(pos)
    maps = []
    for core in cores:
        b, h = core // 2, core % 2
        x = g["x"][b]
        cx = g["ctx"][b]
        if h == 1:
            x = x[::-1]
            cx = cx[::-1]
        xin = np.ascontiguousarray(np.concatenate([cx, x], axis=0), f32)
        cvec = np.ascontiguousarray(np.stack([g["c"][b], g["c_ctx"]], axis=0), f32)
        df, db = g["ret_decay_fwd"][0], g["ret_decay_bwd"][0]
        dec = np.concatenate([df, db] if h == 0 else [db, df]).astype(f32).reshape(1, 16)
        m = dict(shared)
        m.update({"xin": xin, "cvec": cvec, "dec": dec, "rope_r": tabs[h][0], "rope_a": tabs[h][1]})
        maps.append(m)
    return maps


def assemble(results, B=4):
    out = np.zeros((B, 4096, D), np.float32)
    for core, r in enumerate(results):
        b, h = core // 2, core % 2
        o = np.asarray(r["out"])
        if h == 0:
            out[b, :2048] = o
        else:
            out[b, 2048:] = o[::-1]
    return out


_NC_CACHE = {}


def kernel(**inputs):
    if "nc" not in _NC_CACHE:
        _NC_CACHE["nc"] = build_program()
    nc = _NC_CACHE["nc"]
    in_maps = make_in_maps(inputs)
    res = run_bass_kernel_spmd(nc, in_maps, core_ids=list(range(8)))
    return assemble(res.results)
```

```python
import numpy as np
from contextlib import ExitStack
import concourse.bass as bass
import concourse.mybir as mybir
from concourse.bass_utils import run_bass_kernel_spmd

F32 = mybir.dt.float32
BF16 = mybir.dt.bfloat16
AF = mybir.ActivationFunctionType
ALU = mybir.AluOpType
AX = mybir.AxisListType

D = 2048
KC = 16
DFF = 8192
NCTX = 256
R_ALL = 4352
R_FULL = 2432
NT_ALL = 34
NT_FULL = 19
EPS = 1e-6
NS_DMA = 8
ARENA_BYTES = 211968


class Ins:
    __slots__ = ("eng", "fn", "waits", "signal", "sem", "val", "prev_val", "is_dma")

    def __init__(self, eng, fn, is_dma):
        self.eng = eng
        self.fn = fn
        self.waits = []
        self.signal = is_dma
        self.sem = None
        self.val = 0
        self.prev_val = 0
        self.is_dma = is_dma


class Buf:
    __slots__ = ("ap", "w", "r", "name")

    def __init__(self, ap, name=""):
        self.ap = ap
        self.w = None
        self.r = []
        self.name = name


class Sched:
    ENGS = ("sync", "gpsimd", "tensor", "vector", "scalar")

    def __init__(self):
        self.streams = {e: [] for e in self.ENGS}
        self.bar = {e: [] for e in self.ENGS}

    def op(self, eng, fn, reads=(), writes=(), dma=False):
        ins = Ins(eng, fn, dma)
        deps = []
        for b in reads:
            if b.w is not None:
                deps.append(b.w)
        for b in writes:
            if b.w is not None:
                deps.append(b.w)
            deps.extend(b.r)
        if self.bar[eng]:
            deps.extend(self.bar[eng])
            self.bar[eng] = []
        seen = set()
        for d in deps:
            if id(d) in seen or d is ins:
                continue
            seen.add(id(d))
            if d.eng == eng and not d.is_dma and eng == "tensor":
                continue
            ins.waits.append(d)
            d.signal = True
        for b in reads:
            if not dma:
                b.r = [x for x in b.r if x.is_dma or x.eng != eng]
            b.r.append(ins)
        for b in writes:
            b.w = ins
            b.r = []
        self.streams[eng].append(ins)
        return ins

    def dma(self, q, out, in_, reads=(), writes=(), slow=False):
        if slow:
            return self.op(q, lambda e: e.dma_start(out=out, in_=in_, allow_slow_non_contiguous=True),
                           reads, writes, dma=True)
        return self.op(q, lambda e: e.dma_start(out=out, in_=in_), reads, writes, dma=True)

    def barrier(self):
        deps = []
        for e in self.ENGS:
            st = self.streams[e]
            nd = 0
            last_c = None
            for ins in reversed(st):
                if ins.is_dma:
                    if nd < NS_DMA:
                        deps.append(ins)
                        nd += 1
                elif last_c is None:
                    last_c = ins
                    deps.append(ins)
                if nd >= NS_DMA and last_c is not None:
                    break
        for e in self.ENGS:
            self.bar[e] = list(deps)

    def finalize(self, eng_sems, dma_sems):
        for e in self.ENGS:
            cnt = 0
            di = 0
            for ins in self.streams[e]:
                if ins.is_dma:
                    s = di % NS_DMA
                    k = di // NS_DMA
                    ins.sem = dma_sems[e][s]
                    ins.val = 16 * (k + 1)
                    ins.prev_val = 16 * k
                    di += 1
                elif ins.signal:
                    cnt += 1
                    ins.sem = eng_sems[e]
                    ins.val = cnt

    def emit(self, ename, eng):
        known = {}
        for ins in self.streams[ename]:
            waits = {}
            for d in ins.waits:
                k = id(d.sem)
                if k not in waits or waits[k][1] < d.val:
                    waits[k] = (d.sem, d.val)
            if ins.is_dma and ins.prev_val > 0:
                k = id(ins.sem)
                if k not in waits or waits[k][1] < ins.prev_val:
                    waits[k] = (ins.sem, ins.prev_val)
            for k, (sem, val) in waits.items():
                if known.get(k, 0) >= val:
                    continue
                eng.wait_ge(sem, val)
                known[k] = val
            bi = ins.fn(eng)
            if ins.is_dma:
                bi.then_inc(ins.sem, 16)
            elif ins.signal:
                bi.then_inc(ins.sem, 1)


class Arena:
    def __init__(self, ap):
        self.base = ap
        self.off = 0

    def reset(self):
        self.off = 0

    def alloc(self, shape, dt, name=""):
        esz = 4 if dt == F32 else 2
        n = int(np.prod(shape[1:]))
        nbytes = (n * esz + 31) // 32 * 32
        assert self.off + nbytes <= ARENA_BYTES, (name, self.off, nbytes)
        a = self.base[:, self.off // 2: self.off // 2 + n * esz // 2]
        self.off += nbytes
        if dt != BF16:
            a = a.bitcast(dt)
        if len(shape) == 3:
            a = a.rearrange("p (a b) -> p a b", a=shape[1])
        elif len(shape) == 4:
            a = a.rearrange("p (a b c) -> p a b c", a=shape[1], b=shape[2])
        if shape[0] != 128:
            a = a[0:shape[0]]
        return Buf(a, name)


def build_program(debug=None):
    nc = bass.Bass("TRN2", target_bir_lowering=False)
    dbg = set(debug or [])

    def din(name, shape, dt=F32):
        return nc.dram_tensor(name, list(shape), dt, kind="ExternalInput").ap()

    def dscr(name, shape, dt):
        kind = "ExternalOutput" if name in dbg else "Internal"
        return nc.dram_tensor(name, list(shape), dt, kind=kind).ap()

    xin = din("xin", [R_ALL, D])
    cvec = din("cvec", [2, D])
    ada_w = din("ada_w", [2, D, 6 * D])
    ada_b = din("ada_b", [2, 6 * D])
    norm_mix_g = din("norm_mix_g", [2, D])
    norm_mlp_g = din("norm_mlp_g", [2, D])
    mlp_w1 = din("mlp_w1", [2, D, DFF])
    mlp_w2 = din("mlp_w2", [2, DFF, D])
    ret_w_in = din("ret_w_in", [D, 12288])
    ret_w_out = din("ret_w_out", [4096, D])
    dec = din("dec", [1, 16])
    attn_w_in = din("attn_w_in", [D, 3072])
    attn_w_out = din("attn_w_out", [D, D])
    attn_sink = din("attn_sink", [1, 16])
    final_g = din("final_g", [1, D])
    consts = din("consts", [128, 1024 + 8])
    rope_r = din("rope_r", [R_ALL, 1024])
    rope_a = din("rope_a", [R_FULL, 1024])
    amask = din("amask", [128, 768])
    out = nc.dram_tensor("out", [2048, D], F32, kind="ExternalOutput").ap()

    mod_d = dscr("mod_d", [2, 2, 6 * D], F32)
    qT_d = dscr("qT_d", [16, 128, R_FULL], BF16)
    kT_d = dscr("kT_d", [16, 128, R_FULL], BF16)
    k_d = dscr("k_d", [R_ALL, 2048], BF16)
    v_d = dscr("v_d", [R_ALL, 4096], BF16)
    sg_d = dscr("sg_d", [R_FULL, 4096], BF16)
    yT_d = dscr("yT_d", [32, 128, R_FULL], BF16)
    x1_d = dscr("x1_d", [R_FULL, D], F32)
    x2_d = dscr("x2_d", [R_FULL, D], F32)
    aq_d = dscr("aq_d", [16, 128, R_FULL], BF16)
    akT_d = dscr("akT_d", [4, 128, R_FULL], BF16)
    av_d = dscr("av_d", [R_FULL, 512], BF16)
    oT_d = dscr("oT_d", [16, 128, 2048], BF16)
    x3_d = dscr("x3_d", [2048, D], F32)
    x4_d = dscr("x4_d", [2048, D], F32)

    S = Sched()

    with ExitStack() as es:
        arena_t = es.enter_context(nc.sbuf_tensor("arena", [128, ARENA_BYTES // 2], BF16))
        psA = es.enter_context(nc.psum_tensor("psA", [128, 8, 512], F32))
        eng_sems = {e: es.enter_context(nc.semaphore(f"sem_{e}")) for e in Sched.ENGS}
        dma_sems = {e: [es.enter_context(nc.semaphore(f"dsem_{e}{i}")) for i in range(NS_DMA)]
                    for e in ("sync", "gpsimd")}
        A = Arena(arena_t)
        PS = [Buf(psA[:, i, :], f"ps{i}") for i in range(8)]
        ps_bf = [psA[:, i, :].bitcast(BF16) for i in range(8)]

        DRAMB = {}

        def dbuf(name):
            if name not in DRAMB:
                DRAMB[name] = Buf(None, name)
            return DRAMB[name]

        rr = {"ps": 0}

        def next_ps(lo=0, hi=4):
            i = lo + rr.setdefault((lo, hi), 0) % (hi - lo)
            rr[(lo, hi)] += 1
            return i

        P_ident = A.alloc([128, 128], F32, "ident")
        P_identb = A.alloc([128, 128], BF16, "identb")
        P_c = A.alloc([128, 1024 + 8], F32, "consts")
        P_modF = A.alloc([128, 2, 2, 96], F32, "modF")
        P_gF = A.alloc([128, 2, 2, 16], F32, "gF")
        P_AB = A.alloc([128, 4, 2, 32], F32, "AB")
        P_dec = A.alloc([128, 16], F32, "dec")
        P_lg = A.alloc([128, 16], F32, "lg")
        P_sink = A.alloc([128, 16], F32, "sink")
        P_eps = A.alloc([128, 1], F32, "eps")
        persist_off = A.off

        S.dma("sync", P_c.ap, consts, writes=[P_c])
        S.dma("sync", P_ident.ap, consts[:, 0:128], writes=[P_ident])
        S.dma("sync", P_dec.ap, dec.partition_broadcast(128), writes=[P_dec])
        S.dma("sync", P_sink.ap, attn_sink.partition_broadcast(128), writes=[P_sink])
        S.op("vector", lambda e: e.tensor_copy(out=P_identb.ap, in_=P_ident.ap), [P_ident], [P_identb])
        S.op("vector", lambda e: e.memset(P_eps.ap, EPS), [], [P_eps])
        for l in range(2):
            S.dma("sync", P_gF.ap[:, l, 0, :], norm_mix_g[l].rearrange("(c p) -> p c", p=128), writes=[P_gF], slow=True)
            S.dma("sync", P_gF.ap[:, l, 1, :], norm_mlp_g[l].rearrange("(c p) -> p c", p=128), writes=[P_gF], slow=True)

        def cst(lo, hi):
            return P_c.ap[:, lo:hi]

        def phase_ada():
            A.off = persist_off
            cT = A.alloc([128, 16, 2], F32, "cT")
            cTb = A.alloc([128, 16, 2], BF16, "cTb")
            brow = A.alloc([2, 6 * D], F32, "brow")
            mrow = A.alloc([2, 6 * D], F32, "mrow")
            wb = [A.alloc([128, 16, 512], BF16, f"adaw{i}") for i in range(4)]
            for r in range(2):
                S.dma("sync", cT.ap[:, :, r], cvec[r].rearrange("(c p) -> p c", p=128), writes=[cT], slow=True)
            S.op("scalar", lambda e: e.activation(out=cTb.ap, in_=cT.ap, func=AF.Silu), [cT], [cTb])
            asteps = [(l, nb) for l in range(2) for nb in range(24)]
            st = {"p": 0}

            def aload(upto):
                while st["p"] <= min(upto, len(asteps) - 1):
                    l2, nb2 = asteps[st["p"]]
                    w2 = wb[st["p"] % 4]
                    S.dma("gpsimd", w2.ap, ada_w[l2, :, nb2 * 512:(nb2 + 1) * 512].rearrange("(c p) n -> p c n", p=128),
                          writes=[w2])
                    st["p"] += 1

            for l in range(2):
                S.dma("sync", brow.ap, ada_b[l:l + 1, :].partition_broadcast(2), writes=[brow])
                for nb in range(24):
                    i = l * 24 + nb
                    aload(i + 3)
                    w = wb[i % 4]
                    pi = next_ps(0, 4)
                    for kc in range(16):
                        S.op("tensor", lambda e, pi=pi, kc=kc, w=w: e.matmul(
                            PS[pi].ap[0:2, :], lhsT=cTb.ap[:, kc, :], rhs=w.ap[:, kc, :],
                            start=(kc == 0), stop=(kc == 15)), [cTb, w], [PS[pi]])
                    S.op("vector", lambda e, pi=pi, nb=nb: e.tensor_tensor(
                        out=mrow.ap[:, nb * 512:(nb + 1) * 512], in0=PS[pi].ap[0:2, :],
                        in1=brow.ap[:, nb * 512:(nb + 1) * 512], op=ALU.add), [PS[pi], brow], [mrow])
                S.dma("sync", mod_d[l], mrow.ap, reads=[mrow], writes=[dbuf("mod_d")])
            for l in range(2):
                for r in range(2):
                    S.dma("sync", P_modF.ap[:, l, r, :], mod_d[l, r].rearrange("(c p) -> p c", p=128),
                          reads=[dbuf("mod_d")], writes=[P_modF], slow=True)
            for l in range(2):
                for sub in range(2):
                    sh0 = 0 if sub == 0 else 48
                    sc0 = sh0 + 16
                    for r in range(2):
                        S.op("vector", lambda e, l=l, sub=sub, r=r, sc0=sc0: e.scalar_tensor_tensor(
                            out=P_AB.ap[:, l * 2 + sub, r, 0:16], in0=P_modF.ap[:, l, r, sc0:sc0 + 16], scalar=1.0,
                            in1=P_gF.ap[:, l, sub, :], op0=ALU.add, op1=ALU.mult), [P_modF, P_gF], [P_AB])
                        S.op("vector", lambda e, l=l, sub=sub, r=r, sh0=sh0: e.tensor_copy(
                            out=P_AB.ap[:, l * 2 + sub, r, 16:32], in_=P_modF.ap[:, l, r, sh0:sh0 + 16]),
                            [P_modF], [P_AB])
            S.op("scalar", lambda e: e.activation(out=P_lg.ap, in_=P_dec.ap, func=AF.Exp, scale=-1.0), [P_dec], [P_lg])
            S.op("scalar", lambda e: e.activation(out=P_lg.ap, in_=P_lg.ap, func=AF.Ln, bias=1.0), [P_lg], [P_lg])
            S.op("vector", lambda e: e.tensor_scalar(out=P_lg.ap, in0=P_lg.ap, scalar1=-1.0, scalar2=None,
                                                     op0=ALU.mult), [P_lg], [P_lg])
            S.barrier()

        def norm_pre(xt, work):
            junk, ssq, xn = work["junk"], work["ssq"], work["xn"]
            S.op("scalar", lambda e: e.activation(out=junk.ap, in_=xt.ap, func=AF.Square,
                                                  scale=float(D ** -0.5), accum_out=ssq.ap), [xt], [xn, ssq])
            S.op("scalar", lambda e: e.activation(out=ssq.ap, in_=ssq.ap, func=AF.Sqrt, bias=P_eps.ap[:, 0:1]),
                 [ssq, P_eps], [ssq])
            S.op("vector", lambda e: e.reciprocal(out=ssq.ap, in_=ssq.ap), [ssq], [ssq])
            S.op("scalar", lambda e: e.activation(out=xn.ap, in_=xt.ap, func=AF.Copy, scale=ssq.ap[:, 0:1]),
                 [xt, ssq], [xn])

        def norm_tr(row, lsub, aTg, gi, work):
            xn = work["xn"]
            for b4 in range(4):
                pi = next_ps(*GCTX.get("tr_banks", (4, 8)))
                for j in range(4):
                    kc = b4 * 4 + j
                    S.op("tensor", lambda e, pi=pi, j=j, kc=kc: e.transpose(
                        PS[pi].ap[:, j * 128:(j + 1) * 128], xn.ap[:, kc * 128:(kc + 1) * 128], P_ident.ap),
                        [xn, P_ident], [PS[pi]])
                for j in range(4):
                    kc = b4 * 4 + j
                    S.op("vector", lambda e, pi=pi, j=j, kc=kc: e.tensor_scalar(
                        out=aTg.ap[:, kc, gi * 128:(gi + 1) * 128], in0=PS[pi].ap[:, j * 128:(j + 1) * 128],
                        scalar1=P_AB.ap[:, lsub, row, kc:kc + 1], scalar2=P_AB.ap[:, lsub, row, 16 + kc:17 + kc],
                        op0=ALU.mult, op1=ALU.add), [PS[pi], P_AB], [aTg])

        def norm_mod_T(xt, row, lsub, aTg, gi, work):
            norm_pre(xt, work)
            norm_tr(row, lsub, aTg, gi, work)

        def alloc_norm_work():
            xn = A.alloc([128, D], F32, "xn")
            return {"junk": xn, "ssq": A.alloc([128, 1], F32, "ssq"), "xn": xn}

        converted = set()
        GCTX = {}

        def wscratch(name, K, N, kp, ncols):
            nkq = K // 128 // kp
            nbi = N // ncols
            t = nc.dram_tensor("ws_" + name, [nkq, nbi, 128, kp * ncols], BF16, kind="Internal").ap()
            return {"ap": t, "kp": kp, "ncols": ncols, "name": name}

        def wload(ws, Wv, kq, bi, w):
            key = (ws["name"], kq, bi)
            kp, ncols = ws["kp"], ws["ncols"]
            sc = ws["ap"][kq, bi].rearrange("p (c n) -> p c n", c=kp)
            if key in converted:
                S.dma("sync", w.ap, sc, reads=[dbuf(key)], writes=[w])
            else:
                S.dma("gpsimd", w.ap, Wv[:, kq * kp:(kq + 1) * kp, bi * ncols:(bi + 1) * ncols], writes=[w])
                S.dma("gpsimd", sc, w.ap, reads=[w], writes=[dbuf(key)])
                converted.add(key)

        def wconvert(ws, Wv, stage, nhalf=1):
            kp, ncols = ws["kp"], ws["ncols"]
            nkq, nbi = ws["ap"].shape[0], ws["ap"].shape[1]
            hk = kp // nhalf
            i = 0
            for kq in range(nkq):
                for bi in range(nbi):
                    key = (ws["name"], kq, bi)
                    if key in converted:
                        continue
                    sc = ws["ap"][kq, bi].rearrange("p (c n) -> p c n", c=kp)
                    for hf in range(nhalf):
                        st = stage[i % len(stage)]
                        i += 1
                        sv = st.ap.rearrange("p a b -> p (a b)")[:, 0:hk * ncols].rearrange("p (c n) -> p c n", c=hk)
                        S.dma("gpsimd", sv,
                              Wv[:, kq * kp + hf * hk:kq * kp + (hf + 1) * hk, bi * ncols:(bi + 1) * ncols],
                              writes=[st])
                        S.dma("gpsimd", sc[:, hf * hk:(hf + 1) * hk, :], sv, reads=[st],
                              writes=[dbuf(key)])
                    converted.add(key)

        class WStream:
            def __init__(self, ws, W, wbufs, steps):
                self.ws, self.wbufs, self.steps = ws, wbufs, steps
                self.Wv = W.rearrange("(c p) n -> p c n", p=128)
                self.planned = 0
                self.pf = len(wbufs) - 1

            def get(self, i):
                upto = min(i + self.pf, len(self.steps) - 1)
                while self.planned <= upto:
                    kq, bi = self.steps[self.planned]
                    wload(self.ws, self.Wv, kq, bi, self.wbufs[self.planned % len(self.wbufs)])
                    self.planned += 1
                return self.wbufs[i % len(self.wbufs)]

        EARLY = set()

        def gemm_tok(groups, prep, ws, W, kcn, blocks, evac, wbufs, aTgs, post=None, kparts=1, pre=None,
                     prep_main=None, early_blocks=1, pops=1):
            kp = kcn // kparts
            steps = []
            for g, grp in enumerate(groups):
                for bi, (c0, ncols) in enumerate(blocks(grp)):
                    for kq in range(kparts):
                        steps.append((g, bi, c0, ncols, kq))
            stream = WStream(ws, W, wbufs, [(kq, c0 // ncols) for (g, bi, c0, ncols, kq) in steps])
            pend = []
            GCTX["defer"] = lambda fn: pend.append([0, fn])
            prepped = set()
            nblk = {}
            bgq = []
            for (g, bi, c0, ncols, kq) in steps:
                nblk[g] = max(nblk.get(g, 0), bi + 1)
            for i, (g, bi, c0, ncols, kq) in enumerate(steps):
                grp = groups[g]
                aTg = aTgs[g % len(aTgs)]
                w = stream.get(i)
                if bi == 0 and kq == 0:
                    GCTX["tr_banks"] = (4, 8)
                    if g not in prepped:
                        bgq.extend(prep(grp, aTg))
                        prepped.add(g)
                    while bgq:
                        bgq.pop(0)()
                    if prep_main is not None:
                        prep_main(grp, aTg)
                if (bi == max(nblk[g] - early_blocks, 0) and kq == 0 and g + 1 < len(groups)
                        and (g + 1) not in prepped and (len(aTgs) > 1 or prep_main is not None)):
                    bgq.extend(prep(groups[g + 1], aTgs[(g + 1) % len(aTgs)]))
                    prepped.add(g + 1)
                    if kparts > 1:
                        GCTX["tr_banks"] = (0, 4) if bi % 2 == 1 else (4, 8)
                if kq == 0 and pre is not None:
                    pre(grp, bi, c0, ncols)
                for gi, t in enumerate(grp):
                    if kparts == 1:
                        pi = next_ps(0, 4)
                    else:
                        pi = (bi % 2) * 4 + gi
                    for k2 in range(kp):
                        kc = kq * kp + k2
                        S.op("tensor", lambda e, pi=pi, kc=kc, k2=k2, w=w, aTg=aTg, gi=gi, ncols=ncols: e.matmul(
                            PS[pi].ap[:, 0:ncols], lhsT=aTg.ap[:, kc, gi * 128:(gi + 1) * 128],
                            rhs=w.ap[:, k2, 0:ncols], start=(kc == 0), stop=(kc == kcn - 1)),
                            [aTg, w], [PS[pi]])
                    for p in pend:
                        p[0] += 1
                    while pend and pend[0][0] >= 2:
                        pend.pop(0)[1]()
                    for _ in range(pops):
                        if bgq:
                            bgq.pop(0)()
                    if kq == kparts - 1:
                        evac(t, gi, bi, c0, ncols, pi)
                last = (i + 1 == len(steps)) or steps[i + 1][0] != g
                if last:
                    while pend:
                        pend.pop(0)[1]()
                    if post is not None:
                        post(grp)

        WS = {
            "ret_w_in": wscratch("ret_w_in", D, 12288, 16, 512),
            "ret_w_out": wscratch("ret_w_out", 4096, D, 16, 512),
            "mlp_w1_0": wscratch("mlp_w1_0", D, DFF, 16, 256),
            "mlp_w2_0": wscratch("mlp_w2_0", DFF, D, 16, 512),
            "mlp_w1_1": wscratch("mlp_w1_1", D, DFF, 16, 256),
            "mlp_w2_1": wscratch("mlp_w2_1", DFF, D, 16, 512),
            "attn_w_in": wscratch("attn_w_in", D, 3072, 16, 512),
            "attn_w_out": wscratch("attn_w_out", D, D, 16, 512),
        }

        def wview(W):
            return W.rearrange("(c p) n -> p c n", p=128)

        def rope_block(xs, tab, gi_unused, outb, B, work):
            t1, t2 = work["t1"], work["t2"]
            n = 512 // (2 * B)
            xv = xs.ap.rearrange("p (n two b) -> p n two b", two=2, b=B)
            sv = tab.ap[:, 512:1024].rearrange("p (n two b) -> p n two b", two=2, b=B)
            t2v = t2.ap.rearrange("p (n two b) -> p n two b", two=2, b=B)
            S.op("vector", lambda e: e.tensor_tensor(out=t1.ap, in0=xs.ap, in1=tab.ap[:, 0:512], op=ALU.mult),
                 [xs, tab], [t1])
            S.op("vector", lambda e: e.tensor_tensor(out=t2v[:, :, 0, :], in0=xv[:, :, 1, :], in1=sv[:, :, 0, :],
                                                     op=ALU.mult), [xs, tab], [t2])
            S.op("vector", lambda e: e.tensor_tensor(out=t2v[:, :, 1, :], in0=xv[:, :, 0, :], in1=sv[:, :, 1, :],
                                                     op=ALU.mult), [xs, tab], [t2])
            S.op("vector", lambda e: e.tensor_tensor(out=outb[1], in0=t1.ap, in1=t2.ap, op=ALU.add),
                 [t1, t2], [outb[0]])

        def phase_ret_proj():
            A.off = persist_off
            wk = alloc_norm_work()
            xts = [A.alloc([128, D], F32, f"xt{i}") for i in range(2)]
            aTgs = [A.alloc([128, 16, 512], BF16, f"aTg{i}") for i in range(2)]
            wbufs = [A.alloc([128, 16, 512], BF16, f"w{i}") for i in range(3)]
            tabs = [A.alloc([128, 1024], F32, f"tab{i}") for i in range(4)]
            xs_b = [A.alloc([128, 512], F32, f"xs{i}") for i in range(2)]
            rw = {"t1": A.alloc([128, 512], F32, "t1"), "t2": A.alloc([128, 512], F32, "t2")}
            ob = [A.alloc([128, 512], BF16, f"ob{i}") for i in range(6)]
            qTg = A.alloc([128, 16, 512], BF16, "qTg")
            kTg = A.alloc([128, 16, 512], BF16, "kTg")
            cnt = {"x": 0, "xs": 0, "ob": 0}
            full_groups = [[0, 1, 2, 3], [4, 5, 6, 7], [8, 9, 10, 11], [12, 13, 14, 15], [16, 17, 18]]
            far_groups = [[19, 20, 21, 22], [23, 24, 25, 26], [27, 28, 29, 30], [31, 32, 33]]
            groups = full_groups + far_groups
            tabmap = {}

            def prep(grp, aTg):
                tasks = []
                for gi, t in enumerate(grp):
                    def t1(gi=gi, t=t):
                        xt = xts[cnt["x"] % 2]
                        cnt["x"] += 1
                        S.dma("sync", xt.ap, xin[t * 128:(t + 1) * 128, :], writes=[xt])
                        tb = tabs[gi]
                        S.dma("sync", tb.ap, rope_r[t * 128:(t + 1) * 128, :], writes=[tb])
                        tabmap[t] = tb
                        norm_pre(xt, wk)

                    def t2(gi=gi, t=t):
                        norm_tr(1 if t < 2 else 0, 0, aTg, gi, wk)
                    tasks += [t1, t2]
                return tasks

            def blocks(grp):
                if grp[0] >= NT_FULL:
                    return [(2048 + i * 512, 512) for i in range(4)] + [(4096 + i * 512, 512) for i in range(8)]
                return [(i * 512, 512) for i in range(24)]

            def evac(t, gi, bi, c0, ncols, pi):
                full = t < NT_FULL
                r0 = t * 128
                if c0 < 4096:
                    isq = c0 < 2048
                    xs = xs_b[cnt["xs"] % 2]
                    cnt["xs"] += 1
                    S.op("scalar", lambda e: e.activation(out=xs.ap, in_=PS[pi].ap, func=AF.Copy,
                                                          scale=1.0 if isq else 0.0625), [PS[pi]], [xs])
                    o = ob[cnt["ob"] % 6]
                    cnt["ob"] += 1
                    rope_block(xs, tabmap[t], gi, (o, o.ap), 128, rw)
                    cb = (c0 % 2048) // 128
                    if not isq:
                        S.dma("sync", k_d[r0:r0 + 128, c0 - 2048:c0 - 2048 + 512], o.ap, reads=[o],
                              writes=[dbuf("k_d")])
                    if full:
                        tg = qTg if isq else kTg

                        def tr(o=o, tg=tg, cb=cb, gi=gi):
                            pj = next_ps(4, 8)
                            for j in range(4):
                                S.op("tensor", lambda e, j=j, pj=pj, o=o: e.transpose(
                                    ps_bf[pj][:, j * 128:(j + 1) * 128], o.ap[:, j * 128:(j + 1) * 128], P_identb.ap),
                                    [o, P_identb], [PS[pj]])
                            S.op("scalar", lambda e, pj=pj, tg=tg, cb=cb, gi=gi: e.activation(
                                out=tg.ap[:, cb:cb + 4, gi * 128:(gi + 1) * 128],
                                in_=ps_bf[pj][:, 0:512].rearrange("p (c t) -> p c t", c=4), func=AF.Copy),
                                [PS[pj]], [tg])
                        GCTX["defer"](tr)
                elif c0 < 8192:
                    o = ob[cnt["ob"] % 6]
                    cnt["ob"] += 1
                    S.op("scalar", lambda e: e.activation(out=o.ap, in_=PS[pi].ap, func=AF.Copy), [PS[pi]], [o])
                    S.dma("sync", v_d[r0:r0 + 128, c0 - 4096:c0 - 4096 + 512], o.ap, reads=[o], writes=[dbuf("v_d")])
                else:
                    o = ob[cnt["ob"] % 6]
                    cnt["ob"] += 1
                    S.op("scalar", lambda e: e.activation(out=o.ap, in_=PS[pi].ap, func=AF.Silu), [PS[pi]], [o])
                    S.dma("sync", sg_d[r0:r0 + 128, c0 - 8192:c0 - 8192 + 512], o.ap, reads=[o],
                          writes=[dbuf("sg_d")])

            def post(grp):
                if grp[0] >= NT_FULL:
                    return
                r0 = grp[0] * 128
                n = len(grp) * 128
                S.dma("sync", qT_d[:, :, r0:r0 + n].rearrange("c p t -> p c t"), qTg.ap[:, :, 0:n], reads=[qTg],
                      writes=[dbuf("qT_d")])
                S.dma("sync", kT_d[:, :, r0:r0 + n].rearrange("c p t -> p c t"), kTg.ap[:, :, 0:n], reads=[kTg],
                      writes=[dbuf("kT_d")])

            gemm_tok(groups, prep, WS["ret_w_in"], ret_w_in, 16, blocks, evac, wbufs, aTgs, post, early_blocks=2)
            S.barrier()

        def phase_ret():
            A.off = persist_off
            qT = A.alloc([128, 2, R_FULL], BF16, "qT")
            kT = A.alloc([128, 2, R_FULL], BF16, "kT")
            kk = A.alloc([128, NT_ALL, 256], BF16, "kk")
            vv = A.alloc([128, NT_ALL, 512], BF16, "vv")
            sg = A.alloc([128, NT_FULL, 512], BF16, "sg")
            snap = A.alloc([128, NT_FULL, 1024], BF16, "snap")
            yT = A.alloc([128, 4, R_FULL], BF16, "yT")
            Sst = [[A.alloc([128, 512], F32, f"S{i}{dc}") for dc in range(2)] for i in range(2)]
            Sbf = [[[A.alloc([128, 512], BF16, f"Sbf{i}{p}{dc}") for dc in range(2)] for p in range(2)]
                   for i in range(2)]
            maskc = A.alloc([128, 128], F32, "maskc")
            mtmp = A.alloc([128, 128], F32, "mtmp")
            dq = [A.alloc([128, 128], F32, f"dq{i}") for i in range(2)]
            dsc = A.alloc([128, 4], F32, "dsc")
            qs = [A.alloc([128, 2, 128], BF16, f"qs{i}") for i in range(4)]
            ks = [A.alloc([128, 256], BF16, f"ks{i}") for i in range(5)]
            pT = [A.alloc([128, 128], BF16, f"pT{i}") for i in range(2)]
            qsc = [A.alloc([128, 2, R_FULL], BF16, f"qsc{i}") for i in range(2)]
            yb = [A.alloc([128, 512], BF16, f"yb{i}") for i in range(2)]
            junk = A.alloc([128, 512], BF16, "junkr")
            ssq = [A.alloc([128, 1], F32, f"ssqr{i}") for i in range(2)]
            cn = {"qs": 0, "ks": 0, "pT": 0, "ow": 0}
            k_v = k_d.rearrange("(t p) c -> p t c", p=128)
            v_v = v_d.rearrange("(t p) c -> p t c", p=128)
            sg_v = sg_d.rearrange("(t p) c -> p t c", p=128)
            KCH = [(0, 9), (9, 18), (18, 26), (26, 34)]
            SCH = [(0, 10), (10, 19)]
            kkc = [Buf(kk.ap, f"kkc{i}") for i in range(4)]
            vvc = [Buf(vv.ap, f"vvc{i}") for i in range(4)]
            sgc = [Buf(sg.ap, f"sgc{i}") for i in range(2)]

            def kch(t):
                return [i for i, (a0, a1) in enumerate(KCH) if a0 <= t < a1][0]

            def sch(t):
                return 0 if t < 10 else 1
            stage = [A.alloc([128, 4, 512], BF16, f"stg{i}") for i in range(2)]
            wconvert(WS["ret_w_out"], wview(ret_w_out), stage, nhalf=4)
            wconvert(WS["mlp_w1_0"], wview(mlp_w1[0]), stage, nhalf=2)
            wconvert(WS["mlp_w2_0"], wview(mlp_w2[0]), stage, nhalf=4)

            par = {0: 0, 1: 0}
            kvp = {}

            def plan_kv(t, di, banks=(0, 4)):
                kb = ks[cn["ks"] % 5]
                cn["ks"] += 1
                S.op("vector", lambda e: e.tensor_scalar(out=kb.ap, in0=kk.ap[:, t, :], scalar1=dsc.ap[:, di:di + 1],
                                                         scalar2=None, op0=ALU.mult), [kkc[kch(t)], dsc], [kb])
                pis = []
                for dc in range(2):
                    pi = next_ps(*banks)
                    pis.append(pi)
                    S.op("tensor", lambda e, pi=pi, dc=dc: e.matmul(
                        PS[pi].ap, lhsT=kb.ap[:, dc * 128:(dc + 1) * 128], rhs=vv.ap[:, t, :], start=True, stop=True),
                        [kb, vvc[kch(t)]], [PS[pi]])
                kvp[(t, di)] = pis

            def state_update(t, di, dst):
                pis = kvp.pop((t, di))
                for dc in range(2):
                    pi = pis[dc]
                    S.op("vector", lambda e, pi=pi, dc=dc: e.scalar_tensor_tensor(
                        out=Sst[di][dc].ap, in0=Sst[di][dc].ap, scalar=dsc.ap[:, 2 + di:3 + di],
                        in1=PS[pi].ap, op0=ALU.mult, op1=ALU.add), [Sst[di][dc], dsc, PS[pi]], [Sst[di][dc]])
                    if dst is not None:
                        S.op("scalar", lambda e, dc=dc: e.activation(out=dst[1][dc], in_=Sst[di][dc].ap,
                                                                     func=AF.Copy), [Sst[di][dc]], [dst[0][dc]])

            class _QV:
                def __init__(self, buf, ap):
                    self.buf, self.ap = buf, ap

            def q_scaled(t, di):
                return _QV(qsc[di], qsc[di].ap[:, :, t * 128:(t + 1) * 128])

            for h in range(8):
                for ci in [0, 3, 2, 1]:
                    t0, t1 = KCH[ci]
                    S.dma("sync", kk.ap[:, t0:t1, :], k_v[:, t0:t1, h * 256:(h + 1) * 256], reads=[dbuf("k_d")],
                          writes=[kkc[ci]])
                    S.dma("sync", vv.ap[:, t0:t1, :], v_v[:, t0:t1, h * 512:(h + 1) * 512], reads=[dbuf("v_d")],
                          writes=[vvc[ci]])
                S.dma("sync", qT.ap, qT_d[2 * h:2 * h + 2].rearrange("c p t -> p c t"), reads=[dbuf("qT_d")],
                      writes=[qT])
                S.dma("sync", kT.ap, kT_d[2 * h:2 * h + 2].rearrange("c p t -> p c t"), reads=[dbuf("kT_d")],
                      writes=[kT])
                for ci in range(2):
                    t0, t1 = SCH[ci]
                    S.dma("sync", sg.ap[:, t0:t1, :], sg_v[:, t0:t1, h * 512:(h + 1) * 512], reads=[dbuf("sg_d")],
                          writes=[sgc[ci]])
                lgf = P_lg.ap[:, h:h + 1]
                lgb = P_lg.ap[:, 8 + h:9 + h]
                S.op("scalar", lambda e, lgf=lgf: e.activation(out=maskc.ap, in_=cst(128, 256), func=AF.Exp, scale=lgf),
                     [P_c, P_lg], [maskc])
                S.op("vector", lambda e: e.tensor_tensor(out=maskc.ap, in0=maskc.ap, in1=cst(256, 384), op=ALU.mult),
                     [maskc, P_c], [maskc])
                S.op("scalar", lambda e, lgb=lgb: e.activation(out=mtmp.ap, in_=cst(384, 512), func=AF.Exp, scale=lgb),
                     [P_c, P_lg], [mtmp])
                S.op("vector", lambda e: e.tensor_tensor(out=mtmp.ap, in0=mtmp.ap, in1=cst(512, 640), op=ALU.mult),
                     [mtmp, P_c], [mtmp])
                S.op("vector", lambda e: e.tensor_tensor(out=maskc.ap, in0=maskc.ap, in1=mtmp.ap, op=ALU.add),
                     [maskc, mtmp], [maskc])
                S.op("scalar", lambda e, lgf=lgf: e.activation(out=dq[0].ap, in_=cst(640, 768), func=AF.Exp, scale=lgf),
                     [P_c, P_lg], [dq[0]])
                S.op("scalar", lambda e, lgb=lgb: e.activation(out=dq[1].ap, in_=cst(768, 896), func=AF.Exp, scale=lgb),
                     [P_c, P_lg], [dq[1]])
                S.op("scalar", lambda e, lgf=lgf: e.activation(out=dsc.ap[:, 0:1], in_=cst(1024, 1025), func=AF.Exp,
                                                               scale=lgf), [P_c, P_lg], [dsc])
                S.op("scalar", lambda e, lgb=lgb: e.activation(out=dsc.ap[:, 1:2], in_=cst(1025, 1026), func=AF.Exp,
                                                               scale=lgb), [P_c, P_lg], [dsc])
                S.op("scalar", lambda e, lgf=lgf: e.activation(out=dsc.ap[:, 2:3], in_=cst(1026, 1027), func=AF.Exp,
                                                               scale=lgf), [P_c, P_lg], [dsc])
                S.op("scalar", lambda e, lgb=lgb: e.activation(out=dsc.ap[:, 3:4], in_=cst(1026, 1027), func=AF.Exp,
                                                               scale=lgb), [P_c, P_lg], [dsc])
                for di in range(2):
                    for dc in range(2):
                        S.op("vector", lambda e, di=di, dc=dc: e.memset(Sst[di][dc].ap, 0.0), [], [Sst[di][dc]])
                for dc in range(2):
                    sb0 = Sbf[0][par[0]][dc]
                    S.op("vector", lambda e, sb0=sb0: e.memset(sb0.ap, 0.0), [], [sb0])
                for di in range(2):
                    S.op("vector", lambda e, di=di: e.tensor_tensor(
                        out=qsc[di].ap.rearrange("p c (t i) -> p (c t) i", i=128),
                        in0=qT.ap.rearrange("p c (t i) -> p (c t) i", i=128),
                        in1=dq[di].ap.unsqueeze(1).to_broadcast([128, 2 * NT_FULL, 128]), op=ALU.mult),
                        [qT, dq[di]], [qsc[di]])
                seq = [1, 0] + list(range(NT_ALL - 1, 1, -1))
                S.op("vector", lambda e: e.memset(snap.ap[:, seq[0], :], 0.0), [], [snap])
                plan_kv(seq[0], 1, (0, 6))
                plan_kv(seq[1], 1, (0, 6))
                for idx, t in enumerate(seq[:-1]):
                    nt = seq[idx + 1]
                    dst = None
                    if nt < NT_FULL:
                        dst = ([snap, snap], [snap.ap[:, nt, 0:512], snap.ap[:, nt, 512:1024]])
                    if idx + 2 < len(seq) - 1:
                        plan_kv(seq[idx + 2], 1, (0, 6))
                    state_update(t, 1, dst)
                pend = []
                plan_kv(0, 0)
                for t in range(NT_FULL):
                    pi = next_ps(4, 7)
                    for dc in range(2):
                        S.op("tensor", lambda e, pi=pi, dc=dc, t=t: e.matmul(
                            PS[pi].ap[:, 0:128], lhsT=kT.ap[:, dc, t * 128:(t + 1) * 128],
                            rhs=qT.ap[:, dc, t * 128:(t + 1) * 128], start=(dc == 0), stop=(dc == 1)),
                            [kT, qT], [PS[pi]])
                    pb = pT[cn["pT"] % 2]
                    cn["pT"] += 1
                    S.op("vector", lambda e, pi=pi, pb=pb: e.tensor_tensor(out=pb.ap, in0=PS[pi].ap[:, 0:128],
                                                                           in1=maskc.ap, op=ALU.mult),
                         [PS[pi], maskc], [pb])
                    qb = q_scaled(t, 0)
                    qbb = q_scaled(t, 1)
                    po = next_ps(4, 7)
                    sb = Sbf[0][par[0]]
                    S.op("tensor", lambda e, po=po, pb=pb, t=t: e.matmul(PS[po].ap, lhsT=pb.ap, rhs=vv.ap[:, t, :],
                                                                         start=True, stop=False),
                         [pb, vvc[kch(t)]], [PS[po]])
                    for dc in range(2):
                        S.op("tensor", lambda e, po=po, dc=dc, qb=qb, sb=sb: e.matmul(
                            PS[po].ap, lhsT=qb.ap[:, dc, :], rhs=sb[dc].ap, start=False, stop=False),
                            [qb.buf, sb[dc]], [PS[po]])
                    for dc in range(2):
                        S.op("tensor", lambda e, po=po, dc=dc, qbb=qbb, t=t: e.matmul(
                            PS[po].ap, lhsT=qbb.ap[:, dc, :], rhs=snap.ap[:, t, dc * 512:(dc + 1) * 512], start=False,
                            stop=(dc == 1)), [qbb.buf, snap], [PS[po]])
                    if t != NT_FULL - 1:
                        np_ = 1 - par[0]
                        state_update(t, 0, (Sbf[0][np_], [Sbf[0][np_][0].ap, Sbf[0][np_][1].ap]))
                        par[0] = np_
                        if t + 1 != NT_FULL - 1:
                            plan_kv(t + 1, 0)
                    y = yb[cn["ow"] % 2]
                    sq = ssq[cn["ow"] % 2]
                    cn["ow"] += 1
                    o = PS[po]
                    S.op("scalar", lambda e, o=o, sq=sq: e.activation(out=junk.ap, in_=o.ap, func=AF.Square,
                                                                      scale=float(512 ** -0.5), accum_out=sq.ap),
                         [o], [junk, sq])
                    S.op("scalar", lambda e, sq=sq: e.activation(out=sq.ap, in_=sq.ap, func=AF.Sqrt,
                                                                 bias=P_eps.ap[:, 0:1]), [sq, P_eps], [sq])
                    while pend:
                        pend.pop(0)()

                    def tr(y=y, t=t, o=o, sq=sq):
                        S.op("vector", lambda e: e.reciprocal(out=sq.ap, in_=sq.ap), [sq], [sq])
                        S.op("vector", lambda e: e.scalar_tensor_tensor(
                            out=y.ap, in0=o.ap, scalar=sq.ap[:, 0:1], in1=sg.ap[:, t, :], op0=ALU.mult, op1=ALU.mult),
                            [o, sq, sgc[sch(t)]], [y])
                        pj = next_ps(7, 8)
                        for j in range(4):
                            S.op("tensor", lambda e, j=j, pj=pj, y=y: e.transpose(
                                ps_bf[pj][:, j * 128:(j + 1) * 128], y.ap[:, j * 128:(j + 1) * 128], P_identb.ap),
                                [y, P_identb], [PS[pj]])
                        S.op("scalar", lambda e, pj=pj, t=t: e.activation(
                            out=yT.ap[:, :, t * 128:(t + 1) * 128],
                            in_=ps_bf[pj][:, 0:512].rearrange("p (c t) -> p c t", c=4), func=AF.Copy), [PS[pj]], [yT])
                    pend.append(tr)
                while pend:
                    pend.pop(0)()
                S.dma("sync", yT_d[4 * h:4 * h + 4].rearrange("c p t -> p c t"), yT.ap, reads=[yT],
                      writes=[dbuf("yT_d")])
            S.barrier()

        def make_residual(layer, gcol0, x_src, x_dst, row_of, nxb=8):
            gb = [[A.alloc([128, 512], F32, f"gb{r}{i}") for i in range(2)] for r in range(2)]
            xb = [A.alloc([128, 512], F32, f"xb{i}") for i in range(nxb)]
            tmp = [A.alloc([128, 512], F32, f"rtmp{i}") for i in range(2)]
            st = {"gb": 0, "xb": 0, "tmp": 0, "cur": {}, "g": {}}

            def pre(grp, bi, c0, ncols):
                rows = sorted(set(row_of(t) for t in grp))
                for r in rows:
                    b = gb[r][st["gb"] % 2]
                    S.dma("sync", b.ap[:, 0:ncols],
                          mod_d[layer, r:r + 1, gcol0 + c0:gcol0 + c0 + ncols].partition_broadcast(128),
                          reads=[dbuf("mod_d")], writes=[b])
                    st["g"][r] = b
                st["gb"] += 1
                for gi, t in enumerate(grp):
                    b = xb[st["xb"] % nxb]
                    st["xb"] += 1
                    S.dma("sync", b.ap[:, 0:ncols], x_src(t)[:, c0:c0 + ncols], writes=[b])
                    st["cur"][gi] = b

            def evac(t, gi, bi, c0, ncols, pi):
                b = st["cur"][gi]
                g = st["g"][row_of(t)]
                tm = tmp[st["tmp"] % 2]
                st["tmp"] += 1
                S.op("vector", lambda e: e.tensor_tensor(out=tm.ap[:, 0:ncols], in0=PS[pi].ap[:, 0:ncols],
                                                         in1=g.ap[:, 0:ncols], op=ALU.mult), [PS[pi], g], [tm])
                S.op("vector", lambda e: e.tensor_tensor(out=b.ap[:, 0:ncols], in0=b.ap[:, 0:ncols],
                                                         in1=tm.ap[:, 0:ncols], op=ALU.add), [b, tm], [b])
                S.dma("sync", x_dst(t)[:, c0:c0 + ncols], b.ap[:, 0:ncols], reads=[b])

            return pre, evac

        def rows_full(dram):
            return lambda t: dram[t * 128:(t + 1) * 128, :]

        def rows_own(dram):
            return lambda t: dram[(t - 2) * 128:(t - 1) * 128, :]

        full_groups = [[0, 1, 2, 3], [4, 5, 6, 7], [8, 9, 10, 11], [12, 13, 14, 15], [16, 17, 18]]
        own_groups = [[2, 3, 4, 5], [6, 7, 8, 9], [10, 11, 12, 13], [14, 15, 16, 17]]
        blocks4 = lambda grp: [(i * 512, 512) for i in range(4)]

        def phase_ret_out():
            A.off = persist_off
            aTgs = [A.alloc([128, 32, 512], BF16, f"yTg{i}") for i in range(2)]
            wbufs = [A.alloc([128, 16, 512], BF16, f"wo{i}") for i in range(3)]
            pre, evac = make_residual(0, 2 * D, rows_full(xin), rows_full(x1_d), lambda t: 1 if t < 2 else 0)

            def prep(grp, aTg):
                r0 = grp[0] * 128
                n = len(grp) * 128
                return [lambda: S.dma("sync", aTg.ap[:, :, 0:n], yT_d[:, :, r0:r0 + n].rearrange("c p t -> p c t"),
                                      reads=[dbuf("yT_d")], writes=[aTg])]

            gemm_tok(full_groups, prep, WS["ret_w_out"], ret_w_out, 32, blocks4, evac, wbufs, aTgs, kparts=2,
                     pre=pre)
            S.barrier()

        def phase_mlp(layer, groups, x_src, x_dst, row_of):
            A.off = persist_off
            h1T = A.alloc([128, 64, 512], BF16, "h1T")
            wk = alloc_norm_work()
            xts = [A.alloc([128, D], F32, f"mxt{i}") for i in range(2)]
            a16 = A.alloc([128, 16, 512], BF16, "a16")
            w1b = [A.alloc([128, 16, 256], BF16, f"w1b{i}") for i in range(2)]
            w2b = [A.alloc([128, 16, 512], BF16, f"w2b{i}") for i in range(3)]
            rl = [A.alloc([128, 512], F32, f"rl{i}") for i in range(2)]
            pre, evac = make_residual(layer, 5 * D, x_src, x_dst, row_of, nxb=6)
            cn = {"x": 0, "w1": 0, "rl": 0}
            s1 = WStream(WS[f"mlp_w1_{layer}"], mlp_w1[layer], w1b, [(0, fb) for g in groups for fb in range(32)])

            def prep_norm(grp, h1):
                tasks = []
                for gi, t in enumerate(grp):
                    def t1(gi=gi, t=t):
                        xt = xts[cn["x"] % 2]
                        cn["x"] += 1
                        S.dma("sync", xt.ap, x_src(t), writes=[xt])
                        norm_pre(xt, wk)

                    def t2(gi=gi, t=t):
                        norm_tr(row_of(t), layer * 2 + 1, a16, gi, wk)
                    tasks += [t1, t2]
                return tasks

            def prep(grp, h1):
                n = len(grp) * 128
                for fb in range(32):
                    w = s1.get(cn["w1"])
                    cn["w1"] += 1
                    for sub in range(2):
                        pi = next_ps(0, 4)
                        for kc in range(16):
                            S.op("tensor", lambda e, pi=pi, kc=kc, w=w, sub=sub: e.matmul(
                                PS[pi].ap[:, 0:n], lhsT=w.ap[:, kc, sub * 128:(sub + 1) * 128],
                                rhs=a16.ap[:, kc, 0:n], start=(kc == 0), stop=(kc == 15)), [w, a16], [PS[pi]])
                        r = rl[cn["rl"] % 2]
                        cn["rl"] += 1
                        S.op("scalar", lambda e, pi=pi, r=r: e.activation(out=r.ap[:, 0:n], in_=PS[pi].ap[:, 0:n],
                                                                          func=AF.Relu), [PS[pi]], [r])
                        S.op("vector", lambda e, r=r, fb=fb, sub=sub: e.tensor_tensor(
                            out=h1.ap[:, fb * 2 + sub, 0:n], in0=r.ap[:, 0:n], in1=r.ap[:, 0:n], op=ALU.mult),
                            [r], [h1])

            gemm_tok(groups, prep_norm, WS[f"mlp_w2_{layer}"], mlp_w2[layer], 64, blocks4, evac, w2b, [h1T], kparts=4,
                     pre=pre, prep_main=prep)
            S.barrier()

        def phase_att_proj():
            A.off = persist_off
            wk = alloc_norm_work()
            xts = [A.alloc([128, D], F32, f"xt{i}") for i in range(2)]
            aTgs = [A.alloc([128, 16, 512], BF16, f"aTg{i}") for i in range(2)]
            wbufs = [A.alloc([128, 16, 512], BF16, f"w{i}") for i in range(3)]
            tabs = [A.alloc([128, 1024], F32, f"tab{i}") for i in range(4)]
            xs_b = [A.alloc([128, 512], F32, f"xs{i}") for i in range(2)]
            rw = {"t1": A.alloc([128, 512], F32, "t1"), "t2": A.alloc([128, 512], F32, "t2")}
            ob = [A.alloc([128, 512], BF16, f"ob{i}") for i in range(6)]
            qTg = A.alloc([128, 16, 512], BF16, "qTg")
            kTg = A.alloc([128, 4, 512], BF16, "kTg")
            cnt = {"x": 0, "xs": 0, "ob": 0}
            tabmap = {}

            def prep(grp, aTg):
                tasks = []
                for gi, t in enumerate(grp):
                    def t1(gi=gi, t=t):
                        xt = xts[cnt["x"] % 2]
                        cnt["x"] += 1
                        S.dma("sync", xt.ap, x2_d[t * 128:(t + 1) * 128, :], writes=[xt])
                        tb = tabs[gi]
                        S.dma("sync", tb.ap, rope_a[t * 128:(t + 1) * 128, :], writes=[tb])
                        tabmap[t] = tb
                        norm_pre(xt, wk)

                    def t2(gi=gi, t=t):
                        norm_tr(1 if t < 2 else 0, 2, aTg, gi, wk)
                    tasks += [t1, t2]
                return tasks

            def blocks(grp):
                return [(i * 512, 512) for i in range(6)]

            def evac(t, gi, bi, c0, ncols, pi):
                r0 = t * 128
                o = ob[cnt["ob"] % 6]
                cnt["ob"] += 1
                if c0 < 2560:
                    isq = c0 < 2048
                    xs = xs_b[cnt["xs"] % 2]
                    cnt["xs"] += 1
                    S.op("scalar", lambda e: e.activation(out=xs.ap, in_=PS[pi].ap, func=AF.Copy), [PS[pi]], [xs])
                    rope_block(xs, tabmap[t], gi, (o, o.ap), 32, rw)
                    tg = qTg if isq else kTg
                    cb = (c0 // 128) if isq else 0

                    def tr(o=o, tg=tg, cb=cb, gi=gi):
                        pj = next_ps(4, 8)
                        for j in range(4):
                            S.op("tensor", lambda e, j=j, pj=pj, o=o: e.transpose(
                                ps_bf[pj][:, j * 128:(j + 1) * 128], o.ap[:, j * 128:(j + 1) * 128], P_identb.ap),
                                [o, P_identb], [PS[pj]])
                        S.op("scalar", lambda e, pj=pj, tg=tg, cb=cb, gi=gi: e.activation(
                            out=tg.ap[:, cb:cb + 4, gi * 128:(gi + 1) * 128],
                            in_=ps_bf[pj][:, 0:512].rearrange("p (c t) -> p c t", c=4), func=AF.Copy), [PS[pj]], [tg])
                    GCTX["defer"](tr)
                else:
                    S.op("scalar", lambda e: e.activation(out=o.ap, in_=PS[pi].ap, func=AF.Copy), [PS[pi]], [o])
                    S.dma("sync", av_d[r0:r0 + 128, :], o.ap, reads=[o], writes=[dbuf("av_d")])

            def post(grp):
                r0 = grp[0] * 128
                n = len(grp) * 128
                S.dma("sync", aq_d[:, :, r0:r0 + n].rearrange("c p t -> p c t"), qTg.ap[:, :, 0:n], reads=[qTg],
                      writes=[dbuf("aq_d")])
                S.dma("sync", akT_d[:, :, r0:r0 + n].rearrange("c p t -> p c t"), kTg.ap[:, :, 0:n], reads=[kTg],
                      writes=[dbuf("akT_d")])

            gemm_tok(full_groups, prep, WS["attn_w_in"], attn_w_in, 16, blocks, evac, wbufs, aTgs, post, pops=2)
            S.barrier()

        def phase_att():
            A.off = persist_off
            SCALE = float(128 ** -0.5)
            qT = A.alloc([128, 4, R_FULL], BF16, "aqT")
            kT = A.alloc([128, R_FULL], BF16, "akT")
            vv = A.alloc([128, NT_FULL, 128], BF16, "avv")
            oTh = A.alloc([128, 4, 2048], BF16, "oTh")
            am = A.alloc([128, 768], F32, "am")
            sm = [A.alloc([128, 640], F32, f"sm{i}") for i in range(4)]
            pn = [A.alloc([128, 640], BF16, f"pn{i}") for i in range(4)]
            pTa = [A.alloc([128, 5, 512], BF16, f"pTa{i}") for i in range(2)]
            sc = [A.alloc([128, 8], F32, f"asc{i}") for i in range(4)]
            cn = {"i": 0, "pa": 0}
            negsink = A.alloc([128, 16], F32, "negsink")
            S.op("vector", lambda e: e.tensor_scalar(out=negsink.ap, in0=P_sink.ap, scalar1=-1.0, scalar2=None,
                                                     op0=ALU.mult), [P_sink], [negsink])
            stage = [A.alloc([128, 16, 512], BF16, f"stg{i}") for i in range(3)]
            wconvert(WS["attn_w_out"], wview(attn_w_out), stage)
            wconvert(WS["mlp_w1_1"], wview(mlp_w1[1]), stage)
            wconvert(WS["mlp_w2_1"], wview(mlp_w2[1]), stage)
            S.dma("sync", am.ap, amask, writes=[am])
            v_v = av_d.rearrange("(t p) c -> p t c", p=128)
            for kvh in range(4):
                S.dma("sync", qT.ap, aq_d[kvh * 4:kvh * 4 + 4].rearrange("c p t -> p c t"), reads=[dbuf("aq_d")],
                      writes=[qT])
                S.dma("sync", kT.ap, akT_d[kvh], reads=[dbuf("akT_d")], writes=[kT])
                S.dma("sync", vv.ap, v_v[:, :, kvh * 128:(kvh + 1) * 128], reads=[dbuf("av_d")], writes=[vv],
                      slow=True)
                stA, stB, stC = [], [], []
                for n in range(16):
                    t = n + 2
                    r0 = t * 128
                    pa = pTa[cn["pa"] % 2]
                    cn["pa"] += 1
                    for g in range(4):
                        hq = kvh * 4 + g
                        i2 = cn["i"] % 4
                        cn["i"] += 1
                        smb, pnb, scb = sm[i2], pn[i2], sc[i2]
                        m0 = 384 if n == 0 else 0

                        def fa(g=g, r0=r0, smb=smb, scb=scb, hq=hq, m0=m0):
                            p1 = next_ps(0, 4)
                            p2 = next_ps(0, 4)
                            S.op("tensor", lambda e: e.matmul(
                                PS[p1].ap[:, 0:384], lhsT=qT.ap[:, g, r0:r0 + 128], rhs=kT.ap[:, r0 - 128:r0 + 256],
                                start=True, stop=True), [qT, kT], [PS[p1]])
                            S.op("tensor", lambda e: e.matmul(
                                PS[p2].ap[:, 0:256], lhsT=qT.ap[:, g, r0:r0 + 128], rhs=kT.ap[:, 0:256],
                                start=True, stop=True), [qT, kT], [PS[p2]])
                            S.op("vector", lambda e: e.tensor_tensor(
                                out=smb.ap[:, 0:384], in0=PS[p1].ap[:, 0:384], in1=am.ap[:, m0:m0 + 384], op=ALU.add),
                                [PS[p1], am], [smb])
                            S.op("scalar", lambda e: e.activation(out=smb.ap[:, 384:640], in_=PS[p2].ap[:, 0:256],
                                                                  func=AF.Copy), [PS[p2]], [smb])
                            S.op("vector", lambda e: e.reduce_max(out=scb.ap[:, 0:1], in_=smb.ap, axis=AX.X),
                                 [smb], [scb])
                            S.op("vector", lambda e: e.tensor_scalar(
                                out=scb.ap[:, 2:3], in0=scb.ap[:, 0:1], scalar1=-SCALE, scalar2=negsink.ap[:, hq:hq + 1],
                                op0=ALU.mult, op1=ALU.min), [scb, negsink], [scb])
                            S.op("scalar", lambda e: e.activation(
                                out=smb.ap, in_=smb.ap, func=AF.Exp, bias=scb.ap[:, 2:3], scale=SCALE,
                                accum_out=scb.ap[:, 3:4]), [smb, scb], [smb, scb])
                            S.op("scalar", lambda e: e.activation(
                                out=scb.ap[:, 4:5], in_=P_sink.ap[:, hq:hq + 1], func=AF.Exp, bias=scb.ap[:, 2:3]),
                                [scb, P_sink], [scb])

                        def fb(smb=smb, pnb=pnb, scb=scb):
                            S.op("vector", lambda e: e.tensor_tensor(out=scb.ap[:, 5:6], in0=scb.ap[:, 3:4],
                                                                     in1=scb.ap[:, 4:5], op=ALU.add), [scb], [scb])
                            S.op("vector", lambda e: e.reciprocal(out=scb.ap[:, 5:6], in_=scb.ap[:, 5:6]),
                                 [scb], [scb])
                            S.op("vector", lambda e: e.tensor_scalar(
                                out=pnb.ap, in0=smb.ap, scalar1=scb.ap[:, 5:6], scalar2=None, op0=ALU.mult),
                                [smb, scb], [pnb])

                        def fc(pnb=pnb, pa=pa, g=g, n=n, t=t):
                            pj = next_ps(4, 8)
                            for j in range(5):
                                S.op("tensor", lambda e, j=j: e.transpose(
                                    ps_bf[pj][:, j * 128:(j + 1) * 128], pnb.ap[:, j * 128:(j + 1) * 128],
                                    P_identb.ap), [pnb, P_identb], [PS[pj]])
                            S.op("scalar", lambda e: e.activation(
                                out=pa.ap[:, :, g * 128:(g + 1) * 128],
                                in_=ps_bf[pj][:, 0:640].rearrange("p (c t) -> p c t", c=5), func=AF.Copy),
                                [PS[pj]], [pa])
                            if g == 3:
                                po = next_ps(0, 4)
                                vt = [t - 1, t, t + 1, 0, 1]
                                for j in range(5):
                                    S.op("tensor", lambda e, j=j: e.matmul(
                                        PS[po].ap, lhsT=vv.ap[:, vt[j], :], rhs=pa.ap[:, j, :], start=(j == 0),
                                        stop=(j == 4)), [vv, pa], [PS[po]])
                                S.op("scalar", lambda e: e.activation(
                                    out=oTh.ap[:, :, n * 128:(n + 1) * 128],
                                    in_=PS[po].ap.rearrange("p (c t) -> p c t", c=4), func=AF.Copy), [PS[po]], [oTh])
                        stA.append(fa)
                        stB.append(fb)
                        stC.append(fc)
                NI = len(stA)
                for st_ in range(NI + 2):
                    if st_ < NI:
                        stA[st_]()
                    if 0 <= st_ - 1 < NI:
                        stB[st_ - 1]()
                    if 0 <= st_ - 2 < NI:
                        stC[st_ - 2]()
                S.dma("sync", oT_d[kvh * 4:kvh * 4 + 4].rearrange("c p t -> p c t"), oTh.ap, reads=[oTh],
                      writes=[dbuf("oT_d")])
            S.barrier()

        def phase_att_out():
            A.off = persist_off
            aTgs = [A.alloc([128, 16, 512], BF16, f"oTg{i}") for i in range(2)]
            wbufs = [A.alloc([128, 16, 512], BF16, f"wo{i}") for i in range(3)]
            pre, evac = make_residual(1, 2 * D, rows_full(x2_d), rows_own(x3_d), lambda t: 0)

            def prep(grp, aTg):
                c0 = (grp[0] - 2) * 128
                n = len(grp) * 128
                return [lambda: S.dma("sync", aTg.ap[:, :, 0:n], oT_d[:, :, c0:c0 + n].rearrange("c p t -> p c t"),
                                      reads=[dbuf("oT_d")], writes=[aTg])]

            gemm_tok(own_groups, prep, WS["attn_w_out"], attn_w_out, 16, blocks4, evac, wbufs, aTgs, pre=pre)
            S.barrier()

        def phase_final():
            A.off = persist_off
            fg = A.alloc([128, D], F32, "fg")
            xts = [A.alloc([128, D], F32, f"fx{i}") for i in range(3)]
            junk = A.alloc([128, D], BF16, "fjunk")
            ssq = [A.alloc([128, 1], F32, f"fssq{i}") for i in range(2)]
            S.dma("sync", fg.ap, final_g.partition_broadcast(128), writes=[fg])
            for n in range(16):
                xt = xts[n % 3]
                sq = ssq[n % 2]
                S.dma("sync", xt.ap, x4_d[n * 128:(n + 1) * 128, :], writes=[xt])
                S.op("scalar", lambda e, xt=xt, sq=sq: e.activation(out=junk.ap, in_=xt.ap, func=AF.Square,
                                                                    scale=float(D ** -0.5), accum_out=sq.ap),
                     [xt], [junk, sq])
                S.op("vector", lambda e, sq=sq: e.tensor_scalar(out=sq.ap, in0=sq.ap, scalar1=EPS, scalar2=None,
                                                                op0=ALU.add), [sq], [sq])
                S.op("scalar", lambda e, sq=sq: e.activation(out=sq.ap, in_=sq.ap, func=AF.Sqrt), [sq], [sq])
                S.op("vector", lambda e, sq=sq: e.reciprocal(out=sq.ap, in_=sq.ap), [sq], [sq])
                S.op("vector", lambda e, xt=xt, sq=sq: e.scalar_tensor_tensor(
                    out=xt.ap, in0=xt.ap, scalar=sq.ap[:, 0:1], in1=fg.ap, op0=ALU.mult, op1=ALU.mult),
                    [xt, sq, fg], [xt])
                S.dma("sync", out[n * 128:(n + 1) * 128, :], xt.ap, reads=[xt])
            S.barrier()

        def phase_bench(mode):
            A.off = persist_off
            aTg = A.alloc([128, 16, 512], BF16, "b_aTg")
            wb = [A.alloc([128, 16, 512], BF16, f"b_w{i}") for i in range(2)]
            ob = [A.alloc([128, 512], BF16, f"b_o{i}") for i in range(2)]
            S.op("vector", lambda e: e.memset(aTg.ap, 0.5), [], [aTg])
            for w in wb:
                S.op("vector", lambda e, w=w: e.memset(w.ap, 0.25), [], [w])
            for it in range(160):
                w = wb[it % 2]
                if mode >= 2:
                    S.dma("gpsimd", w.ap, ret_w_in[:, (it % 24) * 512:(it % 24 + 1) * 512].rearrange(
                        "(c p) n -> p c n", p=128), writes=[w])
                pi = next_ps(0, 4)
                for kc in range(16):
                    S.op("tensor", lambda e, pi=pi, kc=kc, w=w, it=it: e.matmul(
                        PS[pi].ap, lhsT=aTg.ap[:, kc, (it % 4) * 128:(it % 4 + 1) * 128], rhs=w.ap[:, kc, :],
                        start=(kc == 0) or mode == 0, stop=(kc == 15) or mode == 0), [aTg, w], [PS[pi]])
                o = ob[it % 2]
                S.op("scalar", lambda e, pi=pi, o=o: e.activation(out=o.ap, in_=PS[pi].ap, func=AF.Copy), [PS[pi]], [o])
            S.barrier()

        phases = {
            "ada": phase_ada, "ret_proj": phase_ret_proj, "ret": phase_ret, "ret_out": phase_ret_out,
            "mlp0": lambda: phase_mlp(0, full_groups, rows_full(x1_d), rows_full(x2_d), lambda t: 1 if t < 2 else 0),
            "att_proj": phase_att_proj, "att": phase_att, "att_out": phase_att_out,
            "mlp1": lambda: phase_mlp(1, own_groups, rows_own(x3_d), rows_own(x4_d), lambda t: 0),
            "final": phase_final,
        }
        order = ["ada", "ret_proj", "ret", "ret_out", "mlp0", "att_proj", "att", "att_out", "mlp1", "final"]
        stop_after = None
        for d in dbg:
            if d.startswith("stop:"):
                stop_after = d[5:]
        for d in dbg:
            if d.startswith("bench:"):
                order = []
                phase_bench(int(d[6:]))
        for ph in order:
            phases[ph]()
            if stop_after == ph:
                break

        S.barrier()
        for e in Sched.ENGS:
            S.op(e, lambda en: en.nop(), [], [])

        S.finalize(eng_sems, dma_sems)
        with nc.Block() as block:
            @block.sync
            def _(e):
                S.emit("sync", e)

            @block.gpsimd
            def _(e):
                S.emit("gpsimd", e)

            @block.tensor
            def _(e):
                S.emit("tensor", e)

            @block.vector
            def _(e):
                S.emit("vector", e)

            @block.scalar
            def _(e):
                S.emit("scalar", e)

    return nc


NEG_MASK = -30000.0


def _const_tables():
    c = np.zeros((128, 1032), np.float32)
    p = np.arange(128, dtype=np.float32)
    jj = p[:, None]
    ii = p[None, :]
    c[:, 0:128] = np.eye(128, dtype=np.float32)
    c[:, 128:256] = np.maximum(ii - jj, 0.0)
    c[:, 256:384] = (ii >= jj).astype(np.float32)
    c[:, 384:512] = np.maximum(jj - ii, 0.0)
    c[:, 512:640] = (jj >= ii).astype(np.float32)
    c[:, 640:768] = ii + 1.0
    c[:, 768:896] = 128.0 - ii
    c[:, 1024] = 127.0 - p
    c[:, 1025] = p
    c[:, 1026] = 128.0
    am = np.zeros((128, 768), np.float32)
    prev = np.where(jj.T >= ii.T, 0.0, NEG_MASK)
    i_ = p[:, None]
    j_ = p[None, :]
    prev = np.where(j_ >= i_, 0.0, NEG_MASK).astype(np.float32)
    nxt = np.where(j_ <= i_, 0.0, NEG_MASK).astype(np.float32)
    am[:, 0:128] = prev
    am[:, 256:384] = nxt
    am[:, 384:512] = NEG_MASK
    am[:, 640:768] = nxt
    return c, am


def _rope_tables(pos):
    f32 = np.float32
    L = pos.shape[0]
    inv_r = (f32(10000.0) ** (-np.arange(0, 256, 2, dtype=f32) / f32(256))).astype(f32)
    ang = pos.astype(f32)[:, None] * inv_r[None, :]
    cr, sr = np.cos(ang).astype(f32), np.sin(ang).astype(f32)
    rr = np.zeros((R_ALL, 1024), f32)
    rr[:NCTX, 0:512] = 1.0
    rr[NCTX:, 0:512] = np.tile(cr, (1, 4))
    rr[NCTX:, 512:1024] = np.tile(np.concatenate([-sr, sr], axis=1), (1, 2))
    inv_a = (f32(10000.0) ** (-np.arange(0, 64, 2, dtype=f32) / f32(64))).astype(f32)
    rows = (pos // 64).astype(f32)
    cols = (pos % 64).astype(f32)
    ar = rows[:, None] * inv_a[None, :]
    ac = cols[:, None] * inv_a[None, :]
    c128 = np.concatenate([np.cos(ar), np.cos(ar), np.cos(ac), np.cos(ac)], axis=1).astype(f32)
    s128 = np.concatenate([-np.sin(ar), np.sin(ar), -np.sin(ac), np.sin(ac)], axis=1).astype(f32)
    ra = np.zeros((R_FULL, 1024), f32)
    ra[:NCTX, 0:512] = 1.0
    n = R_FULL - NCTX
    ra[NCTX:, 0:512] = np.tile(c128[:n], (1, 4))
    ra[NCTX:, 512:1024] = np.tile(s128[:n], (1, 4))
    return rr, ra


def make_in_maps(inp, cores=range(8)):
    f32 = np.float32
    g = {k: np.asarray(v) for k, v in inp.items()}
    consts, amask = _const_tables()
    shared = {
        "ada_w": np.ascontiguousarray(g["ada_w"], f32), "ada_b": np.ascontiguousarray(g["ada_b"], f32),
        "norm_mix_g": np.ascontiguousarray(g["norm_mix_g"], f32),
        "norm_mlp_g": np.ascontiguousarray(g["norm_mlp_g"], f32),
        "mlp_w1": np.ascontiguousarray(g["mlp_w1"], f32), "mlp_w2": np.ascontiguousarray(g["mlp_w2"], f32),
        "ret_w_in": np.ascontiguousarray(g["ret_w_in"][0], f32),
        "ret_w_out": np.ascontiguousarray(g["ret_w_out"][0], f32),
        "attn_w_in": np.ascontiguousarray(g["attn_w_in"][0], f32),
        "attn_w_out": np.ascontiguousarray(g["attn_w_out"][0], f32),
        "attn_sink": np.ascontiguousarray(g["attn_sink"], f32).reshape(1, 16),
        "final_g": np.ascontiguousarray(g["final_norm_g"], f32).reshape(1, D),
        "consts": consts, "amask": amask,
    }
    tabs = {}
    for h in range(2):
        pos = np.arange(4096) if h == 0 else np.arange(4095, -1, -1)
        tabs[h] = _rope_tables(pos)
    maps = []
    for core in cores:
        b, h = core // 2, core % 2
        x = g["x"][b]
        cx = g["ctx"][b]
        if h == 1:
            x = x[::-1]
            cx = cx[::-1]
        xin = np.ascontiguousarray(np.concatenate([cx, x], axis=0), f32)
        cvec = np.ascontiguousarray(np.stack([g["c"][b], g["c_ctx"]], axis=0), f32)
        df, db = g["ret_decay_fwd"][0], g["ret_decay_bwd"][0]
        dec = np.concatenate([df, db] if h == 0 else [db, df]).astype(f32).reshape(1, 16)
        m = dict(shared)
        m.update({"xin": xin, "cvec": cvec, "dec": dec, "rope_r": tabs[h][0], "rope_a": tabs[h][1]})
        maps.append(m)
    return maps


def assemble(results, B=4):
    out = np.zeros((B, 4096, D), np.float32)
    for core, r in enumerate(results):
        b, h = core // 2, core % 2
        o = np.asarray(r["out"])
        if h == 0:
            out[b, :2048] = o
        else:
            out[b, 2048:] = o[::-1]
    return out


_NC_CACHE = {}


def kernel(**inputs):
    if "nc" not in _NC_CACHE:
        _NC_CACHE["nc"] = build_program()
    nc = _NC_CACHE["nc"]
    in_maps = make_in_maps(inputs)
    res = run_bass_kernel_spmd(nc, in_maps, core_ids=list(range(8)))
    return assemble(res.results)
```
